# Optimizing a Trainium2 kernel written in Bass

```python
import math
import jax, jax.numpy as jnp
from jax import lax
import numpy as np

D_MODEL = 1024
BATCH = 16
SEQ = 4096
DEPTH = 1

N_HEADS = 8
HEAD_DIM = D_MODEL // (2 * N_HEADS)
V_DIM = 2 * HEAD_DIM
ATT_QK_WIDTH = N_HEADS * 2 * HEAD_DIM
ATT_V_WIDTH = N_HEADS * V_DIM
ROPE_DIM = HEAD_DIM // 4
ROPE_THETA = 500000.0
Q_BLOCK = 128
POOL_WINDOWS = (2, 4, 8, 16)
N_POOL_GROUPS = 4
POOL_GROUP_IN = 128
POOL_WIDTH = N_POOL_GROUPS * POOL_GROUP_IN
POOL_GROUP_OUT = D_MODEL // N_POOL_GROUPS
N_BRANCHES = 2
D_FF = 4 * D_MODEL
IN_WIDTH = 2 * ATT_QK_WIDTH + ATT_V_WIDTH + POOL_WIDTH + N_BRANCHES * D_MODEL
RMS_EPS = 1e-6

kernel_name = "hybrid_diffattn_pool_gated_block"


def lambda_init(layer_idx):
    return 0.8 - 0.6 * math.exp(-0.3 * layer_idx)


def rmsnorm(x, g):
    xf = x.astype(jnp.float32)
    y = xf * lax.rsqrt(jnp.mean(xf * xf, axis=-1, keepdims=True) + RMS_EPS)
    return (y * g.astype(jnp.float32)).astype(x.dtype)


def rope_partial(t, cos, sin):
    half = ROPE_DIM // 2
    t1 = t[..., :half]
    t2 = t[..., half:ROPE_DIM]
    rest = t[..., ROPE_DIM:]
    return jnp.concatenate([t1 * cos - t2 * sin, t2 * cos + t1 * sin, rest], axis=-1)


def diff_attention(q, k, v, positions, lam, subln_g, lam_init):
    B, S = q.shape[0], q.shape[1]
    inv_freq = ROPE_THETA ** (-jnp.arange(0, ROPE_DIM, 2, dtype=jnp.float32) / ROPE_DIM)
    ang = positions.astype(jnp.float32)[..., None] * inv_freq
    cos = jnp.cos(ang)[:, :, None, :].astype(q.dtype)
    sin = jnp.sin(ang)[:, :, None, :].astype(q.dtype)
    q = rope_partial(q.reshape(B, S, N_HEADS * 2, HEAD_DIM), cos, sin).reshape(B, S, N_HEADS, 2, HEAD_DIM)
    k = rope_partial(k.reshape(B, S, N_HEADS * 2, HEAD_DIM), cos, sin).reshape(B, S, N_HEADS, 2, HEAD_DIM)
    q = q * (HEAD_DIM ** -0.5)
    outs = []
    for i in range(S // Q_BLOCK):
        q0 = i * Q_BLOCK
        kend = q0 + Q_BLOCK
        qb = q[:, q0:kend]
        kb = k[:, :kend]
        vb = v[:, :kend]
        s = jnp.einsum('bqhcd,bkhcd->bhcqk', qb, kb).astype(jnp.float32)
        mask = jnp.arange(kend)[None, :] <= (q0 + jnp.arange(Q_BLOCK))[:, None]
        s = jnp.where(mask, s, -jnp.inf)
        p = jax.nn.softmax(s, axis=-1)
        w = p[:, :, 0] - lam * p[:, :, 1]
        outs.append(jnp.einsum('bhqk,bkhd->bqhd', w.astype(v.dtype), vb))
    o = jnp.concatenate(outs, axis=1)
    o = rmsnorm(o, subln_g) * (1.0 - lam_init)
    return o.reshape(B, S, ATT_V_WIDTH)


def multiscale_pool(u, w_pool, pool_scale):
    B, S = u.shape[0], u.shape[1]
    uf = u.astype(jnp.float32).reshape(B, S, N_POOL_GROUPS, POOL_GROUP_IN)
    c = lax.cumsum(uf, axis=1)
    idx = jnp.arange(S)
    parts = []
    for g, w in enumerate(POOL_WINDOWS):
        cg = c[:, :, g]
        c_prev = jnp.pad(cg, ((0, 0), (w, 0), (0, 0)))[:, :S]
        cnt = jnp.minimum(idx + 1, w).astype(jnp.float32)[None, :, None]
        parts.append((cg - c_prev) / cnt - uf[:, :, g])
    d = jnp.stack(parts, axis=2).astype(u.dtype)
    y = jnp.einsum('bsgc,gcd->bsgd', d, w_pool).reshape(B, S, D_MODEL)
    return y * pool_scale


def setup_inputs(seed: int = 0) -> dict:
    key = jax.random.key(seed)
    ks = jax.random.split(key, 18)
    f32 = jnp.float32
    x = jax.random.normal(ks[0], (BATCH, SEQ, D_MODEL), f32)
    offs = jax.random.randint(ks[1], (BATCH, 1), 0, 1024, dtype=jnp.int32)
    positions = (offs + jnp.arange(SEQ, dtype=jnp.int32)[None, :]).astype(jnp.int32)
    gain = lambda k, n: 1.0 + 0.02 * jax.random.normal(k, (DEPTH, n), f32)
    return {
        "x": x,
        "positions": positions,
        "norm_attn_g": gain(ks[2], D_MODEL),
        "w_in": jax.random.normal(ks[3], (DEPTH, D_MODEL, IN_WIDTH), f32) * D_MODEL ** -0.5,
        "lam_q1": 0.1 * jax.random.normal(ks[4], (DEPTH, HEAD_DIM), f32),
        "lam_k1": 0.1 * jax.random.normal(ks[5], (DEPTH, HEAD_DIM), f32),
        "lam_q2": 0.1 * jax.random.normal(ks[6], (DEPTH, HEAD_DIM), f32),
        "lam_k2": 0.1 * jax.random.normal(ks[7], (DEPTH, HEAD_DIM), f32),
        "subln_g": gain(ks[8], V_DIM),
        "w_pool": jax.random.normal(ks[9], (DEPTH, N_POOL_GROUPS, POOL_GROUP_IN, POOL_GROUP_OUT), f32) * POOL_GROUP_IN ** -0.5,
        "pool_scale": 1.0 + 0.1 * jax.random.normal(ks[10], (DEPTH, D_MODEL), f32),
        "w_out": jax.random.normal(ks[11], (DEPTH, D_MODEL, D_MODEL), f32) * D_MODEL ** -0.5,
        "norm_mlp_g": gain(ks[12], D_MODEL),
        "w_up": jax.random.normal(ks[13], (DEPTH, D_MODEL, D_FF), f32) * D_MODEL ** -0.5,
        "w_down": jax.random.normal(ks[14], (DEPTH, D_FF, D_MODEL), f32) * D_FF ** -0.5,
        "final_norm_g": 1.0 + 0.02 * jax.random.normal(ks[15], (D_MODEL,), f32),
    }


def reference(x, positions, norm_attn_g, w_in, lam_q1, lam_k1, lam_q2, lam_k2, subln_g,
              w_pool, pool_scale, w_out, norm_mlp_g, w_up, w_down, final_norm_g):
    B, S = x.shape[0], x.shape[1]
    splits = np.cumsum([ATT_QK_WIDTH, ATT_QK_WIDTH, ATT_V_WIDTH, POOL_WIDTH]).tolist()
    for l in range(DEPTH):
        lam_init = lambda_init(l)
        h = rmsnorm(x, norm_attn_g[l])
        u = h @ w_in[l]
        u_q, u_k, u_v, u_pool, u_gate = jnp.split(u, splits, axis=-1)
        q = u_q.reshape(B, S, N_HEADS, 2, HEAD_DIM)
        k = u_k.reshape(B, S, N_HEADS, 2, HEAD_DIM)
        v = u_v.reshape(B, S, N_HEADS, V_DIM)
        lam = (jnp.exp(jnp.sum(lam_q1[l].astype(jnp.float32) * lam_k1[l].astype(jnp.float32)))
               - jnp.exp(jnp.sum(lam_q2[l].astype(jnp.float32) * lam_k2[l].astype(jnp.float32)))
               + lam_init)
        a = diff_attention(q, k, v, positions, lam, subln_g[l], lam_init)
        p = multiscale_pool(u_pool, w_pool[l], pool_scale[l])
        gates = jax.nn.sigmoid(u_gate.reshape(B, S, N_BRANCHES, D_MODEL))
        merged = gates[:, :, 0] * a + gates[:, :, 1] * p
        x = x + (merged @ w_out[l]).astype(x.dtype)
        h2 = rmsnorm(x, norm_mlp_g[l])
        z = jnp.square(jax.nn.relu(h2 @ w_up[l]))
        x = x + (z @ w_down[l]).astype(x.dtype)
    return rmsnorm(x, final_norm_g)
```

```python
import math
from contextlib import ExitStack

import numpy as np
import concourse.bass as bass
import concourse.mybir as mybir
from concourse.bass_utils import run_bass_kernel_spmd

F32 = mybir.dt.float32
BF16 = mybir.dt.bfloat16
I32 = mybir.dt.int32
AF = mybir.ActivationFunctionType
ALU = mybir.AluOpType

D = 1024
NH = 8
DFF = 4096
INW = 5632
EPS = 1e-6
LAM_INIT = 0.8 - 0.6 * math.exp(-0.3 * 0)
N_CORES = 8
TWO_PI = 2.0 * math.pi
CW1 = 6.28125
CW2 = TWO_PI - CW1


class Tok:
    __slots__ = ("name", "w", "r", "dsem", "dkey", "dval")

    def __init__(self, name):
        self.name = name
        self.w = None
        self.r = {}
        self.dsem = None
        self.dkey = None
        self.dval = 0


class Eng:
    def __init__(self, name, sem):
        self.name = name
        self.sem = sem
        self.prog = []
        self.count = 0
        self.seen = {}
        self.nwait = 0
        self.nops = 0


class Prog:
    def __init__(self, nc, stack):
        self.nc = nc
        self.gstack = stack
        self.stack = stack
        self.engs = {}
        self.semmap = {}
        self.nkey = 0
        for name in ("pe", "act", "dve", "pool", "sp"):
            s = stack.enter_context(nc.semaphore("p_" + name))
            self.engs[name] = Eng(name, s)
        self.stores = []

    def new_dsem(self, name):
        s = self.gstack.enter_context(self.nc.semaphore(name))
        self.nkey += 1
        self.semmap[self.nkey] = s
        return s, self.nkey

    def _deps(self, reads, writes, extra):
        deps = {}

        def add(k, v):
            if deps.get(k, 0) < v:
                deps[k] = v
        for t in reads:
            if t.w is not None:
                add(*t.w)
        for t in writes:
            if t.w is not None:
                add(*t.w)
            for k, v in t.r.items():
                add(k, v)
        for ev in extra:
            if ev is not None:
                add(*ev)
        return deps

    def _wait(self, e, deps):
        for k, v in deps.items():
            if k == "pe" and e.name == "pe":
                continue
            if e.seen.get(k, 0) >= v:
                continue
            e.seen[k] = v
            sem = self.engs[k].sem if isinstance(k, str) else self.semmap[k]
            e.prog.append(("wait", sem, v))
            e.nwait += 1

    @staticmethod
    def _record(ev, reads, writes):
        k, v = ev
        for t in reads:
            if t.r.get(k, 0) < v:
                t.r[k] = v
        for t in writes:
            t.w = ev
            t.r = {}

    def op(self, eng, fn, reads=(), writes=(), signal=True, extra=()):
        e = self.engs[eng]
        self._wait(e, self._deps(reads, writes, extra))
        e.prog.append(("inst", fn, signal))
        e.nops += 1
        if signal:
            e.count += 1
            ev = (e.name, e.count)
        else:
            ev = (e.name, e.count + 1)
        self._record(ev, reads, writes)
        return ev

    def dma(self, q, out, in_, tok, reads=(), writes=(), extra=(), store=False):
        e = self.engs[q]
        self._wait(e, self._deps(reads, writes, extra))
        if tok.dsem is None:
            tok.dsem, tok.dkey = self.new_dsem("d_" + tok.name)
        e.prog.append(("dma", out, in_, tok.dsem))
        e.nops += 1
        tok.dval += 16
        ev = (tok.dkey, tok.dval)
        self._record(ev, reads, writes)
        if store:
            self.stores.append(ev)
        return ev

    def wait_toks(self, q, toks):
        e = self.engs[q]
        self._wait(e, self._deps(toks, (), ()))

    def wait_stores(self, q="sp"):
        e = self.engs[q]
        deps = {}
        for k, v in self.stores:
            if deps.get(k, 0) < v:
                deps[k] = v
        self.stores = []
        self._wait(e, deps)

    def emit(self):
        nc = self.nc
        with nc.Block() as block:
            def mk(e):
                def body(h):
                    for it in e.prog:
                        if it[0] == "wait":
                            h.wait_ge(it[1], it[2])
                        elif it[0] == "inst":
                            nm, a_, kw_ = it[1]
                            inst = getattr(h, nm)(*a_, **kw_)
                            if it[2]:
                                inst.then_inc(e.sem, 1)
                        else:
                            h.dma_start(out=it[1], in_=it[2]).then_inc(it[3], 16)
                    e.prog = []
                return body
            block.tensor(mk(self.engs["pe"]))
            block.scalar(mk(self.engs["act"]))
            block.vector(mk(self.engs["dve"]))
            block.gpsimd(mk(self.engs["pool"]))
            block.sync(mk(self.engs["sp"]))


def I(name, *args, **kw):
    return (name, args, kw)


class Ring:
    def __init__(self, items):
        self.items = items
        self.i = 0

    def next(self):
        it = self.items[self.i % len(self.items)]
        self.i += 1
        return it


class WStream:
    def __init__(self, P, slots, toks, wscr, seq, extra=()):
        self.P = P
        self.slots = slots
        self.toks = toks
        self.wscr = wscr
        self.seq = seq
        self.issued = 0
        self.used = 0
        self.extra = extra

    def _issue(self):
        i = self.issued
        s = i % len(self.slots)
        self.P.dma("sp", self.slots[s][:], self.wscr[self.seq[i]], self.toks[s],
                   writes=[self.toks[s]], extra=self.extra)
        self.issued += 1

    def get(self):
        i = self.used
        while self.issued <= i:
            self._issue()
        self.used += 1
        return self.slots[i % len(self.slots)], self.toks[i % len(self.slots)]

    def prefetch(self):
        R = len(self.slots)
        while self.issued < min(len(self.seq), self.used + R):
            self._issue()


class _Stop(Exception):
    pass


def build_nc(NB, S, upto=3, debug=False, dbgA=10 ** 9, dbgC=10 ** 9):
    NT = S // 512
    NBLK = S // 128
    NTT = NB * NT
    nc = bass.Bass("TRN2", target_bir_lowering=False)

    def din(name, shape, dt=F32):
        return nc.dram_tensor(name, list(shape), dt, kind="ExternalInput").ap()

    x_d = din("x", [NB, S, D])
    post_d = din("pos_t", [NB, 128, NBLK], I32)
    w_in_d = din("w_in", [D, INW])
    w_out_d = din("w_out", [D, D])
    w_up_d = din("w_up", [D, DFF])
    w_down_d = din("w_down", [DFF, D])
    wpool_d = din("w_pool_r", [128, 4, 256])
    gattn_d = din("gcol_attn", [128, 8])
    gmlp_d = din("gcol_mlp", [128, 8])
    gfin_d = din("gfin", [D])
    subln_d = din("subln_col", [128, 1])
    pscale_d = din("pool_scale", [D])
    lam_d = din("lam4", [4, 64])
    ident_d = din("c_ident", [128, 128])
    mask_d = din("c_mask", [128, 128])
    invf_d = din("c_invf", [8])
    wct_d = din("c_wct", [4, 16])
    y_d = nc.dram_tensor("y", [NB, S, D], F32, kind="ExternalOutput").ap()

    def dscr(name, shape, dt=BF16):
        return nc.dram_tensor(name, list(shape), dt, kind=("ExternalOutput" if debug else "Internal")).ap()

    NWG = 29
    wscr = dscr("wscr", [NWG, 128, 8, 512])
    qT_s = dscr("qT_s", [NB, NH, 128, S])
    kT_s = dscr("kT_s", [NB, NH, 128, S])
    v_s = dscr("v_s", [NB, S, D])
    gA_s = dscr("gA_s", [NB, NH, 128, S])
    pp_s = dscr("pp_s", [NB, NH, 128, S])
    m_s = dscr("m_s", [NB, NH, 128, S])

    G_Q, G_K, G_V, G_POOL, G_GA, G_GP = (0, 1), (2, 3), (4, 5), (6,), (7, 8), (9, 10)
    G_OUT = (11, 12)
    G_UP = tuple(range(13, 21))
    G_DOWN = tuple(range(21, 29))

    with ExitStack() as gst:
        P = Prog(nc, gst)

        def gsb(name, shape, dt):
            return gst.enter_context(nc.sbuf_tensor(name, list(shape), dt))

        psum_all = gst.enter_context(nc.psum_tensor("psum_all", [128, 8, 512], F32))
        banks = [psum_all[:, i, :] for i in range(8)]
        bt = [Tok("bank%d" % i) for i in range(8)]

        identb = gsb("identb", [128, 128], BF16)
        onesb = gsb("onesb", [128, 128], BF16)
        onesf = gsb("onesf", [128, 128], F32)
        maskb = gsb("maskb", [128, 128], BF16)
        gfin_bc = gsb("gfin_bc", [128, D], F32)
        wpoolb = gsb("wpoolb", [128, 4, 256], BF16)
        gattn = gsb("gattn", [128, 8], F32)
        gmlp = gsb("gmlp", [128, 8], F32)
        gs_col = gsb("gs_col", [128, 1], F32)
        neglam = gsb("neglam", [128, 1], F32)
        cosb = gsb("cosb", [128, NB * NBLK, 8], F32)
        sinb = gsb("sinb", [128, NB * NBLK, 8], F32)
        wct = gsb("wct", [128, 4, 16], F32)
        mhalf = gsb("mhalf", [128, 512], F32)
        eps_col = gsb("eps_col", [128, 1], F32)
        t_const = Tok("consts")

        with ExitStack() as st:
            P.stack = st

            def sb(name, shape, dt):
                return st.enter_context(nc.sbuf_tensor(name, list(shape), dt))

            idf = sb("idf", [128, 128], F32)
            mkf = sb("mkf", [128, 128], F32)
            wpf = sb("wpf", [128, 4, 256], F32)
            psb = sb("psb", [128, D], F32)
            lamt = sb("lamt", [128, 4, 64], F32)
            lprod = sb("lprod", [128, 2, 64], F32)
            lsum = sb("lsum", [128, 2], F32)
            lexp = sb("lexp", [128, 2], F32)
            subl = sb("subl", [128, 1], F32)
            posi = sb("posi", [128, NB * NBLK], I32)
            posf = sb("posf", [128, NB * NBLK], F32)
            invf = sb("invf", [128, 8], F32)
            NA = NB * NBLK * 8
            ang = sb("ang", [128, NA], F32)
            tq = sb("tq", [128, NA], F32)
            ki = sb("ki", [128, NA], I32)
            kf = sb("kf", [128, NA], F32)
            r0 = sb("r0", [128, NA], F32)
            r1 = sb("r1", [128, NA], F32)
            mk1 = sb("mk1", [128, NA], F32)
            rc = sb("rc", [128, NA], F32)
            tl = {n: Tok(n) for n in ("idf", "mkf", "wpf", "psb", "lamt", "subl", "posi", "invf", "gat", "gml",
                                      "gfin", "wct")}
            P.dma("sp", idf[:], ident_d, tl["idf"], writes=[tl["idf"]])
            P.dma("sp", mkf[:], mask_d, tl["mkf"], writes=[tl["mkf"]])
            P.dma("sp", wpf[:], wpool_d, tl["wpf"], writes=[tl["wpf"]])
            P.dma("sp", psb[:], pscale_d.partition_broadcast(128), tl["psb"], writes=[tl["psb"]])
            P.dma("sp", lamt[:], lam_d.partition_broadcast(128), tl["lamt"], writes=[tl["lamt"]])
            P.dma("sp", subl[:], subln_d, tl["subl"], writes=[tl["subl"]])
            P.dma("sp", posi[:].rearrange("p (b k) -> p b k", b=NB), post_d.rearrange("b p k -> p b k"),
                  tl["posi"], writes=[tl["posi"]])
            P.dma("sp", invf[:], invf_d.partition_broadcast(128), tl["invf"], writes=[tl["invf"]])
            P.dma("sp", gattn[:], gattn_d, tl["gat"], writes=[tl["gat"]])
            P.dma("sp", gmlp[:], gmlp_d, tl["gml"], writes=[tl["gml"]])
            P.dma("sp", gfin_bc[:], gfin_d.partition_broadcast(128), tl["gfin"], writes=[tl["gfin"]])
            P.dma("sp", wct[:], wct_d.partition_broadcast(128), tl["wct"], writes=[tl["wct"]])

            tc_ = t_const
            P.op("dve", I("tensor_copy", out=identb[:], in_=idf[:]), reads=[tl["idf"]], writes=[tc_])
            P.op("dve", I("tensor_copy", out=maskb[:], in_=mkf[:]), reads=[tl["mkf"]], writes=[tc_])
            P.op("pool", I("memset", onesb[:], 1.0), writes=[tc_])
            P.op("pool", I("memset", onesf[:], 1.0), writes=[tc_])
            P.op("pool", I("memset", mhalf[:], -0.5), writes=[tc_])
            P.op("pool", I("memset", eps_col[:], EPS), writes=[tc_])
            for g in range(4):
                wg_ = float(2 ** (g + 1))
                P.op("dve", I("scalar_tensor_tensor",
                    out=wpoolb[:, g, :], in0=wpf[:, g, :], scalar=0.5 / wg_, in1=psb[:, g * 256:(g + 1) * 256],
                    op0=ALU.mult, op1=ALU.mult), reads=[tl["wpf"], tl["psb"]], writes=[tc_])
            P.op("dve", I("tensor_scalar", out=gs_col[:], in0=subl[:], scalar1=0.5 * (1.0 - LAM_INIT),
                                                  scalar2=None, op0=ALU.mult), reads=[tl["subl"]], writes=[tc_])
            tlp = Tok("lprod")
            P.op("dve", I("tensor_tensor", out=lprod[:, 0, :], in0=lamt[:, 0, :], in1=lamt[:, 1, :],
                                                  op=ALU.mult), reads=[tl["lamt"]], writes=[tlp])
            P.op("dve", I("tensor_tensor", out=lprod[:, 1, :], in0=lamt[:, 2, :], in1=lamt[:, 3, :],
                                                  op=ALU.mult), reads=[tl["lamt"], tlp], writes=[tlp])
            P.op("dve", I("tensor_reduce", out=lsum[:], in_=lprod[:], op=ALU.add,
                                                  axis=mybir.AxisListType.X), reads=[tlp], writes=[tlp])
            P.op("act", I("activation", out=lexp[:], in_=lsum[:], func=AF.Exp), reads=[tlp], writes=[tlp])
            P.op("dve", I("tensor_tensor", out=lsum[:, 0:1], in0=lexp[:, 1:2], in1=lexp[:, 0:1],
                                                  op=ALU.subtract), reads=[tlp], writes=[tlp])
            P.op("dve", I("tensor_scalar", out=neglam[:], in0=lsum[:, 0:1], scalar1=-LAM_INIT, scalar2=None,
                                                  op0=ALU.add), reads=[tlp], writes=[tc_])
            ta = Tok("ang")
            P.op("dve", I("tensor_copy", out=posf[:], in_=posi[:]), reads=[tl["posi"]], writes=[ta])
            P.op("dve", I("tensor_tensor",
                out=ang[:].rearrange("p (k f) -> p k f", f=8),
                in0=posf[:].unsqueeze(2).to_broadcast([128, NB * NBLK, 8]),
                in1=invf[:].unsqueeze(1).to_broadcast([128, NB * NBLK, 8]), op=ALU.mult),
                reads=[tl["invf"], ta], writes=[ta])

            def reduce_to(dst, src_shift):
                P.op("dve", I("tensor_scalar", out=tq[:], in0=ang[:], scalar1=src_shift, scalar2=1.0 / TWO_PI,
                                                      op0=ALU.add, op1=ALU.mult), reads=[ta], writes=[ta])
                P.op("dve", I("tensor_copy", out=ki[:], in_=tq[:]), reads=[ta], writes=[ta])
                P.op("dve", I("tensor_copy", out=kf[:], in_=ki[:]), reads=[ta], writes=[ta])
                P.op("dve", I("tensor_scalar", out=rc[:], in0=ang[:], scalar1=src_shift, scalar2=None,
                                                      op0=ALU.add), reads=[ta], writes=[ta])
                P.op("dve", I("scalar_tensor_tensor", out=r0[:], in0=kf[:], scalar=-CW1, in1=rc[:],
                                                             op0=ALU.mult, op1=ALU.add), reads=[ta], writes=[ta])
                P.op("dve", I("scalar_tensor_tensor", out=r1[:], in0=kf[:], scalar=-CW2, in1=r0[:],
                                                             op0=ALU.mult, op1=ALU.add), reads=[ta], writes=[ta])
                P.op("dve", I("tensor_scalar", out=mk1[:], in0=r1[:], scalar1=math.pi, scalar2=-TWO_PI,
                                                      op0=ALU.is_gt, op1=ALU.mult), reads=[ta], writes=[ta])
                P.op("dve", I("tensor_tensor", out=r0[:], in0=r1[:], in1=mk1[:], op=ALU.add),
                     reads=[ta], writes=[ta])
                P.op("dve", I("tensor_scalar", out=mk1[:], in0=r0[:], scalar1=-math.pi, scalar2=TWO_PI,
                                                      op0=ALU.is_lt, op1=ALU.mult), reads=[ta], writes=[ta])
                P.op("dve", I("tensor_tensor", out=r1[:], in0=r0[:], in1=mk1[:], op=ALU.add),
                     reads=[ta], writes=[ta])
                P.op("dve", I("tensor_scalar", out=r1[:], in0=r1[:], scalar1=math.pi, scalar2=-math.pi,
                                                      op0=ALU.min, op1=ALU.max), reads=[ta], writes=[ta])
                P.op("act", I("activation", out=dst[:].rearrange("p k f -> p (k f)"), in_=r1[:], func=AF.Sin),
                     reads=[ta], writes=[ta, tc_])

            reduce_to(sinb, 0.0)
            reduce_to(cosb, 0.5 * math.pi)

            NST = 4
            wst = [sb("wst%d" % i, [128, 8, 512], F32) for i in range(NST)]
            wbo = [sb("wbo%d" % i, [128, 8, 512], BF16) for i in range(NST)]
            wst_t = [Tok("wst%d" % i) for i in range(NST)]
            wbo_t = [Tok("wbo%d" % i) for i in range(NST)]
            w_in_v = w_in_d.rearrange("(kc p) n -> p kc n", p=128)
            w_out_v = w_out_d.rearrange("(kc p) n -> p kc n", p=128)
            w_up_v = w_up_d.rearrange("(kc p) n -> p kc n", p=128)
            w_dn_v = w_down_d.rearrange("(a p) n -> p a n", p=128)
            jobs = []
            for g in range(11):
                jobs.append((g, w_in_v[:, :, g * 512:(g + 1) * 512], gattn, "gat"))
            for g in range(2):
                jobs.append((11 + g, w_out_v[:, :, g * 512:(g + 1) * 512], None, None))
            for g in range(8):
                jobs.append((13 + g, w_up_v[:, :, g * 512:(g + 1) * 512], gmlp, "gml"))
            for nh in range(2):
                for fg in range(4):
                    jobs.append((21 + nh * 4 + fg, w_dn_v[:, fg * 8:(fg + 1) * 8, nh * 512:(nh + 1) * 512], None, None))
            engs3 = ["dve", "act", "dve"]
            for i, (gid, src, gcol, gk) in enumerate(jobs):
                s = i % NST
                P.dma("sp", wst[s][:], src, wst_t[s], writes=[wst_t[s]])
                eng = engs3[i % 3]
                if gcol is None:
                    if eng == "act":
                        P.op("act", I("activation", out=wbo[s][:], in_=wst[s][:], func=AF.Copy),
                             reads=[wst_t[s]], writes=[wbo_t[s]])
                    else:
                        P.op(eng, I("tensor_copy", out=wbo[s][:], in_=wst[s][:]),
                             reads=[wst_t[s]], writes=[wbo_t[s]])
                else:
                    for kc in range(8):
                        if eng == "act":
                            P.op("act", I("activation",
                                out=wbo[s][:, kc, :], in_=wst[s][:, kc, :], func=AF.Copy, scale=gcol[:, kc:kc + 1]),
                                reads=[wst_t[s], tl[gk]], writes=[wbo_t[s]])
                        else:
                            P.op(eng, I("tensor_scalar",
                                out=wbo[s][:, kc, :], in0=wst[s][:, kc, :], scalar1=gcol[:, kc:kc + 1], scalar2=None,
                                op0=ALU.mult), reads=[wst_t[s], tl[gk]], writes=[wbo_t[s]])
                P.dma("sp", wscr[gid], wbo[s][:], wbo_t[s], reads=[wbo_t[s]], store=True)
            P.wait_stores("sp")
            P.wait_toks("sp", list(tl.values()))
            P.emit()

        with ExitStack() as st:
          if upto >= 1:
            P.stack = st

            def sb(name, shape, dt):
                return st.enter_context(nc.sbuf_tensor(name, list(shape), dt))

            xt = [sb("xt%d" % i, [128, 4, D], F32) for i in range(2)]
            xt_t = [Tok("xt%d" % i) for i in range(2)]
            junk = sb("junkA", [128, D], BF16)
            junk_t = Tok("junkA")
            ss = sb("ssA", [128, 4], F32)
            vv = sb("vvA", [128, 4], F32)
            rstd = sb("rstdA", [128, 4], F32)
            st_t = Tok("statsA")
            hb = sb("hbA", [128, 4, D], BF16)
            hb_t = [Tok("hbA%d" % i) for i in range(4)]
            hT = [sb("hT%d" % i, [128, 8, 512], BF16) for i in range(2)]
            hT_t = [[Tok("hT%d_%d" % (i, k)) for k in range(8)] for i in range(2)]
            NWS = 3
            wsl = [sb("wslA%d" % i, [128, 8, 512], BF16) for i in range(NWS)]
            wsl_t = [Tok("wslA%d" % i) for i in range(NWS)]
            qkt = [sb("qkt%d" % i, [128, 4, 512], BF16) for i in range(3)]
            qkt_t = [[Tok("qkt%d_%d" % (i, tb)) for tb in range(4)] for i in range(3)]
            NQS = 4
            qst = [sb("qst%d" % i, [128, 512], BF16) for i in range(NQS)]
            qst_t = [Tok("qst%d" % i) for i in range(NQS)]
            vtk = [sb("vtk%d" % i, [128, 4, 512], BF16) for i in range(2)]
            vtk_t = [Tok("vtk%d" % i) for i in range(2)]
            tg = sb("tgA", [128, 16, 512], BF16)
            tg_t = [Tok("tg%d" % i) for i in range(16)]
            up = [sb("up%d" % i, [128, 4, 528], F32) for i in range(2)]
            up_t = [[Tok("up%d_%d" % (i, c)) for c in range(4)] for i in range(2)]
            T1 = sb("poolT1", [128, 528], F32)
            T2 = sb("poolT2", [128, 528], F32)
            T3 = sb("poolT3", [128, 528], F32)
            pl_t = Tok("poolT")
            dT = sb("dT", [128, 4, 512], BF16)
            dT_t = [Tok("dT%d" % c) for c in range(4)]
            NPS = 3
            pst = [sb("pst%d" % i, [128, 512], BF16) for i in range(NPS)]
            pst_t = [Tok("pst%d" % i) for i in range(NPS)]
            ra = sb("ropeA", [128, 4, 64], F32)
            ra_t4 = [Tok("ropeA%d" % i) for i in range(4)]
            fixt = sb("fixt", [128, 16], F32)
            NQF = 3
            qf = [sb("qf%d" % i, [128, 512], F32) for i in range(NQF)]
            qf_t = [Tok("qf%d" % i) for i in range(NQF)]
            qfr = Ring(list(range(NQF)))

            mmring = Ring([0, 1, 2, 3, 4, 5])
            trring = Ring([6, 7])
            qsr = Ring(list(range(NQS)))
            psr = Ring(list(range(NPS)))

            order = [G_POOL[0], G_GP[0], G_GP[1], G_GA[0], G_GA[1], G_Q[0], G_Q[1], G_K[0], G_K[1], G_V[0], G_V[1]]
            WS = WStream(P, wsl, wsl_t, wscr, order * NTT)

            def load_x(t):
                b, j = divmod(t, NT)
                P.dma("sp", xt[t % 2][:], x_d[b, j * 512:(j + 1) * 512, :].rearrange("(tb p) d -> p tb d", p=128),
                      xt_t[t % 2], writes=[xt_t[t % 2]])

            def prep(t):
                xb, xtok = xt[t % 2], xt_t[t % 2]
                for tb in range(4):
                    P.op("act", I("activation", out=junk[:], in_=xb[:, tb, :], func=AF.Square,
                                                              accum_out=ss[:, tb:tb + 1]),
                         reads=[xtok], writes=[junk_t, st_t])
                P.op("dve", I("tensor_scalar", out=vv[:], in0=ss[:], scalar1=1.0 / D, scalar2=EPS,
                                                      op0=ALU.mult, op1=ALU.add), reads=[st_t], writes=[st_t])
                P.op("act", I("activation", out=vv[:], in_=vv[:], func=AF.Ln), reads=[st_t], writes=[st_t])
                P.op("act", I("activation", out=rstd[:], in_=vv[:], func=AF.Exp, scale=-0.5), reads=[st_t], writes=[st_t])
                for tb in range(4):
                    P.op("act", I("activation", out=hb[:, tb, :], in_=xb[:, tb, :], func=AF.Copy,
                                                              scale=rstd[:, tb:tb + 1]),
                         reads=[xtok, st_t], writes=[hb_t[tb]])
                for kc in range(8):
                    bk = trring.next()
                    pT = banks[bk][:].bitcast(BF16)
                    for tb in range(4):
                        P.op("pe", I("transpose",
                            out=pT[:, tb * 128:(tb + 1) * 128], in_=hb[:, tb, kc * 128:(kc + 1) * 128],
                            identity=identb[:]), reads=[hb_t[tb], t_const], writes=[bt[bk]], signal=(tb == 3))
                    P.op("dve", I("tensor_copy", out=hT[t % 2][:, kc, :], in_=pT[:, 0:512]),
                         reads=[bt[bk]], writes=[hT_t[t % 2][kc]])

            def fm_group(t, Wg, wt, evac):
                for c in range(4):
                    bk = mmring.next()
                    for kc in range(8):
                        P.op("pe", I("matmul",
                            banks[bk][:], lhsT=Wg[:, kc, c * 128:(c + 1) * 128], rhs=hT[t % 2][:, kc, :],
                            start=(kc == 0), stop=(kc == 7)),
                            reads=[wt, hT_t[t % 2][kc]], writes=[bt[bk]], signal=(kc == 7))
                    evac(c, bk)

            def tm_group(t, Wg, wt, evac):
                for tb in range(4):
                    bk = mmring.next()
                    for kc in range(8):
                        P.op("pe", I("matmul",
                            banks[bk][:], lhsT=hT[t % 2][:, kc, tb * 128:(tb + 1) * 128], rhs=Wg[:, kc, :],
                            start=(kc == 0), stop=(kc == 7)),
                            reads=[wt, hT_t[t % 2][kc]], writes=[bt[bk]], signal=(kc == 7))
                    evac(tb, bk)

            deferred = []

            def run_deferred(now):
                keep = []
                for when, fn in deferred:
                    if when <= now:
                        fn()
                    else:
                        keep.append((when, fn))
                deferred[:] = keep

            step = [0]
            stage = [0]

            def chk():
                stage[0] += 1
                if stage[0] >= dbgA:
                    raise _Stop()

            def phaseA_body():
              for t in range(NTT):
                b, j = divmod(t, NT)
                if t == 0:
                    load_x(0)
                    prep(0)
                    chk()
                if t + 1 < NTT:
                    load_x(t + 1)
                ub = up[t % 2]
                ubt = up_t[t % 2]
                pub = up[(t + 1) % 2]
                pubt = up_t[(t + 1) % 2]

                Wg, wt = WS.get()

                def ev_pool(c, bk):
                    P.op("dve", I("tensor_copy", out=ub[:, c, 16:528], in_=banks[bk][:]),
                         reads=[bt[bk]], writes=[ubt[c]])
                    if j == 0:
                        P.op("pool", I("memset", ub[:, c, 0:16], 0.0), writes=[ubt[c]])
                    else:
                        P.op("pool", I("tensor_copy", out=ub[:, c, 0:16], in_=pub[:, c, 512:528]),
                             reads=[pubt[c]], writes=[ubt[c]])
                    U = ub[:, c, :]
                    w = 2 ** (c + 1)
                    dst = dT[:, c, :]
                    if c == 0:
                        P.op("pool", I("tensor_tensor", out=dst, in0=U[:, 15:527], in1=U[:, 16:528],
                                                               op=ALU.subtract), reads=[ubt[c]], writes=[dT_t[c]])
                        if j == 0:
                            P.op("pool", I("memset", dst[:, 0:1], 0.0), writes=[dT_t[c]])
                        return
                    P.op("pool", I("tensor_tensor", out=T1[:, 1:528], in0=U[:, 1:528], in1=U[:, 0:527],
                                                           op=ALU.add), reads=[ubt[c]], writes=[pl_t])
                    P.op("pool", I("tensor_tensor", out=T2[:, 3:528], in0=T1[:, 3:528], in1=T1[:, 1:526],
                                                           op=ALU.add), reads=[pl_t], writes=[pl_t])
                    cur = T2
                    if c >= 2:
                        P.op("pool", I("tensor_tensor", out=T1[:, 7:528], in0=T2[:, 7:528], in1=T2[:, 3:524],
                                                               op=ALU.add), reads=[pl_t], writes=[pl_t])
                        cur = T1
                    if c >= 3:
                        P.op("pool", I("tensor_tensor", out=T2[:, 15:528], in0=T1[:, 15:528],
                                                               in1=T1[:, 7:520], op=ALU.add),
                             reads=[pl_t], writes=[pl_t])
                        cur = T2
                    P.op("pool", I("tensor_scalar", out=T3[:, 16:528], in0=U[:, 16:528], scalar1=-float(w),
                                                           scalar2=None, op0=ALU.mult),
                         reads=[ubt[c], pl_t], writes=[pl_t])
                    P.op("pool", I("tensor_tensor", out=dst, in0=cur[:, 16:528], in1=T3[:, 16:528],
                                                           op=ALU.add), reads=[pl_t], writes=[dT_t[c]])
                    if j == 0:
                        P.op("pool", I("tensor_tensor", out=fixt[:, 0:w - 1], in0=cur[:, 16:16 + w - 1],
                                                               in1=wct[:, c, 0:w - 1], op=ALU.mult),
                             reads=[pl_t, t_const], writes=[pl_t])
                        P.op("pool", I("tensor_tensor", out=dst[:, 0:w - 1], in0=fixt[:, 0:w - 1],
                                                               in1=T3[:, 16:16 + w - 1], op=ALU.add),
                             reads=[pl_t], writes=[dT_t[c], pl_t])

                fm_group(t, Wg, wt, ev_pool)
                chk()
                WS.prefetch()

                for gi, base in ((0, 8), (1, 12), (2, 0), (3, 4)):
                    Wg, wt = WS.get()

                    def ev_gate(c, bk, base=base):
                        idx = base + c
                        P.op("act", I("activation", out=tg[:, idx, :], in_=banks[bk][:], func=AF.Tanh,
                                                           scale=0.5), reads=[bt[bk]], writes=[tg_t[idx]])
                        if idx < 8:
                            P.dma("sp", gA_s[b, idx, :, j * 512:(j + 1) * 512], tg[:, idx, :], tg_t[idx],
                                  reads=[tg_t[idx]], store=True)
                    fm_group(t, Wg, wt, ev_gate)
                    chk()
                    WS.prefetch()
                    if gi == 1:
                        def do_pool_out(b=b, j=j):
                            for c in range(4):
                                for e in range(2):
                                    bk = mmring.next()
                                    P.op("pe", I("matmul",
                                        banks[bk][:], lhsT=wpoolb[:, c, e * 128:(e + 1) * 128], rhs=dT[:, c, :],
                                        start=True, stop=True), reads=[dT_t[c], t_const], writes=[bt[bk]])
                                    ps_i = psr.next()
                                    idx = 8 + 2 * c + e
                                    P.op("dve", I("scalar_tensor_tensor",
                                        out=pst[ps_i][:], in0=tg[:, idx, :], scalar=1.0, in1=banks[bk][:],
                                        op0=ALU.add, op1=ALU.mult), reads=[bt[bk], tg_t[idx]], writes=[pst_t[ps_i]])
                                    P.dma("sp", pp_s[b, 2 * c + e, :, j * 512:(j + 1) * 512], pst[ps_i][:], pst_t[ps_i],
                                          reads=[pst_t[ps_i]], store=True)


                        deferred.append((step[0] + 4, do_pool_out))

                for kind, scr in (("q", qT_s), ("k", kT_s)):
                    for half in range(2):
                        Wg, wt = WS.get()
                        qb = step[0] % 3
                        step[0] += 1
                        qk_ = qkt[qb]
                        qk_tt = qkt_t[qb]

                        def ev_qk(tb, bk, qk_=qk_, qk_tt=qk_tt):
                            qi_ = qfr.next()
                            P.op("act", I("activation", out=qf[qi_][:], in_=banks[bk][:], func=AF.Copy),
                                 reads=[bt[bk]], writes=[qf_t[qi_]])
                            src = qf[qi_][:].rearrange("p (a d) -> p a d", a=8)
                            dst = qk_[:, tb, :].rearrange("p (a d) -> p a d", a=8)
                            blk = b * NBLK + 4 * j + tb
                            cs = cosb[:, blk, :].unsqueeze(1).to_broadcast([128, 8, 8])
                            sn = sinb[:, blk, :].unsqueeze(1).to_broadcast([128, 8, 8])
                            rav = ra[:].rearrange("p k (a f) -> p k a f", a=8)
                            P.op("act", I("activation", out=dst[:, :, 16:64], in_=src[:, :, 16:64], func=AF.Copy),
                                 reads=[qf_t[qi_]], writes=[qk_tt[tb]])
                            t1, t2 = src[:, :, 0:8], src[:, :, 8:16]
                            P.op("dve", I("tensor_tensor", out=rav[:, 0], in0=t1, in1=cs, op=ALU.mult),
                                 reads=[qf_t[qi_], t_const], writes=[ra_t4[0]])
                            P.op("dve", I("tensor_tensor", out=rav[:, 1], in0=t2, in1=sn, op=ALU.mult),
                                 reads=[qf_t[qi_], t_const], writes=[ra_t4[1]])
                            P.op("dve", I("tensor_tensor", out=rav[:, 2], in0=t2, in1=cs, op=ALU.mult),
                                 reads=[qf_t[qi_], t_const], writes=[ra_t4[2]])
                            P.op("dve", I("tensor_tensor", out=rav[:, 3], in0=t1, in1=sn, op=ALU.mult),
                                 reads=[qf_t[qi_], t_const], writes=[ra_t4[3]])
                            P.op("dve", I("tensor_tensor", out=dst[:, :, 0:8], in0=rav[:, 0], in1=rav[:, 1],
                                          op=ALU.subtract), reads=[ra_t4[0], ra_t4[1]], writes=[qk_tt[tb]])
                            P.op("dve", I("tensor_tensor", out=dst[:, :, 8:16], in0=rav[:, 2], in1=rav[:, 3],
                                          op=ALU.add), reads=[ra_t4[2], ra_t4[3]], writes=[qk_tt[tb]])

                        tm_group(t, Wg, wt, ev_qk)
                        chk()
                        WS.prefetch()

                        def do_tr(qk_=qk_, qk_tt=qk_tt, half=half, scr=scr, b=b, j=j):
                            for c in range(4):
                                bk = trring.next()
                                pT = banks[bk][:].bitcast(BF16)
                                for tb in range(4):
                                    P.op("pe", I("transpose",
                                        out=pT[:, tb * 128:(tb + 1) * 128], in_=qk_[:, tb, c * 128:(c + 1) * 128],
                                        identity=identb[:]), reads=[qk_tt[tb], t_const], writes=[bt[bk]],
                                        signal=(tb == 3))
                                qi = qsr.next()
                                P.op("dve", I("tensor_copy", out=qst[qi][:], in_=pT[:, 0:512]),
                                     reads=[bt[bk]], writes=[qst_t[qi]])
                                P.dma("sp", scr[b, half * 4 + c, :, j * 512:(j + 1) * 512], qst[qi][:], qst_t[qi],
                                      reads=[qst_t[qi]], store=True)
                        run_deferred(step[0])
                        deferred.append((step[0] + 2, do_tr))
                        if kind == "q" and half == 0 and t + 1 < NTT:
                            prep(t + 1)

                for half in range(2):
                    Wg, wt = WS.get()
                    vb = step[0] % 2
                    step[0] += 1

                    def ev_v(tb, bk, vb=vb):
                        if tb % 2 == 0:
                            P.op("act", I("activation", out=vtk[vb][:, tb, :], in_=banks[bk][:], func=AF.Copy),
                                 reads=[bt[bk]], writes=[vtk_t[vb]])
                        else:
                            P.op("dve", I("tensor_copy", out=vtk[vb][:, tb, :], in_=banks[bk][:]),
                                 reads=[bt[bk]], writes=[vtk_t[vb]])
                    tm_group(t, Wg, wt, ev_v)
                    chk()
                    WS.prefetch()
                    P.dma("sp", v_s[b, j * 512:(j + 1) * 512, (half) * 512:(half + 1) * 512].rearrange(
                        "(tb p) d -> p tb d", p=128), vtk[vb][:], vtk_t[vb], reads=[vtk_t[vb]], store=True)
                    run_deferred(step[0])
                run_deferred(10 ** 9)
            try:
                phaseA_body()
            except _Stop:
                pass
            P.wait_stores("sp")
            P.emit()

        with ExitStack() as st:
          if upto >= 2:
            P.stack = st

            def sb(name, shape, dt):
                return st.enter_context(nc.sbuf_tensor(name, list(shape), dt))

            kTb = [sb("kTb%d" % i, [128, S], BF16) for i in range(2)]
            qTb = [sb("qTb%d" % i, [128, S], BF16) for i in range(2)]
            vb_ = [sb("vb%d" % i, [128, NBLK, 128], BF16) for i in range(2)]
            gAb = [sb("gAb%d" % i, [128, S], BF16) for i in range(2)]
            ppb = [sb("ppb%d" % i, [128, S], BF16) for i in range(2)]
            ld_t = [{n: Tok("%s%d" % (n, i)) for n in ("k", "q", "v", "g", "p")} for i in range(2)]
            NE = 4
            Et = [sb("E%d" % i, [128, 2, 512], BF16) for i in range(NE)]
            Et_t = [Tok("E%d" % i) for i in range(NE)]
            rz = [sb("rz%d" % c, [128, 512], F32) for c in range(2)]
            tt_ = [sb("tt%d" % c, [128, 512], F32) for c in range(2)]
            ob = sb("ob", [128, 512], F32)
            sq = sb("sq", [128, 512], F32)
            vo = sb("vo", [128, 512], F32)
            rso = sb("rso", [128, 512], F32)
            a1 = sb("a1", [128, 512], F32)
            a2 = sb("a2", [128, 512], F32)
            post_t = {n: Tok("post_" + n) for n in ("rz0", "rz1", "tt0", "tt1", "ob", "sq", "vo", "rso", "a1", "a2")}
            mst = [sb("mst%d" % i, [128, 512], BF16) for i in range(2)]
            mst_t = [Tok("mst%d" % i) for i in range(2)]

            spairs = Ring([(0, 1), (2, 3)])
            O1, O2, Z1, Z2 = 4, 5, 6, 7
            ering = Ring(list(range(NE)))

            def load_bh(i):
                b, hh = divmod(i, NH)
                s = i % 2
                P.dma("sp", kTb[s][:], kT_s[b, hh], ld_t[s]["k"], writes=[ld_t[s]["k"]])
                P.dma("sp", qTb[s][:], qT_s[b, hh], ld_t[s]["q"], writes=[ld_t[s]["q"]])
                P.dma("sp", vb_[s][:], v_s[b, :, hh * 128:(hh + 1) * 128].rearrange("(k p) d -> p k d", p=128),
                      ld_t[s]["v"], writes=[ld_t[s]["v"]])
                P.dma("sp", gAb[s][:], gA_s[b, hh], ld_t[s]["g"], writes=[ld_t[s]["g"]])
                P.dma("sp", ppb[s][:], pp_s[b, hh], ld_t[s]["p"], writes=[ld_t[s]["p"]])

            NBH = NB * NH
            load_bh(0)
            mcount = 0
            for i in range(NBH):
                b, hh = divmod(i, NH)
                s = i % 2
                if i + 1 < NBH:
                    load_bh(i + 1)
                L = ld_t[s]
                for j in range(NT):
                    nkb = 4 * j + 4
                    qc0 = j * 512

                    def s_mm(kb):
                        q0 = 128 * max(0, kb - 4 * j)
                        pr = spairs.next()
                        for c in range(2):
                            lo, hi = 64 * c, 64 * (c + 1)
                            P.op("pe", I("matmul",
                                banks[pr[c]][:, q0:512], lhsT=kTb[s][lo:hi, kb * 128:(kb + 1) * 128],
                                rhs=qTb[s][lo:hi, qc0 + q0:qc0 + 512], start=True, stop=True),
                                reads=[L["k"], L["q"]], writes=[bt[pr[c]]])
                        return pr, q0

                    def exp_pv(kb, pr, q0):
                        ei = ering.next()
                        P.op("act", I("activation", out=Et[ei][:, :, q0:512], in_=psum_all[:, pr[0]:pr[0] + 2, q0:512],
                                      func=AF.Exp, scale=0.125),
                             reads=[bt[pr[0]], bt[pr[1]]], writes=[Et_t[ei]])
                        if kb >= 4 * j:
                            P.op("pool", I("tensor_tensor", out=Et[ei][:, :, q0:q0 + 128], in0=Et[ei][:, :, q0:q0 + 128],
                                           in1=maskb[:].unsqueeze(1).to_broadcast([128, 2, 128]), op=ALU.mult),
                                 reads=[t_const], writes=[Et_t[ei]])
                        first, last = (kb == 0), (kb == nkb - 1)
                        for c, (ob_, zb_) in enumerate(((O1, Z1), (O2, Z2))):
                            P.op("pe", I("matmul",
                                banks[ob_][:, q0:512], lhsT=vb_[s][:, kb, :], rhs=Et[ei][:, c, q0:512],
                                start=first, stop=last), reads=[L["v"], Et_t[ei]], writes=[bt[ob_]], signal=False)
                            P.op("pe", I("matmul",
                                banks[zb_][:, q0:512], lhsT=onesb[:], rhs=Et[ei][:, c, q0:512],
                                start=first, stop=last), reads=[t_const, Et_t[ei]], writes=[bt[zb_]],
                                signal=(c == 1))

                    prev = s_mm(0)
                    for kb in range(nkb):
                        nxt = s_mm(kb + 1) if kb + 1 < nkb else None
                        exp_pv(kb, prev[0], prev[1])
                        prev = nxt
                    pt = post_t
                    P.op("dve", I("reciprocal", out=rz[0][:], in_=banks[Z1][:]), reads=[bt[Z1]], writes=[pt["rz0"]])
                    P.op("dve", I("reciprocal", out=rz[1][:], in_=banks[Z2][:]), reads=[bt[Z2]], writes=[pt["rz1"]])
                    P.op("dve", I("tensor_tensor", out=tt_[0][:], in0=banks[O1][:], in1=rz[0][:], op=ALU.mult),
                         reads=[bt[O1], pt["rz0"]], writes=[pt["tt0"]])
                    P.op("dve", I("tensor_tensor", out=tt_[1][:], in0=banks[O2][:], in1=rz[1][:], op=ALU.mult),
                         reads=[bt[O2], pt["rz1"]], writes=[pt["tt1"]])
                    P.op("dve", I("scalar_tensor_tensor", out=ob[:], in0=tt_[1][:], scalar=neglam[:, 0:1],
                                                                 in1=tt_[0][:], op0=ALU.mult, op1=ALU.add),
                         reads=[pt["tt0"], pt["tt1"], t_const], writes=[pt["ob"]])
                    P.op("act", I("activation", out=sq[:], in_=ob[:], func=AF.Square),
                         reads=[pt["ob"]], writes=[pt["sq"]])
                    pr = spairs.next()
                    P.op("pe", I("matmul", banks[pr[0]][:], lhsT=onesf[:], rhs=sq[:], start=True, stop=True),
                         reads=[pt["sq"], t_const], writes=[bt[pr[0]], bt[pr[1]]])
                    P.op("act", I("activation", out=vo[:], in_=banks[pr[0]][:], func=AF.Ln, scale=1.0 / 128,
                                  bias=eps_col[:, 0:1]), reads=[bt[pr[0]], t_const], writes=[pt["vo"]])
                    P.op("act", I("activation", out=rso[:], in_=vo[:], func=AF.Exp, scale=-0.5),
                         reads=[pt["vo"]], writes=[pt["rso"]])
                    P.op("dve", I("scalar_tensor_tensor", out=a1[:], in0=ob[:], scalar=gs_col[:, 0:1],
                                                                 in1=rso[:], op0=ALU.mult, op1=ALU.mult),
                         reads=[pt["ob"], pt["rso"], t_const], writes=[pt["a1"]])
                    P.op("dve", I("scalar_tensor_tensor", out=a2[:], in0=gAb[s][:, qc0:qc0 + 512], scalar=1.0,
                                                                 in1=a1[:], op0=ALU.add, op1=ALU.mult),
                         reads=[L["g"], pt["a1"]], writes=[pt["a2"]])
                    mi = mcount % 2
                    mcount += 1
                    P.op("pool", I("tensor_tensor", out=mst[mi][:], in0=a2[:],
                                                                  in1=ppb[s][:, qc0:qc0 + 512], op=ALU.add),
                         reads=[pt["a2"], L["p"]], writes=[mst_t[mi]])
                    P.dma("sp", m_s[b, hh, :, qc0:qc0 + 512], mst[mi][:], mst_t[mi], reads=[mst_t[mi]], store=True)
            P.wait_stores("sp")
            P.emit()

        with ExitStack() as st:
          if upto >= 3:
            P.stack = st

            def sb(name, shape, dt):
                return st.enter_context(nc.sbuf_tensor(name, list(shape), dt))

            xt = [sb("xc%d" % i, [128, 4, D], F32) for i in range(2)]
            xt_t = [[Tok("xc%d_%d" % (i, tb)) for tb in range(4)] for i in range(2)]
            mT = [sb("mT%d" % i, [128, 8, 512], BF16) for i in range(2)]
            mT_t = [Tok("mT%d" % i) for i in range(2)]
            junk = sb("junkC", [128, D], BF16)
            junk_t = Tok("junkC")
            ss = sb("ssC", [128, 8], F32)
            vv = sb("vvC", [128, 8], F32)
            rstd = sb("rstdC", [128, 8], F32)
            st2_t = Tok("stats2")
            st3_t = Tok("stats3")
            hb = sb("hbC", [128, 4, D], BF16)
            hb_t = [Tok("hbC%d" % i) for i in range(4)]
            h2T = sb("h2T", [128, 8, 512], BF16)
            h2T_t = [Tok("h2T%d" % k) for k in range(8)]
            zT = sb("zT", [128, 32, 512], BF16)
            zT_t = [Tok("zT%d" % k) for k in range(32)]
            NR = 3
            rt = [sb("rt%d" % i, [128, 512], F32) for i in range(NR)]
            rt_t = [Tok("rt%d" % i) for i in range(NR)]
            NWS = 4
            wsl = [sb("wslC%d" % i, [128, 8, 512], BF16) for i in range(NWS)]
            wsl_t = [Tok("wslC%d" % i) for i in range(NWS)]

            mmring = Ring([0, 1, 2, 3, 4, 5])
            trring = Ring([6, 7])
            rring = Ring(list(range(NR)))
            order = list(G_OUT) + list(G_UP) + list(G_DOWN)
            WS = WStream(P, wsl, wsl_t, wscr, order * NTT)

            def load_c(t):
                b, j = divmod(t, NT)
                P.dma("sp", xt[t % 2][:], x_d[b, j * 512:(j + 1) * 512, :].rearrange("(tb p) d -> p tb d", p=128),
                      xt_t[t % 2][0], writes=xt_t[t % 2])
                P.dma("sp", mT[t % 2][:], m_s[b, :, :, j * 512:(j + 1) * 512].rearrange("h p n -> p h n"),
                      mT_t[t % 2], writes=[mT_t[t % 2]])

            load_c(0)
            stageC = [0]

            def chkC():
                stageC[0] += 1
                if stageC[0] >= dbgC:
                    raise _Stop()

            def phaseC_body():
              for t in range(NTT):
                b, j = divmod(t, NT)
                xb, xtk = xt[t % 2], xt_t[t % 2]
                mb, mtk = mT[t % 2], mT_t[t % 2]
                if t + 1 < NTT:
                    load_c(t + 1)
                for nh in range(2):
                    Wg, wt = WS.get()
                    for tb in range(4):
                        bk = mmring.next()
                        for kc in range(8):
                            P.op("pe", I("matmul",
                                banks[bk][:], lhsT=mb[:, kc, tb * 128:(tb + 1) * 128], rhs=Wg[:, kc, :],
                                start=(kc == 0), stop=(kc == 7)), reads=[wt, mtk], writes=[bt[bk]], signal=(kc == 7))
                        P.op("dve", I("tensor_tensor",
                            out=xb[:, tb, nh * 512:(nh + 1) * 512], in0=banks[bk][:],
                            in1=xb[:, tb, nh * 512:(nh + 1) * 512], op=ALU.add), reads=[bt[bk], xtk[tb]], writes=[xtk[tb]])
                    WS.prefetch()
                chkC()
                for tb in range(4):
                    P.op("act", I("activation", out=junk[:], in_=xb[:, tb, :], func=AF.Square,
                                                              accum_out=ss[:, tb:tb + 1]),
                         reads=[xtk[tb]], writes=[junk_t, st2_t])
                P.op("dve", I("tensor_scalar", out=vv[:, 0:4], in0=ss[:, 0:4], scalar1=1.0 / D, scalar2=EPS,
                                                      op0=ALU.mult, op1=ALU.add), reads=[st2_t], writes=[st2_t])
                P.op("act", I("activation", out=vv[:, 0:4], in_=vv[:, 0:4], func=AF.Ln), reads=[st2_t], writes=[st2_t])
                P.op("act", I("activation", out=rstd[:, 0:4], in_=vv[:, 0:4], func=AF.Exp, scale=-0.5), reads=[st2_t], writes=[st2_t])
                for tb in range(4):
                    P.op("act", I("activation", out=hb[:, tb, :], in_=xb[:, tb, :], func=AF.Copy,
                                                              scale=rstd[:, tb:tb + 1]),
                         reads=[xtk[tb], st2_t], writes=[hb_t[tb]])
                for kc in range(8):
                    bk = trring.next()
                    pT = banks[bk][:].bitcast(BF16)
                    for tb in range(4):
                        P.op("pe", I("transpose",
                            out=pT[:, tb * 128:(tb + 1) * 128], in_=hb[:, tb, kc * 128:(kc + 1) * 128],
                            identity=identb[:]), reads=[hb_t[tb], t_const], writes=[bt[bk]], signal=(tb == 3))
                    P.op("dve", I("tensor_copy", out=h2T[:, kc, :], in_=pT[:, 0:512]),
                         reads=[bt[bk]], writes=[h2T_t[kc]])
                chkC()
                for g in range(8):
                    Wg, wt = WS.get()
                    for c in range(4):
                        bk = mmring.next()
                        for kc in range(8):
                            P.op("pe", I("matmul",
                                banks[bk][:], lhsT=Wg[:, kc, c * 128:(c + 1) * 128], rhs=h2T[:, kc, :],
                                start=(kc == 0), stop=(kc == 7)), reads=[wt, h2T_t[kc]], writes=[bt[bk]],
                                signal=(kc == 7))
                        ri = rring.next()
                        fi = 4 * g + c
                        P.op("act", I("activation", out=rt[ri][:], in_=banks[bk][:], func=AF.Relu),
                             reads=[bt[bk]], writes=[rt_t[ri]])
                        P.op("dve", I("tensor_tensor",
                            out=zT[:, fi, :], in0=banks[bk][:], in1=rt[ri][:], op=ALU.mult),
                            reads=[bt[bk], rt_t[ri]], writes=[zT_t[fi]])
                    WS.prefetch()
                chkC()
                for nh in range(2):
                    bks = [mmring.next() for _ in range(4)]
                    for fg in range(4):
                        Wg, wt = WS.get()
                        for tb in range(4):
                            bk = bks[tb]
                            for fc in range(8):
                                fi = 8 * fg + fc
                                P.op("pe", I("matmul",
                                    banks[bk][:], lhsT=zT[:, fi, tb * 128:(tb + 1) * 128], rhs=Wg[:, fc, :],
                                    start=(fg == 0 and fc == 0), stop=(fg == 3 and fc == 7)),
                                    reads=[wt, zT_t[fi]], writes=[bt[bk]], signal=(fc == 7))
                            if fg == 3:
                                P.op("dve", I("tensor_tensor",
                                    out=xb[:, tb, nh * 512:(nh + 1) * 512], in0=banks[bk][:],
                                    in1=xb[:, tb, nh * 512:(nh + 1) * 512], op=ALU.add),
                                    reads=[bt[bk], xtk[tb]], writes=[xtk[tb]])
                        WS.prefetch()
                chkC()
                for tb in range(4):
                    P.op("act", I("activation", out=junk[:], in_=xb[:, tb, :], func=AF.Square,
                                                              accum_out=ss[:, 4 + tb:5 + tb]),
                         reads=[xtk[tb]], writes=[junk_t, st3_t])
                P.op("dve", I("tensor_scalar", out=vv[:, 4:8], in0=ss[:, 4:8], scalar1=1.0 / D, scalar2=EPS,
                                                      op0=ALU.mult, op1=ALU.add), reads=[st3_t], writes=[st3_t])
                P.op("act", I("activation", out=vv[:, 4:8], in_=vv[:, 4:8], func=AF.Ln), reads=[st3_t], writes=[st3_t])
                P.op("act", I("activation", out=rstd[:, 4:8], in_=vv[:, 4:8], func=AF.Exp, scale=-0.5), reads=[st3_t], writes=[st3_t])
                for tb in range(4):
                    P.op("dve", I("scalar_tensor_tensor",
                        out=xb[:, tb, :], in0=xb[:, tb, :], scalar=rstd[:, 4 + tb:5 + tb], in1=gfin_bc[:],
                        op0=ALU.mult, op1=ALU.mult), reads=[xtk[tb], st3_t, t_const], writes=[xtk[tb]])
                P.dma("sp", y_d[b, j * 512:(j + 1) * 512, :].rearrange("(tb p) d -> p tb d", p=128), xb[:],
                      xtk[0], reads=xtk, store=True)
            try:
                phaseC_body()
            except _Stop:
                pass
            P.wait_stores("sp")
            P.emit()
    return nc


def _consts():
    ident = np.eye(128, dtype=np.float32)
    kk = np.arange(128)[:, None]
    qq = np.arange(128)[None, :]
    mask = (qq >= kk).astype(np.float32)
    invf = np.power(np.float32(500000.0), -(np.arange(0, 16, 2, dtype=np.float32) / np.float32(16))).astype(np.float32)
    wct = np.zeros((4, 16), np.float32)
    for c in range(4):
        w = 2 ** (c + 1)
        for t_ in range(16):
            wct[c, t_] = w / min(t_ + 1, w)
    return ident, mask, invf, wct


def prep_core_inputs(inputs, b0, NB):
    S = inputs["x"].shape[1]
    f = lambda a: np.ascontiguousarray(np.asarray(a, dtype=np.float32))
    ident, mask, invf, wct = _consts()
    pos = np.asarray(inputs["positions"])[b0:b0 + NB].astype(np.int32)
    pos_t = np.ascontiguousarray(pos.reshape(NB, S // 128, 128).transpose(0, 2, 1))
    return {
        "x": f(inputs["x"][b0:b0 + NB]),
        "pos_t": pos_t,
        "w_in": f(inputs["w_in"][0]),
        "w_out": f(inputs["w_out"][0]),
        "w_up": f(inputs["w_up"][0]),
        "w_down": f(inputs["w_down"][0]),
        "w_pool_r": f(np.asarray(inputs["w_pool"][0]).transpose(1, 0, 2)),
        "gcol_attn": f(np.asarray(inputs["norm_attn_g"][0]).reshape(8, 128).T),
        "gcol_mlp": f(np.asarray(inputs["norm_mlp_g"][0]).reshape(8, 128).T),
        "gfin": f(inputs["final_norm_g"]),
        "subln_col": f(np.asarray(inputs["subln_g"][0]).reshape(128, 1)),
        "pool_scale": f(inputs["pool_scale"][0]),
        "lam4": f(np.stack([np.asarray(inputs[k][0]) for k in ("lam_q1", "lam_k1", "lam_q2", "lam_k2")])),
        "c_ident": ident, "c_mask": mask, "c_invf": invf, "c_wct": wct,
    }


_NC_CACHE = {}


def kernel(**inputs):
    x = np.asarray(inputs["x"])
    B, S, _ = x.shape
    n = N_CORES
    NB = B // n
    key = (NB, S)
    if key not in _NC_CACHE:
        _NC_CACHE[key] = build_nc(NB, S)
    nc = _NC_CACHE[key]
    in_maps = [prep_core_inputs(inputs, i * NB, NB) for i in range(n)]
    res = run_bass_kernel_spmd(nc, in_maps, core_ids=list(range(n)))
    return np.concatenate([np.asarray(r["y"]) for r in res.results], axis=0).astype(np.float32)
```

```python
import math
from contextlib import ExitStack

import numpy as np
import concourse.bass as bass
import concourse.mybir as mybir
from concourse.bass_utils import run_bass_kernel_spmd

F32 = mybir.dt.float32
BF16 = mybir.dt.bfloat16
I32 = mybir.dt.int32
AF = mybir.ActivationFunctionType
ALU = mybir.AluOpType

D = 1024
NH = 8
DFF = 4096
INW = 5632
EPS = 1e-6
LAM_INIT = 0.8 - 0.6 * math.exp(-0.3 * 0)
N_CORES = 8
TWO_PI = 2.0 * math.pi
CW1 = 6.28125
CW2 = TWO_PI - CW1


class Tok:
    __slots__ = ("name", "w", "r", "dsem", "dkey", "dval")

    def __init__(self, name):
        self.name = name
        self.w = None
        self.r = {}
        self.dsem = None
        self.dkey = None
        self.dval = 0


class Eng:
    def __init__(self, name, sem):
        self.name = name
        self.sem = sem
        self.prog = []
        self.count = 0
        self.seen = {}
        self.nwait = 0
        self.nops = 0


class Prog:
    def __init__(self, nc, stack):
        self.nc = nc
        self.gstack = stack
        self.stack = stack
        self.engs = {}
        self.semmap = {}
        self.nkey = 0
        for name in ("pe", "act", "dve", "pool", "sp"):
            s = stack.enter_context(nc.semaphore("p_" + name))
            self.engs[name] = Eng(name, s)
        self.stores = []

    def new_dsem(self, name):
        s = self.gstack.enter_context(self.nc.semaphore(name))
        self.nkey += 1
        self.semmap[self.nkey] = s
        return s, self.nkey

    def _deps(self, reads, writes, extra):
        deps = {}

        def add(k, v):
            if deps.get(k, 0) < v:
                deps[k] = v
        for t in reads:
            if t.w is not None:
                add(*t.w)
        for t in writes:
            if t.w is not None:
                add(*t.w)
            for k, v in t.r.items():
                add(k, v)
        for ev in extra:
            if ev is not None:
                add(*ev)
        return deps

    def _wait(self, e, deps):
        for k, v in deps.items():
            if k == "pe" and e.name == "pe":
                continue
            if e.seen.get(k, 0) >= v:
                continue
            e.seen[k] = v
            sem = self.engs[k].sem if isinstance(k, str) else self.semmap[k]
            e.prog.append(("wait", sem, v))
            e.nwait += 1

    @staticmethod
    def _record(ev, reads, writes):
        k, v = ev
        for t in reads:
            if t.r.get(k, 0) < v:
                t.r[k] = v
        for t in writes:
            t.w = ev
            t.r = {}

    def op(self, eng, fn, reads=(), writes=(), signal=True, extra=()):
        e = self.engs[eng]
        self._wait(e, self._deps(reads, writes, extra))
        e.prog.append(("inst", fn, signal))
        e.nops += 1
        if signal:
            e.count += 1
            ev = (e.name, e.count)
        else:
            ev = (e.name, e.count + 1)
        self._record(ev, reads, writes)
        return ev

    def dma(self, q, out, in_, tok, reads=(), writes=(), extra=(), store=False):
        e = self.engs[q]
        self._wait(e, self._deps(reads, writes, extra))
        if tok.dsem is None:
            tok.dsem, tok.dkey = self.new_dsem("d_" + tok.name)
        e.prog.append(("dma", out, in_, tok.dsem))
        e.nops += 1
        tok.dval += 16
        ev = (tok.dkey, tok.dval)
        self._record(ev, reads, writes)
        if store:
            self.stores.append(ev)
        return ev

    def wait_toks(self, q, toks):
        e = self.engs[q]
        self._wait(e, self._deps(toks, (), ()))

    def wait_stores(self, q="sp"):
        e = self.engs[q]
        deps = {}
        for k, v in self.stores:
            if deps.get(k, 0) < v:
                deps[k] = v
        self.stores = []
        self._wait(e, deps)

    def emit(self):
        nc = self.nc
        with nc.Block() as block:
            def mk(e):
                def body(h):
                    for it in e.prog:
                        if it[0] == "wait":
                            h.wait_ge(it[1], it[2])
                        elif it[0] == "inst":
                            nm, a_, kw_ = it[1]
                            inst = getattr(h, nm)(*a_, **kw_)
                            if it[2]:
                                inst.then_inc(e.sem, 1)
                        else:
                            h.dma_start(out=it[1], in_=it[2]).then_inc(it[3], 16)
                    e.prog = []
                return body
            block.tensor(mk(self.engs["pe"]))
            block.scalar(mk(self.engs["act"]))
            block.vector(mk(self.engs["dve"]))
            block.gpsimd(mk(self.engs["pool"]))
            block.sync(mk(self.engs["sp"]))


def I(name, *args, **kw):
    return (name, args, kw)


class Ring:
    def __init__(self, items):
        self.items = items
        self.i = 0

    def next(self):
        it = self.items[self.i % len(self.items)]
        self.i += 1
        return it


class WStream:
    def __init__(self, P, slots, toks, wscr, seq, extra=()):
        self.P = P
        self.slots = slots
        self.toks = toks
        self.wscr = wscr
        self.seq = seq
        self.issued = 0
        self.used = 0
        self.extra = extra

    def _issue(self):
        i = self.issued
        s = i % len(self.slots)
        self.P.dma("sp", self.slots[s][:], self.wscr[self.seq[i]], self.toks[s],
                   writes=[self.toks[s]], extra=self.extra)
        self.issued += 1

    def get(self):
        i = self.used
        while self.issued <= i:
            self._issue()
        self.used += 1
        return self.slots[i % len(self.slots)], self.toks[i % len(self.slots)]

    def prefetch(self):
        R = len(self.slots)
        while self.issued < min(len(self.seq), self.used + R):
            self._issue()


class _Stop(Exception):
    pass


def build_nc(NB, S, upto=3, debug=False, dbgA=10 ** 9, dbgC=10 ** 9):
    NT = S // 512
    NBLK = S // 128
    NTT = NB * NT
    nc = bass.Bass("TRN2", target_bir_lowering=False)

    def din(name, shape, dt=F32):
        return nc.dram_tensor(name, list(shape), dt, kind="ExternalInput").ap()

    x_d = din("x", [NB, S, D])
    post_d = din("pos_t", [NB, 128, NBLK], I32)
    w_in_d = din("w_in", [D, INW])
    w_out_d = din("w_out", [D, D])
    w_up_d = din("w_up", [D, DFF])
    w_down_d = din("w_down", [DFF, D])
    wpool_d = din("w_pool_r", [128, 4, 256])
    gattn_d = din("gcol_attn", [128, 8])
    gmlp_d = din("gcol_mlp", [128, 8])
    gfin_d = din("gfin", [D])
    subln_d = din("subln_col", [128, 1])
    pscale_d = din("pool_scale", [D])
    lam_d = din("lam4", [4, 64])
    ident_d = din("c_ident", [128, 128])
    mask_d = din("c_mask", [128, 128])
    invf_d = din("c_invf", [8])
    wct_d = din("c_wct", [4, 16])
    y_d = nc.dram_tensor("y", [NB, S, D], F32, kind="ExternalOutput").ap()

    def dscr(name, shape, dt=BF16):
        return nc.dram_tensor(name, list(shape), dt, kind=("ExternalOutput" if debug else "Internal")).ap()

    NWG = 29
    wscr = dscr("wscr", [NWG, 128, 8, 512])
    qT_s = dscr("qT_s", [NB, NH, 128, S])
    kT_s = dscr("kT_s", [NB, NH, 128, S])
    v_s = dscr("v_s", [NB, S, D])
    gA_s = dscr("gA_s", [NB, NH, 128, S])
    pp_s = dscr("pp_s", [NB, NH, 128, S])
    m_s = dscr("m_s", [NB, NH, 128, S])

    G_Q, G_K, G_V, G_POOL, G_GA, G_GP = (0, 1), (2, 3), (4, 5), (6,), (7, 8), (9, 10)
    G_OUT = (11, 12)
    G_UP = tuple(range(13, 21))
    G_DOWN = tuple(range(21, 29))

    with ExitStack() as gst:
        P = Prog(nc, gst)

        def gsb(name, shape, dt):
            return gst.enter_context(nc.sbuf_tensor(name, list(shape), dt))

        psum_all = gst.enter_context(nc.psum_tensor("psum_all", [128, 8, 512], F32))
        banks = [psum_all[:, i, :] for i in range(8)]
        bt = [Tok("bank%d" % i) for i in range(8)]

        identb = gsb("identb", [128, 128], BF16)
        onesb = gsb("onesb", [128, 128], BF16)
        onesf = gsb("onesf", [128, 128], F32)
        maskb = gsb("maskb", [128, 128], BF16)
        gfin_bc = gsb("gfin_bc", [128, D], F32)
        wpoolb = gsb("wpoolb", [128, 4, 256], BF16)
        gattn = gsb("gattn", [128, 8], F32)
        gmlp = gsb("gmlp", [128, 8], F32)
        gs_col = gsb("gs_col", [128, 1], F32)
        neglam = gsb("neglam", [128, 1], F32)
        cosb = gsb("cosb", [128, NB * NBLK, 8], F32)
        sinb = gsb("sinb", [128, NB * NBLK, 8], F32)
        wct = gsb("wct", [128, 4, 16], F32)
        mhalf = gsb("mhalf", [128, 512], F32)
        eps_col = gsb("eps_col", [128, 1], F32)
        t_const = Tok("consts")

        with ExitStack() as st:
            P.stack = st

            def sb(name, shape, dt):
                return st.enter_context(nc.sbuf_tensor(name, list(shape), dt))

            idf = sb("idf", [128, 128], F32)
            mkf = sb("mkf", [128, 128], F32)
            wpf = sb("wpf", [128, 4, 256], F32)
            psb = sb("psb", [128, D], F32)
            lamt = sb("lamt", [128, 4, 64], F32)
            lprod = sb("lprod", [128, 2, 64], F32)
            lsum = sb("lsum", [128, 2], F32)
            lexp = sb("lexp", [128, 2], F32)
            subl = sb("subl", [128, 1], F32)
            posi = sb("posi", [128, NB * NBLK], I32)
            posf = sb("posf", [128, NB * NBLK], F32)
            invf = sb("invf", [128, 8], F32)
            NA = NB * NBLK * 8
            ang = sb("ang", [128, NA], F32)
            tq = sb("tq", [128, NA], F32)
            ki = sb("ki", [128, NA], I32)
            kf = sb("kf", [128, NA], F32)
            r0 = sb("r0", [128, NA], F32)
            r1 = sb("r1", [128, NA], F32)
            mk1 = sb("mk1", [128, NA], F32)
            rc = sb("rc", [128, NA], F32)
            tl = {n: Tok(n) for n in ("idf", "mkf", "wpf", "psb", "lamt", "subl", "posi", "invf", "gat", "gml",
                                      "gfin", "wct")}
            P.dma("sp", idf[:], ident_d, tl["idf"], writes=[tl["idf"]])
            P.dma("sp", mkf[:], mask_d, tl["mkf"], writes=[tl["mkf"]])
            P.dma("sp", wpf[:], wpool_d, tl["wpf"], writes=[tl["wpf"]])
            P.dma("sp", psb[:], pscale_d.partition_broadcast(128), tl["psb"], writes=[tl["psb"]])
            P.dma("sp", lamt[:], lam_d.partition_broadcast(128), tl["lamt"], writes=[tl["lamt"]])
            P.dma("sp", subl[:], subln_d, tl["subl"], writes=[tl["subl"]])
            P.dma("sp", posi[:].rearrange("p (b k) -> p b k", b=NB), post_d.rearrange("b p k -> p b k"),
                  tl["posi"], writes=[tl["posi"]])
            P.dma("sp", invf[:], invf_d.partition_broadcast(128), tl["invf"], writes=[tl["invf"]])
            P.dma("sp", gattn[:], gattn_d, tl["gat"], writes=[tl["gat"]])
            P.dma("sp", gmlp[:], gmlp_d, tl["gml"], writes=[tl["gml"]])
            P.dma("sp", gfin_bc[:], gfin_d.partition_broadcast(128), tl["gfin"], writes=[tl["gfin"]])
            P.dma("sp", wct[:], wct_d.partition_broadcast(128), tl["wct"], writes=[tl["wct"]])

            tc_ = t_const
            P.op("dve", I("tensor_copy", out=identb[:], in_=idf[:]), reads=[tl["idf"]], writes=[tc_])
            P.op("dve", I("tensor_copy", out=maskb[:], in_=mkf[:]), reads=[tl["mkf"]], writes=[tc_])
            P.op("pool", I("memset", onesb[:], 1.0), writes=[tc_])
            P.op("pool", I("memset", onesf[:], 1.0), writes=[tc_])
            P.op("pool", I("memset", mhalf[:], -0.5), writes=[tc_])
            P.op("pool", I("memset", eps_col[:], EPS), writes=[tc_])
            for g in range(4):
                wg_ = float(2 ** (g + 1))
                P.op("dve", I("scalar_tensor_tensor",
                    out=wpoolb[:, g, :], in0=wpf[:, g, :], scalar=0.5 / wg_, in1=psb[:, g * 256:(g + 1) * 256],
                    op0=ALU.mult, op1=ALU.mult), reads=[tl["wpf"], tl["psb"]], writes=[tc_])
            P.op("dve", I("tensor_scalar", out=gs_col[:], in0=subl[:], scalar1=0.5 * (1.0 - LAM_INIT),
                                                  scalar2=None, op0=ALU.mult), reads=[tl["subl"]], writes=[tc_])
            tlp = Tok("lprod")
            P.op("dve", I("tensor_tensor", out=lprod[:, 0, :], in0=lamt[:, 0, :], in1=lamt[:, 1, :],
                                                  op=ALU.mult), reads=[tl["lamt"]], writes=[tlp])
            P.op("dve", I("tensor_tensor", out=lprod[:, 1, :], in0=lamt[:, 2, :], in1=lamt[:, 3, :],
                                                  op=ALU.mult), reads=[tl["lamt"], tlp], writes=[tlp])
            P.op("dve", I("tensor_reduce", out=lsum[:], in_=lprod[:], op=ALU.add,
                                                  axis=mybir.AxisListType.X), reads=[tlp], writes=[tlp])
            P.op("act", I("activation", out=lexp[:], in_=lsum[:], func=AF.Exp), reads=[tlp], writes=[tlp])
            P.op("dve", I("tensor_tensor", out=lsum[:, 0:1], in0=lexp[:, 1:2], in1=lexp[:, 0:1],
                                                  op=ALU.subtract), reads=[tlp], writes=[tlp])
            P.op("dve", I("tensor_scalar", out=neglam[:], in0=lsum[:, 0:1], scalar1=-LAM_INIT, scalar2=None,
                                                  op0=ALU.add), reads=[tlp], writes=[tc_])
            ta = Tok("ang")
            P.op("dve", I("tensor_copy", out=posf[:], in_=posi[:]), reads=[tl["posi"]], writes=[ta])
            P.op("dve", I("tensor_tensor",
                out=ang[:].rearrange("p (k f) -> p k f", f=8),
                in0=posf[:].unsqueeze(2).to_broadcast([128, NB * NBLK, 8]),
                in1=invf[:].unsqueeze(1).to_broadcast([128, NB * NBLK, 8]), op=ALU.mult),
                reads=[tl["invf"], ta], writes=[ta])

            def reduce_to(dst, src_shift):
                P.op("dve", I("tensor_scalar", out=tq[:], in0=ang[:], scalar1=src_shift, scalar2=1.0 / TWO_PI,
                                                      op0=ALU.add, op1=ALU.mult), reads=[ta], writes=[ta])
                P.op("dve", I("tensor_copy", out=ki[:], in_=tq[:]), reads=[ta], writes=[ta])
                P.op("dve", I("tensor_copy", out=kf[:], in_=ki[:]), reads=[ta], writes=[ta])
                P.op("dve", I("tensor_scalar", out=rc[:], in0=ang[:], scalar1=src_shift, scalar2=None,
                                                      op0=ALU.add), reads=[ta], writes=[ta])
                P.op("dve", I("scalar_tensor_tensor", out=r0[:], in0=kf[:], scalar=-CW1, in1=rc[:],
                                                             op0=ALU.mult, op1=ALU.add), reads=[ta], writes=[ta])
                P.op("dve", I("scalar_tensor_tensor", out=r1[:], in0=kf[:], scalar=-CW2, in1=r0[:],
                                                             op0=ALU.mult, op1=ALU.add), reads=[ta], writes=[ta])
                P.op("dve", I("tensor_scalar", out=mk1[:], in0=r1[:], scalar1=math.pi, scalar2=-TWO_PI,
                                                      op0=ALU.is_gt, op1=ALU.mult), reads=[ta], writes=[ta])
                P.op("dve", I("tensor_tensor", out=r0[:], in0=r1[:], in1=mk1[:], op=ALU.add),
                     reads=[ta], writes=[ta])
                P.op("dve", I("tensor_scalar", out=mk1[:], in0=r0[:], scalar1=-math.pi, scalar2=TWO_PI,
                                                      op0=ALU.is_lt, op1=ALU.mult), reads=[ta], writes=[ta])
                P.op("dve", I("tensor_tensor", out=r1[:], in0=r0[:], in1=mk1[:], op=ALU.add),
                     reads=[ta], writes=[ta])
                P.op("dve", I("tensor_scalar", out=r1[:], in0=r1[:], scalar1=math.pi, scalar2=-math.pi,
                                                      op0=ALU.min, op1=ALU.max), reads=[ta], writes=[ta])
                P.op("act", I("activation", out=dst[:].rearrange("p k f -> p (k f)"), in_=r1[:], func=AF.Sin),
                     reads=[ta], writes=[ta, tc_])

            reduce_to(sinb, 0.0)
            reduce_to(cosb, 0.5 * math.pi)

            NST = 4
            wst = [sb("wst%d" % i, [128, 8, 512], F32) for i in range(NST)]
            wbo = [sb("wbo%d" % i, [128, 8, 512], BF16) for i in range(NST)]
            wst_t = [Tok("wst%d" % i) for i in range(NST)]
            wbo_t = [Tok("wbo%d" % i) for i in range(NST)]
            w_in_v = w_in_d.rearrange("(kc p) n -> p kc n", p=128)
            w_out_v = w_out_d.rearrange("(kc p) n -> p kc n", p=128)
            w_up_v = w_up_d.rearrange("(kc p) n -> p kc n", p=128)
            w_dn_v = w_down_d.rearrange("(a p) n -> p a n", p=128)
            jobs = []
            for g in range(11):
                jobs.append((g, w_in_v[:, :, g * 512:(g + 1) * 512], gattn, "gat"))
            for g in range(2):
                jobs.append((11 + g, w_out_v[:, :, g * 512:(g + 1) * 512], None, None))
            for g in range(8):
                jobs.append((13 + g, w_up_v[:, :, g * 512:(g + 1) * 512], gmlp, "gml"))
            for nh in range(2):
                for fg in range(4):
                    jobs.append((21 + nh * 4 + fg, w_dn_v[:, fg * 8:(fg + 1) * 8, nh * 512:(nh + 1) * 512], None, None))
            engs3 = ["dve", "act", "dve"]
            for i, (gid, src, gcol, gk) in enumerate(jobs):
                s = i % NST
                P.dma("sp", wst[s][:], src, wst_t[s], writes=[wst_t[s]])
                eng = engs3[i % 3]
                if gcol is None:
                    if eng == "act":
                        P.op("act", I("activation", out=wbo[s][:], in_=wst[s][:], func=AF.Copy),
                             reads=[wst_t[s]], writes=[wbo_t[s]])
                    else:
                        P.op(eng, I("tensor_copy", out=wbo[s][:], in_=wst[s][:]),
                             reads=[wst_t[s]], writes=[wbo_t[s]])
                else:
                    for kc in range(8):
                        if eng == "act":
                            P.op("act", I("activation",
                                out=wbo[s][:, kc, :], in_=wst[s][:, kc, :], func=AF.Copy, scale=gcol[:, kc:kc + 1]),
                                reads=[wst_t[s], tl[gk]], writes=[wbo_t[s]])
                        else:
                            P.op(eng, I("tensor_scalar",
                                out=wbo[s][:, kc, :], in0=wst[s][:, kc, :], scalar1=gcol[:, kc:kc + 1], scalar2=None,
                                op0=ALU.mult), reads=[wst_t[s], tl[gk]], writes=[wbo_t[s]])
                P.dma("sp", wscr[gid], wbo[s][:], wbo_t[s], reads=[wbo_t[s]], store=True)
            P.wait_stores("sp")
            P.wait_toks("sp", list(tl.values()))
            P.emit()

        with ExitStack() as st:
          if upto >= 1:
            P.stack = st

            def sb(name, shape, dt):
                return st.enter_context(nc.sbuf_tensor(name, list(shape), dt))

            xt = [sb("xt%d" % i, [128, 4, D], F32) for i in range(2)]
            xt_t = [Tok("xt%d" % i) for i in range(2)]
            junk = sb("junkA", [128, D], BF16)
            junk_t = Tok("junkA")
            ss = sb("ssA", [128, 4], F32)
            vv = sb("vvA", [128, 4], F32)
            rstd = sb("rstdA", [128, 4], F32)
            st_t = Tok("statsA")
            hb = sb("hbA", [128, 4, D], BF16)
            hb_t = [Tok("hbA%d" % i) for i in range(4)]
            hT = [sb("hT%d" % i, [128, 8, 512], BF16) for i in range(2)]
            hT_t = [[Tok("hT%d_%d" % (i, k)) for k in range(8)] for i in range(2)]
            NWS = 3
            wsl = [sb("wslA%d" % i, [128, 8, 512], BF16) for i in range(NWS)]
            wsl_t = [Tok("wslA%d" % i) for i in range(NWS)]
            qkt = [sb("qkt%d" % i, [128, 4, 512], BF16) for i in range(3)]
            qkt_t = [[Tok("qkt%d_%d" % (i, tb)) for tb in range(4)] for i in range(3)]
            NQS = 4
            qst = [sb("qst%d" % i, [128, 512], BF16) for i in range(NQS)]
            qst_t = [Tok("qst%d" % i) for i in range(NQS)]
            vtk = [sb("vtk%d" % i, [128, 4, 512], BF16) for i in range(2)]
            vtk_t = [Tok("vtk%d" % i) for i in range(2)]
            tg = sb("tgA", [128, 16, 512], BF16)
            tg_t = [Tok("tg%d" % i) for i in range(16)]
            up = [sb("up%d" % i, [128, 4, 528], F32) for i in range(2)]
            up_t = [[Tok("up%d_%d" % (i, c)) for c in range(4)] for i in range(2)]
            T1 = sb("poolT1", [128, 528], F32)
            T2 = sb("poolT2", [128, 528], F32)
            T3 = sb("poolT3", [128, 528], F32)
            pl_t = Tok("poolT")
            dT = sb("dT", [128, 4, 512], BF16)
            dT_t = [Tok("dT%d" % c) for c in range(4)]
            NPS = 3
            pst = [sb("pst%d" % i, [128, 512], BF16) for i in range(NPS)]
            pst_t = [Tok("pst%d" % i) for i in range(NPS)]
            ra = sb("ropeA", [128, 4, 64], F32)
            ra_t4 = [Tok("ropeA%d" % i) for i in range(4)]
            fixt = sb("fixt", [128, 16], F32)
            NQF = 3
            qf = [sb("qf%d" % i, [128, 512], F32) for i in range(NQF)]
            qf_t = [Tok("qf%d" % i) for i in range(NQF)]
            qfr = Ring(list(range(NQF)))

            mmring = Ring([0, 1, 2, 3, 4, 5])
            trring = Ring([6, 7])
            qsr = Ring(list(range(NQS)))
            psr = Ring(list(range(NPS)))

            order = [G_POOL[0], G_GP[0], G_GP[1], G_GA[0], G_GA[1], G_Q[0], G_Q[1], G_K[0], G_K[1], G_V[0], G_V[1]]
            WS = WStream(P, wsl, wsl_t, wscr, order * NTT)

            def load_x(t):
                b, j = divmod(t, NT)
                P.dma("sp", xt[t % 2][:], x_d[b, j * 512:(j + 1) * 512, :].rearrange("(tb p) d -> p tb d", p=128),
                      xt_t[t % 2], writes=[xt_t[t % 2]])

            def prep(t):
                xb, xtok = xt[t % 2], xt_t[t % 2]
                for tb in range(4):
                    P.op("act", I("activation", out=junk[:], in_=xb[:, tb, :], func=AF.Square,
                                                              accum_out=ss[:, tb:tb + 1]),
                         reads=[xtok], writes=[junk_t, st_t])
                P.op("dve", I("tensor_scalar", out=vv[:], in0=ss[:], scalar1=1.0 / D, scalar2=EPS,
                                                      op0=ALU.mult, op1=ALU.add), reads=[st_t], writes=[st_t])
                P.op("act", I("activation", out=vv[:], in_=vv[:], func=AF.Ln), reads=[st_t], writes=[st_t])
                P.op("act", I("activation", out=rstd[:], in_=vv[:], func=AF.Exp, scale=-0.5), reads=[st_t], writes=[st_t])
                for tb in range(4):
                    P.op("act", I("activation", out=hb[:, tb, :], in_=xb[:, tb, :], func=AF.Copy,
                                                              scale=rstd[:, tb:tb + 1]),
                         reads=[xtok, st_t], writes=[hb_t[tb]])
                for kc in range(8):
                    bk = trring.next()
                    pT = banks[bk][:].bitcast(BF16)
                    for tb in range(4):
                        P.op("pe", I("transpose",
                            out=pT[:, tb * 128:(tb + 1) * 128], in_=hb[:, tb, kc * 128:(kc + 1) * 128],
                            identity=identb[:]), reads=[hb_t[tb], t_const], writes=[bt[bk]], signal=(tb == 3))
                    P.op("dve", I("tensor_copy", out=hT[t % 2][:, kc, :], in_=pT[:, 0:512]),
                         reads=[bt[bk]], writes=[hT_t[t % 2][kc]])

            def fm_group(t, Wg, wt, evac):
                for c in range(4):
                    bk = mmring.next()
                    for kc in range(8):
                        P.op("pe", I("matmul",
                            banks[bk][:], lhsT=Wg[:, kc, c * 128:(c + 1) * 128], rhs=hT[t % 2][:, kc, :],
                            start=(kc == 0), stop=(kc == 7)),
                            reads=[wt, hT_t[t % 2][kc]], writes=[bt[bk]], signal=(kc == 7))
                    evac(c, bk)

            def tm_group(t, Wg, wt, evac):
                for tb in range(4):
                    bk = mmring.next()
                    for kc in range(8):
                        P.op("pe", I("matmul",
                            banks[bk][:], lhsT=hT[t % 2][:, kc, tb * 128:(tb + 1) * 128], rhs=Wg[:, kc, :],
                            start=(kc == 0), stop=(kc == 7)),
                            reads=[wt, hT_t[t % 2][kc]], writes=[bt[bk]], signal=(kc == 7))
                    evac(tb, bk)

            deferred = []

            def run_deferred(now):
                keep = []
                for when, fn in deferred:
                    if when <= now:
                        fn()
                    else:
                        keep.append((when, fn))
                deferred[:] = keep

            step = [0]
            stage = [0]

            def chk():
                stage[0] += 1
                if stage[0] >= dbgA:
                    raise _Stop()

            def phaseA_body():
              for t in range(NTT):
                b, j = divmod(t, NT)
                if t == 0:
                    load_x(0)
                    prep(0)
                    chk()
                if t + 1 < NTT:
                    load_x(t + 1)
                ub = up[t % 2]
                ubt = up_t[t % 2]
                pub = up[(t + 1) % 2]
                pubt = up_t[(t + 1) % 2]

                Wg, wt = WS.get()

                def ev_pool(c, bk):
                    P.op("dve", I("tensor_copy", out=ub[:, c, 16:528], in_=banks[bk][:]),
                         reads=[bt[bk]], writes=[ubt[c]])
                    if j == 0:
                        P.op("pool", I("memset", ub[:, c, 0:16], 0.0), writes=[ubt[c]])
                    else:
                        P.op("pool", I("tensor_copy", out=ub[:, c, 0:16], in_=pub[:, c, 512:528]),
                             reads=[pubt[c]], writes=[ubt[c]])
                    U = ub[:, c, :]
                    w = 2 ** (c + 1)
                    dst = dT[:, c, :]
                    if c == 0:
                        P.op("pool", I("tensor_tensor", out=dst, in0=U[:, 15:527], in1=U[:, 16:528],
                                                               op=ALU.subtract), reads=[ubt[c]], writes=[dT_t[c]])
                        if j == 0:
                            P.op("pool", I("memset", dst[:, 0:1], 0.0), writes=[dT_t[c]])
                        return
                    P.op("pool", I("tensor_tensor", out=T1[:, 1:528], in0=U[:, 1:528], in1=U[:, 0:527],
                                                           op=ALU.add), reads=[ubt[c]], writes=[pl_t])
                    P.op("pool", I("tensor_tensor", out=T2[:, 3:528], in0=T1[:, 3:528], in1=T1[:, 1:526],
                                                           op=ALU.add), reads=[pl_t], writes=[pl_t])
                    cur = T2
                    if c >= 2:
                        P.op("pool", I("tensor_tensor", out=T1[:, 7:528], in0=T2[:, 7:528], in1=T2[:, 3:524],
                                                               op=ALU.add), reads=[pl_t], writes=[pl_t])
                        cur = T1
                    if c >= 3:
                        P.op("pool", I("tensor_tensor", out=T2[:, 15:528], in0=T1[:, 15:528],
                                                               in1=T1[:, 7:520], op=ALU.add),
                             reads=[pl_t], writes=[pl_t])
                        cur = T2
                    P.op("pool", I("tensor_scalar", out=T3[:, 16:528], in0=U[:, 16:528], scalar1=-float(w),
                                                           scalar2=None, op0=ALU.mult),
                         reads=[ubt[c], pl_t], writes=[pl_t])
                    P.op("pool", I("tensor_tensor", out=dst, in0=cur[:, 16:528], in1=T3[:, 16:528],
                                                           op=ALU.add), reads=[pl_t], writes=[dT_t[c]])
                    if j == 0:
                        P.op("pool", I("tensor_tensor", out=fixt[:, 0:w - 1], in0=cur[:, 16:16 + w - 1],
                                                               in1=wct[:, c, 0:w - 1], op=ALU.mult),
                             reads=[pl_t, t_const], writes=[pl_t])
                        P.op("pool", I("tensor_tensor", out=dst[:, 0:w - 1], in0=fixt[:, 0:w - 1],
                                                               in1=T3[:, 16:16 + w - 1], op=ALU.add),
                             reads=[pl_t], writes=[dT_t[c], pl_t])

                fm_group(t, Wg, wt, ev_pool)
                chk()
                WS.prefetch()

                for gi, base in ((0, 8), (1, 12), (2, 0), (3, 4)):
                    Wg, wt = WS.get()

                    def ev_gate(c, bk, base=base):
                        idx = base + c
                        P.op("act", I("activation", out=tg[:, idx, :], in_=banks[bk][:], func=AF.Tanh,
                                                           scale=0.5), reads=[bt[bk]], writes=[tg_t[idx]])
                        if idx < 8:
                            P.dma("sp", gA_s[b, idx, :, j * 512:(j + 1) * 512], tg[:, idx, :], tg_t[idx],
                                  reads=[tg_t[idx]], store=True)
                    fm_group(t, Wg, wt, ev_gate)
                    chk()
                    WS.prefetch()
                    if gi == 1:
                        def do_pool_out(b=b, j=j):
                            for c in range(4):
                                for e in range(2):
                                    bk = mmring.next()
                                    P.op("pe", I("matmul",
                                        banks[bk][:], lhsT=wpoolb[:, c, e * 128:(e + 1) * 128], rhs=dT[:, c, :],
                                        start=True, stop=True), reads=[dT_t[c], t_const], writes=[bt[bk]])
                                    ps_i = psr.next()
                                    idx = 8 + 2 * c + e
                                    P.op("dve", I("scalar_tensor_tensor",
                                        out=pst[ps_i][:], in0=tg[:, idx, :], scalar=1.0, in1=banks[bk][:],
                                        op0=ALU.add, op1=ALU.mult), reads=[bt[bk], tg_t[idx]], writes=[pst_t[ps_i]])
                                    P.dma("sp", pp_s[b, 2 * c + e, :, j * 512:(j + 1) * 512], pst[ps_i][:], pst_t[ps_i],
                                          reads=[pst_t[ps_i]], store=True)


                        deferred.append((step[0] + 4, do_pool_out))

                for kind, scr in (("q", qT_s), ("k", kT_s)):
                    for half in range(2):
                        Wg, wt = WS.get()
                        qb = step[0] % 3
                        step[0] += 1
                        qk_ = qkt[qb]
                        qk_tt = qkt_t[qb]

                        def ev_qk(tb, bk, qk_=qk_, qk_tt=qk_tt):
                            qi_ = qfr.next()
                            P.op("act", I("activation", out=qf[qi_][:], in_=banks[bk][:], func=AF.Copy),
                                 reads=[bt[bk]], writes=[qf_t[qi_]])
                            src = qf[qi_][:].rearrange("p (a d) -> p a d", a=8)
                            dst = qk_[:, tb, :].rearrange("p (a d) -> p a d", a=8)
                            blk = b * NBLK + 4 * j + tb
                            cs = cosb[:, blk, :].unsqueeze(1).to_broadcast([128, 8, 8])
                            sn = sinb[:, blk, :].unsqueeze(1).to_broadcast([128, 8, 8])
                            rav = ra[:].rearrange("p k (a f) -> p k a f", a=8)
                            P.op("act", I("activation", out=dst[:, :, 16:64], in_=src[:, :, 16:64], func=AF.Copy),
                                 reads=[qf_t[qi_]], writes=[qk_tt[tb]])
                            t1, t2 = src[:, :, 0:8], src[:, :, 8:16]
                            P.op("dve", I("tensor_tensor", out=rav[:, 0], in0=t1, in1=cs, op=ALU.mult),
                                 reads=[qf_t[qi_], t_const], writes=[ra_t4[0]])
                            P.op("dve", I("tensor_tensor", out=rav[:, 1], in0=t2, in1=sn, op=ALU.mult),
                                 reads=[qf_t[qi_], t_const], writes=[ra_t4[1]])
                            P.op("dve", I("tensor_tensor", out=rav[:, 2], in0=t2, in1=cs, op=ALU.mult),
                                 reads=[qf_t[qi_], t_const], writes=[ra_t4[2]])
                            P.op("dve", I("tensor_tensor", out=rav[:, 3], in0=t1, in1=sn, op=ALU.mult),
                                 reads=[qf_t[qi_], t_const], writes=[ra_t4[3]])
                            P.op("dve", I("tensor_tensor", out=dst[:, :, 0:8], in0=rav[:, 0], in1=rav[:, 1],
                                          op=ALU.subtract), reads=[ra_t4[0], ra_t4[1]], writes=[qk_tt[tb]])
                            P.op("dve", I("tensor_tensor", out=dst[:, :, 8:16], in0=rav[:, 2], in1=rav[:, 3],
                                          op=ALU.add), reads=[ra_t4[2], ra_t4[3]], writes=[qk_tt[tb]])

                        tm_group(t, Wg, wt, ev_qk)
                        chk()
                        WS.prefetch()

                        def do_tr(qk_=qk_, qk_tt=qk_tt, half=half, scr=scr, b=b, j=j):
                            for c in range(4):
                                bk = trring.next()
                                pT = banks[bk][:].bitcast(BF16)
                                for tb in range(4):
                                    P.op("pe", I("transpose",
                                        out=pT[:, tb * 128:(tb + 1) * 128], in_=qk_[:, tb, c * 128:(c + 1) * 128],
                                        identity=identb[:]), reads=[qk_tt[tb], t_const], writes=[bt[bk]],
                                        signal=(tb == 3))
                                qi = qsr.next()
                                P.op("dve", I("tensor_copy", out=qst[qi][:], in_=pT[:, 0:512]),
                                     reads=[bt[bk]], writes=[qst_t[qi]])
                                P.dma("sp", scr[b, half * 4 + c, :, j * 512:(j + 1) * 512], qst[qi][:], qst_t[qi],
                                      reads=[qst_t[qi]], store=True)
                        run_deferred(step[0])
                        deferred.append((step[0] + 2, do_tr))
                        if kind == "q" and half == 0 and t + 1 < NTT:
                            prep(t + 1)

                for half in range(2):
                    Wg, wt = WS.get()
                    vb = step[0] % 2
                    step[0] += 1

                    def ev_v(tb, bk, vb=vb):
                        if tb % 2 == 0:
                            P.op("act", I("activation", out=vtk[vb][:, tb, :], in_=banks[bk][:], func=AF.Copy),
                                 reads=[bt[bk]], writes=[vtk_t[vb]])
                        else:
                            P.op("dve", I("tensor_copy", out=vtk[vb][:, tb, :], in_=banks[bk][:]),
                                 reads=[bt[bk]], writes=[vtk_t[vb]])
                    tm_group(t, Wg, wt, ev_v)
                    chk()
                    WS.prefetch()
                    P.dma("sp", v_s[b, j * 512:(j + 1) * 512, (half) * 512:(half + 1) * 512].rearrange(
                        "(tb p) d -> p tb d", p=128), vtk[vb][:], vtk_t[vb], reads=[vtk_t[vb]], store=True)
                    run_deferred(step[0])
                run_deferred(10 ** 9)
            try:
                phaseA_body()
            except _Stop:
                pass
            P.wait_stores("sp")
            P.emit()

        with ExitStack() as st:
          if upto >= 2:
            P.stack = st

            def sb(name, shape, dt):
                return st.enter_context(nc.sbuf_tensor(name, list(shape), dt))

            kTb = [sb("kTb%d" % i, [128, S], BF16) for i in range(2)]
            qTb = [sb("qTb%d" % i, [128, S], BF16) for i in range(2)]
            vb_ = [sb("vb%d" % i, [128, NBLK, 128], BF16) for i in range(2)]
            gAb = [sb("gAb%d" % i, [128, S], BF16) for i in range(2)]
            ppb = [sb("ppb%d" % i, [128, S], BF16) for i in range(2)]
            ld_t = [{n: Tok("%s%d" % (n, i)) for n in ("k", "q", "v", "g", "p")} for i in range(2)]
            NE = 4
            Et = [sb("E%d" % i, [128, 2, 512], BF16) for i in range(NE)]
            Et_t = [Tok("E%d" % i) for i in range(NE)]
            rz = [sb("rz%d" % c, [128, 512], F32) for c in range(2)]
            tt_ = [sb("tt%d" % c, [128, 512], F32) for c in range(2)]
            ob = sb("ob", [128, 512], F32)
            sq = sb("sq", [128, 512], F32)
            vo = sb("vo", [128, 512], F32)
            rso = sb("rso", [128, 512], F32)
            a1 = sb("a1", [128, 512], F32)
            a2 = sb("a2", [128, 512], F32)
            post_t = {n: Tok("post_" + n) for n in ("rz0", "rz1", "tt0", "tt1", "ob", "sq", "vo", "rso", "a1", "a2")}
            mst = [sb("mst%d" % i, [128, 512], BF16) for i in range(2)]
            mst_t = [Tok("mst%d" % i) for i in range(2)]
            Ocp = [sb("Ocp%d" % i, [128, 2, 512], F32) for i in range(2)]
            Zcp = [sb("Zcp%d" % i, [128, 2, 512], F32) for i in range(2)]
            Ocp_t = [Tok("Ocp%d" % i) for i in range(2)]
            Zcp_t = [Tok("Zcp%d" % i) for i in range(2)]
            ob2 = [sb("ob2_%d" % i, [128, 512], F32) for i in range(2)]
            ob2_t = [Tok("ob2_%d" % i) for i in range(2)]
            sq2 = [sb("sq2_%d" % i, [128, 512], F32) for i in range(2)]
            sq2_t = [Tok("sq2_%d" % i) for i in range(2)]
            pending = []

            spairs = Ring([(0, 1), (2, 3)])
            O1, O2, Z1, Z2 = 4, 5, 6, 7
            ering = Ring(list(range(NE)))

            def load_bh(i):
                b, hh = divmod(i, NH)
                s = i % 2
                P.dma("sp", kTb[s][:], kT_s[b, hh], ld_t[s]["k"], writes=[ld_t[s]["k"]])
                P.dma("sp", qTb[s][:], qT_s[b, hh], ld_t[s]["q"], writes=[ld_t[s]["q"]])
                P.dma("sp", vb_[s][:], v_s[b, :, hh * 128:(hh + 1) * 128].rearrange("(k p) d -> p k d", p=128),
                      ld_t[s]["v"], writes=[ld_t[s]["v"]])
                P.dma("sp", gAb[s][:], gA_s[b, hh], ld_t[s]["g"], writes=[ld_t[s]["g"]])
                P.dma("sp", ppb[s][:], pp_s[b, hh], ld_t[s]["p"], writes=[ld_t[s]["p"]])

            NBH = NB * NH
            load_bh(0)
            mcount = 0
            for i in range(NBH):
                b, hh = divmod(i, NH)
                s = i % 2
                L = ld_t[s]
                for j in range(NT):
                    nkb = 4 * j + 4
                    qc0 = j * 512
                    if j == 1 or (NT == 1 and j == 0):
                        if NT == 1:
                            while pending:
                                pending.pop(0)()
                        if i + 1 < NBH:
                            load_bh(i + 1)

                    def s_mm(kb):
                        q0 = 128 * max(0, kb - 4 * j)
                        pr = spairs.next()
                        for c in range(2):
                            lo, hi = 64 * c, 64 * (c + 1)
                            P.op("pe", I("matmul",
                                banks[pr[c]][:, q0:512], lhsT=kTb[s][lo:hi, kb * 128:(kb + 1) * 128],
                                rhs=qTb[s][lo:hi, qc0 + q0:qc0 + 512], start=True, stop=True),
                                reads=[L["k"], L["q"]], writes=[bt[pr[c]]])
                        return pr, q0

                    def exp_pv(kb, pr, q0):
                        ei = ering.next()
                        P.op("act", I("activation", out=Et[ei][:, :, q0:512], in_=psum_all[:, pr[0]:pr[0] + 2, q0:512],
                                      func=AF.Exp, scale=0.125),
                             reads=[bt[pr[0]], bt[pr[1]]], writes=[Et_t[ei]])
                        if kb >= 4 * j:
                            P.op("pool", I("tensor_tensor", out=Et[ei][:, :, q0:q0 + 128], in0=Et[ei][:, :, q0:q0 + 128],
                                           in1=maskb[:].unsqueeze(1).to_broadcast([128, 2, 128]), op=ALU.mult),
                                 reads=[t_const], writes=[Et_t[ei]])
                        first, last = (kb == 0), (kb == nkb - 1)
                        for c, (ob_, zb_) in enumerate(((O1, Z1), (O2, Z2))):
                            P.op("pe", I("matmul",
                                banks[ob_][:, q0:512], lhsT=vb_[s][:, kb, :], rhs=Et[ei][:, c, q0:512],
                                start=first, stop=last), reads=[L["v"], Et_t[ei]], writes=[bt[ob_]], signal=False)
                            P.op("pe", I("matmul",
                                banks[zb_][:, q0:512], lhsT=onesb[:], rhs=Et[ei][:, c, q0:512],
                                start=first, stop=last), reads=[t_const, Et_t[ei]], writes=[bt[zb_]],
                                signal=(c == 1))

                    prev = s_mm(0)
                    trig = min(9, nkb - 1)
                    for kb in range(nkb):
                        nxt = s_mm(kb + 1) if kb + 1 < nkb else None
                        exp_pv(kb, prev[0], prev[1])
                        prev = nxt
                        if kb == trig and pending:
                            pending.pop(0)()
                    pi_ = mcount % 2
                    P.op("act", I("activation", out=Ocp[pi_][:], in_=psum_all[:, 4:6, :], func=AF.Copy),
                         reads=[bt[O1], bt[O2]], writes=[Ocp_t[pi_]])
                    P.op("dve", I("tensor_copy", out=Zcp[pi_][:], in_=psum_all[:, 6:8, :]),
                         reads=[bt[Z1], bt[Z2]], writes=[Zcp_t[pi_]])
                    P.op("dve", I("reciprocal", out=Zcp[pi_][:], in_=Zcp[pi_][:]), reads=[Zcp_t[pi_]], writes=[Zcp_t[pi_]])
                    P.op("dve", I("tensor_tensor", out=Ocp[pi_][:], in0=Ocp[pi_][:], in1=Zcp[pi_][:], op=ALU.mult),
                         reads=[Zcp_t[pi_], Ocp_t[pi_]], writes=[Ocp_t[pi_]])
                    P.op("dve", I("scalar_tensor_tensor", out=ob2[pi_][:], in0=Ocp[pi_][:, 1, :], scalar=neglam[:, 0:1],
                                  in1=Ocp[pi_][:, 0, :], op0=ALU.mult, op1=ALU.add),
                         reads=[Ocp_t[pi_], t_const], writes=[ob2_t[pi_]])
                    P.op("act", I("activation", out=sq2[pi_][:], in_=ob2[pi_][:], func=AF.Square),
                         reads=[ob2_t[pi_]], writes=[sq2_t[pi_]])
                    mi = mcount % 2
                    mcount += 1

                    def part2(pi_=pi_, mi=mi, s=s, L=L, qc0=qc0, b=b, hh=hh):
                        pt = post_t
                        pr = spairs.next()
                        spairs.next()
                        P.op("pe", I("matmul", banks[pr[0]][:], lhsT=onesf[:], rhs=sq2[pi_][:], start=True, stop=True),
                             reads=[sq2_t[pi_], t_const], writes=[bt[pr[0]], bt[pr[1]]])
                        P.op("act", I("activation", out=vo[:], in_=banks[pr[0]][:], func=AF.Ln, scale=1.0 / 128,
                                      bias=eps_col[:, 0:1]), reads=[bt[pr[0]], t_const], writes=[pt["vo"]])
                        P.op("act", I("activation", out=rso[:], in_=vo[:], func=AF.Exp, scale=-0.5),
                             reads=[pt["vo"]], writes=[pt["rso"]])
                        P.op("dve", I("scalar_tensor_tensor", out=a1[:], in0=ob2[pi_][:], scalar=gs_col[:, 0:1],
                                      in1=rso[:], op0=ALU.mult, op1=ALU.mult),
                             reads=[ob2_t[pi_], pt["rso"], t_const], writes=[pt["a1"]])
                        P.op("dve", I("scalar_tensor_tensor", out=a2[:], in0=gAb[s][:, qc0:qc0 + 512], scalar=1.0,
                                      in1=a1[:], op0=ALU.add, op1=ALU.mult),
                             reads=[L["g"], pt["a1"]], writes=[pt["a2"]])
                        P.op("pool", I("tensor_tensor", out=mst[mi][:], in0=a2[:], in1=ppb[s][:, qc0:qc0 + 512],
                                       op=ALU.add), reads=[pt["a2"], L["p"]], writes=[mst_t[mi]])
                        P.dma("sp", m_s[b, hh, :, qc0:qc0 + 512], mst[mi][:], mst_t[mi], reads=[mst_t[mi]], store=True)
                    pending.append(part2)
            while pending:
                pending.pop(0)()
            P.wait_stores("sp")
            P.emit()

        with ExitStack() as st:
          if upto >= 3:
            P.stack = st

            def sb(name, shape, dt):
                return st.enter_context(nc.sbuf_tensor(name, list(shape), dt))

            xt = [sb("xc%d" % i, [128, 4, D], F32) for i in range(2)]
            xt_t = [[Tok("xc%d_%d" % (i, tb)) for tb in range(4)] for i in range(2)]
            mT = [sb("mT%d" % i, [128, 8, 512], BF16) for i in range(2)]
            mT_t = [Tok("mT%d" % i) for i in range(2)]
            junk = sb("junkC", [128, D], BF16)
            junk_t = Tok("junkC")
            ss = sb("ssC", [128, 8], F32)
            vv = sb("vvC", [128, 8], F32)
            rstd = sb("rstdC", [128, 8], F32)
            st2_t = Tok("stats2")
            st3_t = Tok("stats3")
            hb = sb("hbC", [128, 4, D], BF16)
            hb_t = [Tok("hbC%d" % i) for i in range(4)]
            h2T = sb("h2T", [128, 8, 512], BF16)
            h2T_t = [Tok("h2T%d" % k) for k in range(8)]
            zT = sb("zT", [128, 32, 512], BF16)
            zT_t = [Tok("zT%d" % k) for k in range(32)]
            NR = 3
            rt = [sb("rt%d" % i, [128, 512], F32) for i in range(NR)]
            rt_t = [Tok("rt%d" % i) for i in range(NR)]
            NWS = 4
            wsl = [sb("wslC%d" % i, [128, 8, 512], BF16) for i in range(NWS)]
            wsl_t = [Tok("wslC%d" % i) for i in range(NWS)]

            mmring = Ring([0, 1, 2, 3, 4, 5])
            trring = Ring([6, 7])
            rring = Ring(list(range(NR)))
            order = list(G_OUT) + list(G_UP) + list(G_DOWN)
            WS = WStream(P, wsl, wsl_t, wscr, order * NTT)

            def load_c(t):
                b, j = divmod(t, NT)
                P.dma("sp", xt[t % 2][:], x_d[b, j * 512:(j + 1) * 512, :].rearrange("(tb p) d -> p tb d", p=128),
                      xt_t[t % 2][0], writes=xt_t[t % 2])
                P.dma("sp", mT[t % 2][:], m_s[b, :, :, j * 512:(j + 1) * 512].rearrange("h p n -> p h n"),
                      mT_t[t % 2], writes=[mT_t[t % 2]])

            load_c(0)
            stageC = [0]

            def chkC():
                stageC[0] += 1
                if stageC[0] >= dbgC:
                    raise _Stop()

            def phaseC_body():
              for t in range(NTT):
                b, j = divmod(t, NT)
                xb, xtk = xt[t % 2], xt_t[t % 2]
                mb, mtk = mT[t % 2], mT_t[t % 2]
                if t + 1 < NTT:
                    load_c(t + 1)
                for nh in range(2):
                    Wg, wt = WS.get()
                    for tb in range(4):
                        bk = mmring.next()
                        for kc in range(8):
                            P.op("pe", I("matmul",
                                banks[bk][:], lhsT=mb[:, kc, tb * 128:(tb + 1) * 128], rhs=Wg[:, kc, :],
                                start=(kc == 0), stop=(kc == 7)), reads=[wt, mtk], writes=[bt[bk]], signal=(kc == 7))
                        P.op("dve", I("tensor_tensor",
                            out=xb[:, tb, nh * 512:(nh + 1) * 512], in0=banks[bk][:],
                            in1=xb[:, tb, nh * 512:(nh + 1) * 512], op=ALU.add), reads=[bt[bk], xtk[tb]], writes=[xtk[tb]])
                    WS.prefetch()
                chkC()
                for tb in range(4):
                    P.op("act", I("activation", out=junk[:], in_=xb[:, tb, :], func=AF.Square,
                                                              accum_out=ss[:, tb:tb + 1]),
                         reads=[xtk[tb]], writes=[junk_t, st2_t])
                P.op("dve", I("tensor_scalar", out=vv[:, 0:4], in0=ss[:, 0:4], scalar1=1.0 / D, scalar2=EPS,
                                                      op0=ALU.mult, op1=ALU.add), reads=[st2_t], writes=[st2_t])
                P.op("act", I("activation", out=vv[:, 0:4], in_=vv[:, 0:4], func=AF.Ln), reads=[st2_t], writes=[st2_t])
                P.op("act", I("activation", out=rstd[:, 0:4], in_=vv[:, 0:4], func=AF.Exp, scale=-0.5), reads=[st2_t], writes=[st2_t])
                for tb in range(4):
                    P.op("act", I("activation", out=hb[:, tb, :], in_=xb[:, tb, :], func=AF.Copy,
                                                              scale=rstd[:, tb:tb + 1]),
                         reads=[xtk[tb], st2_t], writes=[hb_t[tb]])
                for kc in range(8):
                    bk = trring.next()
                    pT = banks[bk][:].bitcast(BF16)
                    for tb in range(4):
                        P.op("pe", I("transpose",
                            out=pT[:, tb * 128:(tb + 1) * 128], in_=hb[:, tb, kc * 128:(kc + 1) * 128],
                            identity=identb[:]), reads=[hb_t[tb], t_const], writes=[bt[bk]], signal=(tb == 3))
                    P.op("dve", I("tensor_copy", out=h2T[:, kc, :], in_=pT[:, 0:512]),
                         reads=[bt[bk]], writes=[h2T_t[kc]])
                chkC()
                for g in range(8):
                    Wg, wt = WS.get()
                    for c in range(4):
                        bk = mmring.next()
                        for kc in range(8):
                            P.op("pe", I("matmul",
                                banks[bk][:], lhsT=Wg[:, kc, c * 128:(c + 1) * 128], rhs=h2T[:, kc, :],
                                start=(kc == 0), stop=(kc == 7)), reads=[wt, h2T_t[kc]], writes=[bt[bk]],
                                signal=(kc == 7))
                        ri = rring.next()
                        fi = 4 * g + c
                        P.op("act", I("activation", out=rt[ri][:], in_=banks[bk][:], func=AF.Relu),
                             reads=[bt[bk]], writes=[rt_t[ri]])
                        P.op("dve", I("tensor_tensor",
                            out=zT[:, fi, :], in0=banks[bk][:], in1=rt[ri][:], op=ALU.mult),
                            reads=[bt[bk], rt_t[ri]], writes=[zT_t[fi]])
                    WS.prefetch()
                chkC()
                for nh in range(2):
                    bks = [mmring.next() for _ in range(4)]
                    for fg in range(4):
                        Wg, wt = WS.get()
                        for tb in range(4):
                            bk = bks[tb]
                            for fc in range(8):
                                fi = 8 * fg + fc
                                P.op("pe", I("matmul",
                                    banks[bk][:], lhsT=zT[:, fi, tb * 128:(tb + 1) * 128], rhs=Wg[:, fc, :],
                                    start=(fg == 0 and fc == 0), stop=(fg == 3 and fc == 7)),
                                    reads=[wt, zT_t[fi]], writes=[bt[bk]], signal=(fc == 7))
                            if fg == 3:
                                P.op("dve", I("tensor_tensor",
                                    out=xb[:, tb, nh * 512:(nh + 1) * 512], in0=banks[bk][:],
                                    in1=xb[:, tb, nh * 512:(nh + 1) * 512], op=ALU.add),
                                    reads=[bt[bk], xtk[tb]], writes=[xtk[tb]])
                        WS.prefetch()
                chkC()
                for tb in range(4):
                    P.op("act", I("activation", out=junk[:], in_=xb[:, tb, :], func=AF.Square,
                                                              accum_out=ss[:, 4 + tb:5 + tb]),
                         reads=[xtk[tb]], writes=[junk_t, st3_t])
                P.op("dve", I("tensor_scalar", out=vv[:, 4:8], in0=ss[:, 4:8], scalar1=1.0 / D, scalar2=EPS,
                                                      op0=ALU.mult, op1=ALU.add), reads=[st3_t], writes=[st3_t])
                P.op("act", I("activation", out=vv[:, 4:8], in_=vv[:, 4:8], func=AF.Ln), reads=[st3_t], writes=[st3_t])
                P.op("act", I("activation", out=rstd[:, 4:8], in_=vv[:, 4:8], func=AF.Exp, scale=-0.5), reads=[st3_t], writes=[st3_t])
                for tb in range(4):
                    P.op("dve", I("scalar_tensor_tensor",
                        out=xb[:, tb, :], in0=xb[:, tb, :], scalar=rstd[:, 4 + tb:5 + tb], in1=gfin_bc[:],
                        op0=ALU.mult, op1=ALU.mult), reads=[xtk[tb], st3_t, t_const], writes=[xtk[tb]])
                P.dma("sp", y_d[b, j * 512:(j + 1) * 512, :].rearrange("(tb p) d -> p tb d", p=128), xb[:],
                      xtk[0], reads=xtk, store=True)
            try:
                phaseC_body()
            except _Stop:
                pass
            P.wait_stores("sp")
            P.emit()
    return nc


def _consts():
    ident = np.eye(128, dtype=np.float32)
    kk = np.arange(128)[:, None]
    qq = np.arange(128)[None, :]
    mask = (qq >= kk).astype(np.float32)
    invf = np.power(np.float32(500000.0), -(np.arange(0, 16, 2, dtype=np.float32) / np.float32(16))).astype(np.float32)
    wct = np.zeros((4, 16), np.float32)
    for c in range(4):
        w = 2 ** (c + 1)
        for t_ in range(16):
            wct[c, t_] = w / min(t_ + 1, w)
    return ident, mask, invf, wct


def prep_core_inputs(inputs, b0, NB):
    S = inputs["x"].shape[1]
    f = lambda a: np.ascontiguousarray(np.asarray(a, dtype=np.float32))
    ident, mask, invf, wct = _consts()
    pos = np.asarray(inputs["positions"])[b0:b0 + NB].astype(np.int32)
    pos_t = np.ascontiguousarray(pos.reshape(NB, S // 128, 128).transpose(0, 2, 1))
    return {
        "x": f(inputs["x"][b0:b0 + NB]),
        "pos_t": pos_t,
        "w_in": f(inputs["w_in"][0]),
        "w_out": f(inputs["w_out"][0]),
        "w_up": f(inputs["w_up"][0]),
        "w_down": f(inputs["w_down"][0]),
        "w_pool_r": f(np.asarray(inputs["w_pool"][0]).transpose(1, 0, 2)),
        "gcol_attn": f(np.asarray(inputs["norm_attn_g"][0]).reshape(8, 128).T),
        "gcol_mlp": f(np.asarray(inputs["norm_mlp_g"][0]).reshape(8, 128).T),
        "gfin": f(inputs["final_norm_g"]),
        "subln_col": f(np.asarray(inputs["subln_g"][0]).reshape(128, 1)),
        "pool_scale": f(inputs["pool_scale"][0]),
        "lam4": f(np.stack([np.asarray(inputs[k][0]) for k in ("lam_q1", "lam_k1", "lam_q2", "lam_k2")])),
        "c_ident": ident, "c_mask": mask, "c_invf": invf, "c_wct": wct,
    }


_NC_CACHE = {}


def kernel(**inputs):
    x = np.asarray(inputs["x"])
    B, S, _ = x.shape
    n = N_CORES
    NB = B // n
    key = (NB, S)
    if key not in _NC_CACHE:
        _NC_CACHE[key] = build_nc(NB, S)
    nc = _NC_CACHE[key]
    in_maps = [prep_core_inputs(inputs, i * NB, NB) for i in range(n)]
    res = run_bass_kernel_spmd(nc, in_maps, core_ids=list(range(n)))
    return np.concatenate([np.asarray(r["y"]) for r in res.results], axis=0).astype(np.float32)
```

```python
import math
from contextlib import ExitStack

import numpy as np
import concourse.bass as bass
import concourse.mybir as mybir
from concourse.bass_utils import run_bass_kernel_spmd

F32 = mybir.dt.float32
BF16 = mybir.dt.bfloat16
I32 = mybir.dt.int32
AF = mybir.ActivationFunctionType
ALU = mybir.AluOpType

D = 1024
NH = 8
DFF = 4096
INW = 5632
EPS = 1e-6
LAM_INIT = 0.8 - 0.6 * math.exp(-0.3 * 0)
N_CORES = 8
TWO_PI = 2.0 * math.pi
CW1 = 6.28125
CW2 = TWO_PI - CW1


class Tok:
    __slots__ = ("name", "w", "r", "dsem", "dkey", "dval")

    def __init__(self, name):
        self.name = name
        self.w = None
        self.r = {}
        self.dsem = None
        self.dkey = None
        self.dval = 0


class Eng:
    def __init__(self, name, sem):
        self.name = name
        self.sem = sem
        self.prog = []
        self.count = 0
        self.seen = {}
        self.nwait = 0
        self.nops = 0


class Prog:
    def __init__(self, nc, stack):
        self.nc = nc
        self.gstack = stack
        self.stack = stack
        self.engs = {}
        self.semmap = {}
        self.nkey = 0
        for name in ("pe", "act", "dve", "pool", "sp"):
            s = stack.enter_context(nc.semaphore("p_" + name))
            self.engs[name] = Eng(name, s)
        self.stores = []

    def new_dsem(self, name):
        s = self.gstack.enter_context(self.nc.semaphore(name))
        self.nkey += 1
        self.semmap[self.nkey] = s
        return s, self.nkey

    def _deps(self, reads, writes, extra):
        deps = {}

        def add(k, v):
            if deps.get(k, 0) < v:
                deps[k] = v
        for t in reads:
            if t.w is not None:
                add(*t.w)
        for t in writes:
            if t.w is not None:
                add(*t.w)
            for k, v in t.r.items():
                add(k, v)
        for ev in extra:
            if ev is not None:
                add(*ev)
        return deps

    def _wait(self, e, deps):
        for k, v in deps.items():
            if k == "pe" and e.name == "pe":
                continue
            if e.seen.get(k, 0) >= v:
                continue
            e.seen[k] = v
            sem = self.engs[k].sem if isinstance(k, str) else self.semmap[k]
            e.prog.append(("wait", sem, v))
            e.nwait += 1

    @staticmethod
    def _record(ev, reads, writes):
        k, v = ev
        for t in reads:
            if t.r.get(k, 0) < v:
                t.r[k] = v
        for t in writes:
            t.w = ev
            t.r = {}

    def op(self, eng, fn, reads=(), writes=(), signal=True, extra=()):
        e = self.engs[eng]
        self._wait(e, self._deps(reads, writes, extra))
        e.prog.append(("inst", fn, signal))
        e.nops += 1
        if signal:
            e.count += 1
            ev = (e.name, e.count)
        else:
            ev = (e.name, e.count + 1)
        self._record(ev, reads, writes)
        return ev

    def dma(self, q, out, in_, tok, reads=(), writes=(), extra=(), store=False):
        e = self.engs[q]
        self._wait(e, self._deps(reads, writes, extra))
        if tok.dsem is None:
            tok.dsem, tok.dkey = self.new_dsem("d_" + tok.name)
        e.prog.append(("dma", out, in_, tok.dsem))
        e.nops += 1
        tok.dval += 16
        ev = (tok.dkey, tok.dval)
        self._record(ev, reads, writes)
        if store:
            self.stores.append(ev)
        return ev

    def wait_toks(self, q, toks):
        e = self.engs[q]
        self._wait(e, self._deps(toks, (), ()))

    def wait_stores(self, q="sp"):
        e = self.engs[q]
        deps = {}
        for k, v in self.stores:
            if deps.get(k, 0) < v:
                deps[k] = v
        self.stores = []
        self._wait(e, deps)

    def emit(self):
        nc = self.nc
        with nc.Block() as block:
            def mk(e):
                def body(h):
                    for it in e.prog:
                        if it[0] == "wait":
                            h.wait_ge(it[1], it[2])
                        elif it[0] == "inst":
                            nm, a_, kw_ = it[1]
                            inst = getattr(h, nm)(*a_, **kw_)
                            if it[2]:
                                inst.then_inc(e.sem, 1)
                        else:
                            h.dma_start(out=it[1], in_=it[2]).then_inc(it[3], 16)
                    e.prog = []
                return body
            block.tensor(mk(self.engs["pe"]))
            block.scalar(mk(self.engs["act"]))
            block.vector(mk(self.engs["dve"]))
            block.gpsimd(mk(self.engs["pool"]))
            block.sync(mk(self.engs["sp"]))


def I(name, *args, **kw):
    return (name, args, kw)


class Ring:
    def __init__(self, items):
        self.items = items
        self.i = 0

    def next(self):
        it = self.items[self.i % len(self.items)]
        self.i += 1
        return it


class WStream:
    def __init__(self, P, slots, toks, wscr, seq, extra=()):
        self.P = P
        self.slots = slots
        self.toks = toks
        self.wscr = wscr
        self.seq = seq
        self.issued = 0
        self.used = 0
        self.extra = extra

    def _issue(self):
        i = self.issued
        s = i % len(self.slots)
        self.P.dma("sp", self.slots[s][:], self.wscr[self.seq[i]], self.toks[s],
                   writes=[self.toks[s]], extra=self.extra)
        self.issued += 1

    def get(self):
        i = self.used
        while self.issued <= i:
            self._issue()
        self.used += 1
        return self.slots[i % len(self.slots)], self.toks[i % len(self.slots)]

    def prefetch(self):
        R = len(self.slots)
        while self.issued < min(len(self.seq), self.used + R):
            self._issue()


class _Stop(Exception):
    pass


def build_nc(NB, S, upto=3, debug=False, dbgA=10 ** 9, dbgC=10 ** 9):
    NT = S // 512
    NBLK = S // 128
    NTT = NB * NT
    nc = bass.Bass("TRN2", target_bir_lowering=False)

    def din(name, shape, dt=F32):
        return nc.dram_tensor(name, list(shape), dt, kind="ExternalInput").ap()

    x_d = din("x", [NB, S, D])
    post_d = din("pos_t", [NB, 128, NBLK], I32)
    w_in_d = din("w_in", [D, INW])
    w_out_d = din("w_out", [D, D])
    w_up_d = din("w_up", [D, DFF])
    w_down_d = din("w_down", [DFF, D])
    wpool_d = din("w_pool_r", [128, 4, 256])
    gattn_d = din("gcol_attn", [128, 8])
    gmlp_d = din("gcol_mlp", [128, 8])
    gfin_d = din("gfin", [D])
    subln_d = din("subln_col", [128, 1])
    pscale_d = din("pool_scale", [D])
    lam_d = din("lam4", [4, 64])
    ident_d = din("c_ident", [128, 128])
    mask_d = din("c_mask", [128, 128])
    invf_d = din("c_invf", [8])
    wct_d = din("c_wct", [4, 16])
    y_d = nc.dram_tensor("y", [NB, S, D], F32, kind="ExternalOutput").ap()

    def dscr(name, shape, dt=BF16):
        return nc.dram_tensor(name, list(shape), dt, kind=("ExternalOutput" if debug else "Internal")).ap()

    NWG = 29
    wscr = dscr("wscr", [NWG, 128, 8, 512])
    qT_s = dscr("qT_s", [NB, NH, 128, S])
    kT_s = dscr("kT_s", [NB, NH, 128, S])
    v_s = dscr("v_s", [NB, S, D])
    gA_s = dscr("gA_s", [NB, NH, 128, S])
    pp_s = dscr("pp_s", [NB, NH, 128, S])
    m_s = dscr("m_s", [NB, NH, 128, S])

    G_Q, G_K, G_V, G_POOL, G_GA, G_GP = (0, 1), (2, 3), (4, 5), (6,), (7, 8), (9, 10)
    G_OUT = (11, 12)
    G_UP = tuple(range(13, 21))
    G_DOWN = tuple(range(21, 29))

    with ExitStack() as gst:
        P = Prog(nc, gst)

        def gsb(name, shape, dt):
            return gst.enter_context(nc.sbuf_tensor(name, list(shape), dt))

        psum_all = gst.enter_context(nc.psum_tensor("psum_all", [128, 8, 512], F32))
        banks = [psum_all[:, i, :] for i in range(8)]
        bt = [Tok("bank%d" % i) for i in range(8)]

        identb = gsb("identb", [128, 128], BF16)
        onesb = gsb("onesb", [128, 128], BF16)
        onesf = gsb("onesf", [128, 128], F32)
        maskb = gsb("maskb", [128, 128], BF16)
        gfin_bc = gsb("gfin_bc", [128, D], F32)
        wpoolb = gsb("wpoolb", [128, 4, 256], BF16)
        gattn = gsb("gattn", [128, 8], F32)
        gmlp = gsb("gmlp", [128, 8], F32)
        gs_col = gsb("gs_col", [128, 1], F32)
        neglam = gsb("neglam", [128, 1], F32)
        cosb = gsb("cosb", [128, NB * NBLK, 8], F32)
        sinb = gsb("sinb", [128, NB * NBLK, 8], F32)
        wct = gsb("wct", [128, 4, 16], F32)
        mhalf = gsb("mhalf", [128, 512], F32)
        eps_col = gsb("eps_col", [128, 1], F32)
        t_const = Tok("consts")

        with ExitStack() as st:
            P.stack = st

            def sb(name, shape, dt):
                return st.enter_context(nc.sbuf_tensor(name, list(shape), dt))

            idf = sb("idf", [128, 128], F32)
            mkf = sb("mkf", [128, 128], F32)
            wpf = sb("wpf", [128, 4, 256], F32)
            psb = sb("psb", [128, D], F32)
            lamt = sb("lamt", [128, 4, 64], F32)
            lprod = sb("lprod", [128, 2, 64], F32)
            lsum = sb("lsum", [128, 2], F32)
            lexp = sb("lexp", [128, 2], F32)
            subl = sb("subl", [128, 1], F32)
            posi = sb("posi", [128, NB * NBLK], I32)
            posf = sb("posf", [128, NB * NBLK], F32)
            invf = sb("invf", [128, 8], F32)
            NA = NB * NBLK * 8
            ang = sb("ang", [128, NA], F32)
            tq = sb("tq", [128, NA], F32)
            ki = sb("ki", [128, NA], I32)
            kf = sb("kf", [128, NA], F32)
            r0 = sb("r0", [128, NA], F32)
            r1 = sb("r1", [128, NA], F32)
            mk1 = sb("mk1", [128, NA], F32)
            rc = sb("rc", [128, NA], F32)
            tl = {n: Tok(n) for n in ("idf", "mkf", "wpf", "psb", "lamt", "subl", "posi", "invf", "gat", "gml",
                                      "gfin", "wct")}
            P.dma("sp", idf[:], ident_d, tl["idf"], writes=[tl["idf"]])
            P.dma("sp", mkf[:], mask_d, tl["mkf"], writes=[tl["mkf"]])
            P.dma("sp", wpf[:], wpool_d, tl["wpf"], writes=[tl["wpf"]])
            P.dma("sp", psb[:], pscale_d.partition_broadcast(128), tl["psb"], writes=[tl["psb"]])
            P.dma("sp", lamt[:], lam_d.partition_broadcast(128), tl["lamt"], writes=[tl["lamt"]])
            P.dma("sp", subl[:], subln_d, tl["subl"], writes=[tl["subl"]])
            P.dma("sp", posi[:].rearrange("p (b k) -> p b k", b=NB), post_d.rearrange("b p k -> p b k"),
                  tl["posi"], writes=[tl["posi"]])
            P.dma("sp", invf[:], invf_d.partition_broadcast(128), tl["invf"], writes=[tl["invf"]])
            P.dma("sp", gattn[:], gattn_d, tl["gat"], writes=[tl["gat"]])
            P.dma("sp", gmlp[:], gmlp_d, tl["gml"], writes=[tl["gml"]])
            P.dma("sp", gfin_bc[:], gfin_d.partition_broadcast(128), tl["gfin"], writes=[tl["gfin"]])
            P.dma("sp", wct[:], wct_d.partition_broadcast(128), tl["wct"], writes=[tl["wct"]])

            tc_ = t_const
            P.op("dve", I("tensor_copy", out=identb[:], in_=idf[:]), reads=[tl["idf"]], writes=[tc_])
            P.op("dve", I("tensor_copy", out=maskb[:], in_=mkf[:]), reads=[tl["mkf"]], writes=[tc_])
            P.op("pool", I("memset", onesb[:], 1.0), writes=[tc_])
            P.op("pool", I("memset", onesf[:], 1.0), writes=[tc_])
            P.op("pool", I("memset", mhalf[:], -0.5), writes=[tc_])
            P.op("pool", I("memset", eps_col[:], EPS), writes=[tc_])
            for g in range(4):
                wg_ = float(2 ** (g + 1))
                P.op("dve", I("scalar_tensor_tensor",
                    out=wpoolb[:, g, :], in0=wpf[:, g, :], scalar=0.5 / wg_, in1=psb[:, g * 256:(g + 1) * 256],
                    op0=ALU.mult, op1=ALU.mult), reads=[tl["wpf"], tl["psb"]], writes=[tc_])
            P.op("dve", I("tensor_scalar", out=gs_col[:], in0=subl[:], scalar1=0.5 * (1.0 - LAM_INIT),
                                                  scalar2=None, op0=ALU.mult), reads=[tl["subl"]], writes=[tc_])
            tlp = Tok("lprod")
            P.op("dve", I("tensor_tensor", out=lprod[:, 0, :], in0=lamt[:, 0, :], in1=lamt[:, 1, :],
                                                  op=ALU.mult), reads=[tl["lamt"]], writes=[tlp])
            P.op("dve", I("tensor_tensor", out=lprod[:, 1, :], in0=lamt[:, 2, :], in1=lamt[:, 3, :],
                                                  op=ALU.mult), reads=[tl["lamt"], tlp], writes=[tlp])
            P.op("dve", I("tensor_reduce", out=lsum[:], in_=lprod[:], op=ALU.add,
                                                  axis=mybir.AxisListType.X), reads=[tlp], writes=[tlp])
            P.op("act", I("activation", out=lexp[:], in_=lsum[:], func=AF.Exp), reads=[tlp], writes=[tlp])
            P.op("dve", I("tensor_tensor", out=lsum[:, 0:1], in0=lexp[:, 1:2], in1=lexp[:, 0:1],
                                                  op=ALU.subtract), reads=[tlp], writes=[tlp])
            P.op("dve", I("tensor_scalar", out=neglam[:], in0=lsum[:, 0:1], scalar1=-LAM_INIT, scalar2=None,
                                                  op0=ALU.add), reads=[tlp], writes=[tc_])
            ta = Tok("ang")
            P.op("dve", I("tensor_copy", out=posf[:], in_=posi[:]), reads=[tl["posi"]], writes=[ta])
            P.op("dve", I("tensor_tensor",
                out=ang[:].rearrange("p (k f) -> p k f", f=8),
                in0=posf[:].unsqueeze(2).to_broadcast([128, NB * NBLK, 8]),
                in1=invf[:].unsqueeze(1).to_broadcast([128, NB * NBLK, 8]), op=ALU.mult),
                reads=[tl["invf"], ta], writes=[ta])

            def reduce_to(dst, src_shift):
                P.op("dve", I("tensor_scalar", out=tq[:], in0=ang[:], scalar1=src_shift, scalar2=1.0 / TWO_PI,
                                                      op0=ALU.add, op1=ALU.mult), reads=[ta], writes=[ta])
                P.op("dve", I("tensor_copy", out=ki[:], in_=tq[:]), reads=[ta], writes=[ta])
                P.op("dve", I("tensor_copy", out=kf[:], in_=ki[:]), reads=[ta], writes=[ta])
                P.op("dve", I("tensor_scalar", out=rc[:], in0=ang[:], scalar1=src_shift, scalar2=None,
                                                      op0=ALU.add), reads=[ta], writes=[ta])
                P.op("dve", I("scalar_tensor_tensor", out=r0[:], in0=kf[:], scalar=-CW1, in1=rc[:],
                                                             op0=ALU.mult, op1=ALU.add), reads=[ta], writes=[ta])
                P.op("dve", I("scalar_tensor_tensor", out=r1[:], in0=kf[:], scalar=-CW2, in1=r0[:],
                                                             op0=ALU.mult, op1=ALU.add), reads=[ta], writes=[ta])
                P.op("dve", I("tensor_scalar", out=mk1[:], in0=r1[:], scalar1=math.pi, scalar2=-TWO_PI,
                                                      op0=ALU.is_gt, op1=ALU.mult), reads=[ta], writes=[ta])
                P.op("dve", I("tensor_tensor", out=r0[:], in0=r1[:], in1=mk1[:], op=ALU.add),
                     reads=[ta], writes=[ta])
                P.op("dve", I("tensor_scalar", out=mk1[:], in0=r0[:], scalar1=-math.pi, scalar2=TWO_PI,
                                                      op0=ALU.is_lt, op1=ALU.mult), reads=[ta], writes=[ta])
                P.op("dve", I("tensor_tensor", out=r1[:], in0=r0[:], in1=mk1[:], op=ALU.add),
                     reads=[ta], writes=[ta])
                P.op("dve", I("tensor_scalar", out=r1[:], in0=r1[:], scalar1=math.pi, scalar2=-math.pi,
                                                      op0=ALU.min, op1=ALU.max), reads=[ta], writes=[ta])
                P.op("act", I("activation", out=dst[:].rearrange("p k f -> p (k f)"), in_=r1[:], func=AF.Sin),
                     reads=[ta], writes=[ta, tc_])

            reduce_to(sinb, 0.0)
            reduce_to(cosb, 0.5 * math.pi)

            NST = 4
            wst = [sb("wst%d" % i, [128, 8, 512], F32) for i in range(NST)]
            wbo = [sb("wbo%d" % i, [128, 8, 512], BF16) for i in range(NST)]
            wst_t = [Tok("wst%d" % i) for i in range(NST)]
            wbo_t = [Tok("wbo%d" % i) for i in range(NST)]
            w_in_v = w_in_d.rearrange("(kc p) n -> p kc n", p=128)
            w_out_v = w_out_d.rearrange("(kc p) n -> p kc n", p=128)
            w_up_v = w_up_d.rearrange("(kc p) n -> p kc n", p=128)
            w_dn_v = w_down_d.rearrange("(a p) n -> p a n", p=128)
            jobs = []
            for g in range(11):
                jobs.append((g, w_in_v[:, :, g * 512:(g + 1) * 512], gattn, "gat"))
            for g in range(2):
                jobs.append((11 + g, w_out_v[:, :, g * 512:(g + 1) * 512], None, None))
            for g in range(8):
                jobs.append((13 + g, w_up_v[:, :, g * 512:(g + 1) * 512], gmlp, "gml"))
            for nh in range(2):
                for fg in range(4):
                    jobs.append((21 + nh * 4 + fg, w_dn_v[:, fg * 8:(fg + 1) * 8, nh * 512:(nh + 1) * 512], None, None))
            engs3 = ["dve", "act", "dve"]
            for i, (gid, src, gcol, gk) in enumerate(jobs):
                s = i % NST
                P.dma("sp", wst[s][:], src, wst_t[s], writes=[wst_t[s]])
                eng = engs3[i % 3]
                if gcol is None:
                    if eng == "act":
                        P.op("act", I("activation", out=wbo[s][:], in_=wst[s][:], func=AF.Copy),
                             reads=[wst_t[s]], writes=[wbo_t[s]])
                    else:
                        P.op(eng, I("tensor_copy", out=wbo[s][:], in_=wst[s][:]),
                             reads=[wst_t[s]], writes=[wbo_t[s]])
                else:
                    for kc in range(8):
                        if eng == "act":
                            P.op("act", I("activation",
                                out=wbo[s][:, kc, :], in_=wst[s][:, kc, :], func=AF.Copy, scale=gcol[:, kc:kc + 1]),
                                reads=[wst_t[s], tl[gk]], writes=[wbo_t[s]])
                        else:
                            P.op(eng, I("tensor_scalar",
                                out=wbo[s][:, kc, :], in0=wst[s][:, kc, :], scalar1=gcol[:, kc:kc + 1], scalar2=None,
                                op0=ALU.mult), reads=[wst_t[s], tl[gk]], writes=[wbo_t[s]])
                P.dma("sp", wscr[gid], wbo[s][:], wbo_t[s], reads=[wbo_t[s]], store=True)
            P.wait_stores("sp")
            P.wait_toks("sp", list(tl.values()))
            P.emit()

        with ExitStack() as st:
          if upto >= 1:
            P.stack = st

            def sb(name, shape, dt):
                return st.enter_context(nc.sbuf_tensor(name, list(shape), dt))

            xt = [sb("xt%d" % i, [128, 4, D], F32) for i in range(2)]
            xt_t = [Tok("xt%d" % i) for i in range(2)]
            junk = sb("junkA", [128, D], BF16)
            junk_t = Tok("junkA")
            ss = sb("ssA", [128, 4], F32)
            vv = sb("vvA", [128, 4], F32)
            rstd = sb("rstdA", [128, 4], F32)
            st_t = Tok("statsA")
            hb = sb("hbA", [128, 4, D], BF16)
            hb_t = [Tok("hbA%d" % i) for i in range(4)]
            hT = [sb("hT%d" % i, [128, 8, 512], BF16) for i in range(2)]
            hT_t = [[Tok("hT%d_%d" % (i, k)) for k in range(8)] for i in range(2)]
            NWS = 3
            wsl = [sb("wslA%d" % i, [128, 8, 512], BF16) for i in range(NWS)]
            wsl_t = [Tok("wslA%d" % i) for i in range(NWS)]
            qkt = [sb("qkt%d" % i, [128, 4, 512], BF16) for i in range(3)]
            qkt_t = [[Tok("qkt%d_%d" % (i, tb)) for tb in range(4)] for i in range(3)]
            NQS = 4
            qst = [sb("qst%d" % i, [128, 512], BF16) for i in range(NQS)]
            qst_t = [Tok("qst%d" % i) for i in range(NQS)]
            vtk = [sb("vtk%d" % i, [128, 4, 512], BF16) for i in range(2)]
            vtk_t = [Tok("vtk%d" % i) for i in range(2)]
            tg = sb("tgA", [128, 16, 512], BF16)
            tg_t = [Tok("tg%d" % i) for i in range(16)]
            up = [sb("up%d" % i, [128, 4, 528], F32) for i in range(2)]
            up_t = [[Tok("up%d_%d" % (i, c)) for c in range(4)] for i in range(2)]
            T1 = sb("poolT1", [128, 528], F32)
            T2 = sb("poolT2", [128, 528], F32)
            T3 = sb("poolT3", [128, 528], F32)
            pl_t = Tok("poolT")
            dT = sb("dT", [128, 4, 512], BF16)
            dT_t = [Tok("dT%d" % c) for c in range(4)]
            NPS = 3
            pst = [sb("pst%d" % i, [128, 512], BF16) for i in range(NPS)]
            pst_t = [Tok("pst%d" % i) for i in range(NPS)]
            ra = sb("ropeA", [128, 4, 64], F32)
            ra_t4 = [Tok("ropeA%d" % i) for i in range(4)]
            fixt = sb("fixt", [128, 16], F32)
            NQF = 3
            qf = [sb("qf%d" % i, [128, 512], F32) for i in range(NQF)]
            qf_t = [Tok("qf%d" % i) for i in range(NQF)]
            qfr = Ring(list(range(NQF)))

            mmring = Ring([0, 1, 2, 3, 4, 5])
            trring = Ring([6, 7])
            qsr = Ring(list(range(NQS)))
            psr = Ring(list(range(NPS)))

            order = [G_POOL[0], G_GP[0], G_GP[1], G_GA[0], G_GA[1], G_Q[0], G_Q[1], G_K[0], G_K[1], G_V[0], G_V[1]]
            WS = WStream(P, wsl, wsl_t, wscr, order * NTT)

            def load_x(t):
                b, j = divmod(t, NT)
                P.dma("sp", xt[t % 2][:], x_d[b, j * 512:(j + 1) * 512, :].rearrange("(tb p) d -> p tb d", p=128),
                      xt_t[t % 2], writes=[xt_t[t % 2]])

            def prep(t):
                xb, xtok = xt[t % 2], xt_t[t % 2]
                for tb in range(4):
                    P.op("act", I("activation", out=junk[:], in_=xb[:, tb, :], func=AF.Square,
                                                              accum_out=ss[:, tb:tb + 1]),
                         reads=[xtok], writes=[junk_t, st_t])
                P.op("dve", I("tensor_scalar", out=vv[:], in0=ss[:], scalar1=1.0 / D, scalar2=EPS,
                                                      op0=ALU.mult, op1=ALU.add), reads=[st_t], writes=[st_t])
                P.op("act", I("activation", out=vv[:], in_=vv[:], func=AF.Ln), reads=[st_t], writes=[st_t])
                P.op("act", I("activation", out=rstd[:], in_=vv[:], func=AF.Exp, scale=-0.5), reads=[st_t], writes=[st_t])
                for tb in range(4):
                    P.op("act", I("activation", out=hb[:, tb, :], in_=xb[:, tb, :], func=AF.Copy,
                                                              scale=rstd[:, tb:tb + 1]),
                         reads=[xtok, st_t], writes=[hb_t[tb]])
                for kc in range(8):
                    bk = trring.next()
                    pT = banks[bk][:].bitcast(BF16)
                    for tb in range(4):
                        P.op("pe", I("transpose",
                            out=pT[:, tb * 128:(tb + 1) * 128], in_=hb[:, tb, kc * 128:(kc + 1) * 128],
                            identity=identb[:]), reads=[hb_t[tb], t_const], writes=[bt[bk]], signal=(tb == 3))
                    P.op("dve", I("tensor_copy", out=hT[t % 2][:, kc, :], in_=pT[:, 0:512]),
                         reads=[bt[bk]], writes=[hT_t[t % 2][kc]])

            def fm_group(t, Wg, wt, evac):
                for c in range(4):
                    bk = mmring.next()
                    for kc in range(8):
                        P.op("pe", I("matmul",
                            banks[bk][:], lhsT=Wg[:, kc, c * 128:(c + 1) * 128], rhs=hT[t % 2][:, kc, :],
                            start=(kc == 0), stop=(kc == 7)),
                            reads=[wt, hT_t[t % 2][kc]], writes=[bt[bk]], signal=(kc == 7))
                    evac(c, bk)

            def tm_group(t, Wg, wt, evac):
                for tb in range(4):
                    bk = mmring.next()
                    for kc in range(8):
                        P.op("pe", I("matmul",
                            banks[bk][:], lhsT=hT[t % 2][:, kc, tb * 128:(tb + 1) * 128], rhs=Wg[:, kc, :],
                            start=(kc == 0), stop=(kc == 7)),
                            reads=[wt, hT_t[t % 2][kc]], writes=[bt[bk]], signal=(kc == 7))
                    evac(tb, bk)

            deferred = []

            def run_deferred(now):
                keep = []
                for when, fn in deferred:
                    if when <= now:
                        fn()
                    else:
                        keep.append((when, fn))
                deferred[:] = keep

            step = [0]
            stage = [0]

            def chk():
                stage[0] += 1
                if stage[0] >= dbgA:
                    raise _Stop()

            def phaseA_body():
              for t in range(NTT):
                b, j = divmod(t, NT)
                if t == 0:
                    load_x(0)
                    prep(0)
                    chk()
                if t + 1 < NTT:
                    load_x(t + 1)
                ub = up[t % 2]
                ubt = up_t[t % 2]
                pub = up[(t + 1) % 2]
                pubt = up_t[(t + 1) % 2]

                Wg, wt = WS.get()

                def ev_pool(c, bk):
                    P.op("dve", I("tensor_copy", out=ub[:, c, 16:528], in_=banks[bk][:]),
                         reads=[bt[bk]], writes=[ubt[c]])
                    if j == 0:
                        P.op("pool", I("memset", ub[:, c, 0:16], 0.0), writes=[ubt[c]])
                    else:
                        P.op("pool", I("tensor_copy", out=ub[:, c, 0:16], in_=pub[:, c, 512:528]),
                             reads=[pubt[c]], writes=[ubt[c]])
                    U = ub[:, c, :]
                    w = 2 ** (c + 1)
                    dst = dT[:, c, :]
                    if c == 0:
                        P.op("pool", I("tensor_tensor", out=dst, in0=U[:, 15:527], in1=U[:, 16:528],
                                                               op=ALU.subtract), reads=[ubt[c]], writes=[dT_t[c]])
                        if j == 0:
                            P.op("pool", I("memset", dst[:, 0:1], 0.0), writes=[dT_t[c]])
                        return
                    P.op("pool", I("tensor_tensor", out=T1[:, 1:528], in0=U[:, 1:528], in1=U[:, 0:527],
                                                           op=ALU.add), reads=[ubt[c]], writes=[pl_t])
                    P.op("pool", I("tensor_tensor", out=T2[:, 3:528], in0=T1[:, 3:528], in1=T1[:, 1:526],
                                                           op=ALU.add), reads=[pl_t], writes=[pl_t])
                    cur = T2
                    if c >= 2:
                        P.op("pool", I("tensor_tensor", out=T1[:, 7:528], in0=T2[:, 7:528], in1=T2[:, 3:524],
                                                               op=ALU.add), reads=[pl_t], writes=[pl_t])
                        cur = T1
                    if c >= 3:
                        P.op("pool", I("tensor_tensor", out=T2[:, 15:528], in0=T1[:, 15:528],
                                                               in1=T1[:, 7:520], op=ALU.add),
                             reads=[pl_t], writes=[pl_t])
                        cur = T2
                    P.op("pool", I("tensor_scalar", out=T3[:, 16:528], in0=U[:, 16:528], scalar1=-float(w),
                                                           scalar2=None, op0=ALU.mult),
                         reads=[ubt[c], pl_t], writes=[pl_t])
                    P.op("pool", I("tensor_tensor", out=dst, in0=cur[:, 16:528], in1=T3[:, 16:528],
                                                           op=ALU.add), reads=[pl_t], writes=[dT_t[c]])
                    if j == 0:
                        P.op("pool", I("tensor_tensor", out=fixt[:, 0:w - 1], in0=cur[:, 16:16 + w - 1],
                                                               in1=wct[:, c, 0:w - 1], op=ALU.mult),
                             reads=[pl_t, t_const], writes=[pl_t])
                        P.op("pool", I("tensor_tensor", out=dst[:, 0:w - 1], in0=fixt[:, 0:w - 1],
                                                               in1=T3[:, 16:16 + w - 1], op=ALU.add),
                             reads=[pl_t], writes=[dT_t[c], pl_t])

                fm_group(t, Wg, wt, ev_pool)
                chk()
                WS.prefetch()

                for gi, base in ((0, 8), (1, 12), (2, 0), (3, 4)):
                    Wg, wt = WS.get()

                    def ev_gate(c, bk, base=base):
                        idx = base + c
                        P.op("act", I("activation", out=tg[:, idx, :], in_=banks[bk][:], func=AF.Tanh,
                                                           scale=0.5), reads=[bt[bk]], writes=[tg_t[idx]])
                        if idx < 8:
                            P.dma("sp", gA_s[b, idx, :, j * 512:(j + 1) * 512], tg[:, idx, :], tg_t[idx],
                                  reads=[tg_t[idx]], store=True)
                    fm_group(t, Wg, wt, ev_gate)
                    chk()
                    WS.prefetch()
                    if gi == 1:
                        def do_pool_out(b=b, j=j):
                            for c in range(4):
                                for e in range(2):
                                    bk = mmring.next()
                                    P.op("pe", I("matmul",
                                        banks[bk][:], lhsT=wpoolb[:, c, e * 128:(e + 1) * 128], rhs=dT[:, c, :],
                                        start=True, stop=True), reads=[dT_t[c], t_const], writes=[bt[bk]])
                                    ps_i = psr.next()
                                    idx = 8 + 2 * c + e
                                    P.op("dve", I("scalar_tensor_tensor",
                                        out=pst[ps_i][:], in0=tg[:, idx, :], scalar=1.0, in1=banks[bk][:],
                                        op0=ALU.add, op1=ALU.mult), reads=[bt[bk], tg_t[idx]], writes=[pst_t[ps_i]])
                                    P.dma("sp", pp_s[b, 2 * c + e, :, j * 512:(j + 1) * 512], pst[ps_i][:], pst_t[ps_i],
                                          reads=[pst_t[ps_i]], store=True)


                        deferred.append((step[0] + 4, do_pool_out))

                for kind, scr in (("q", qT_s), ("k", kT_s)):
                    for half in range(2):
                        Wg, wt = WS.get()
                        qb = step[0] % 3
                        step[0] += 1
                        qk_ = qkt[qb]
                        qk_tt = qkt_t[qb]

                        def ev_qk(tb, bk, qk_=qk_, qk_tt=qk_tt):
                            qi_ = qfr.next()
                            P.op("act", I("activation", out=qf[qi_][:], in_=banks[bk][:], func=AF.Copy),
                                 reads=[bt[bk]], writes=[qf_t[qi_]])
                            src = qf[qi_][:].rearrange("p (a d) -> p a d", a=8)
                            dst = qk_[:, tb, :].rearrange("p (a d) -> p a d", a=8)
                            blk = b * NBLK + 4 * j + tb
                            cs = cosb[:, blk, :].unsqueeze(1).to_broadcast([128, 8, 8])
                            sn = sinb[:, blk, :].unsqueeze(1).to_broadcast([128, 8, 8])
                            rav = ra[:].rearrange("p k (a f) -> p k a f", a=8)
                            P.op("act", I("activation", out=dst[:, :, 16:64], in_=src[:, :, 16:64], func=AF.Copy),
                                 reads=[qf_t[qi_]], writes=[qk_tt[tb]])
                            t1, t2 = src[:, :, 0:8], src[:, :, 8:16]
                            P.op("dve", I("tensor_tensor", out=rav[:, 0], in0=t1, in1=cs, op=ALU.mult),
                                 reads=[qf_t[qi_], t_const], writes=[ra_t4[0]])
                            P.op("dve", I("tensor_tensor", out=rav[:, 1], in0=t2, in1=sn, op=ALU.mult),
                                 reads=[qf_t[qi_], t_const], writes=[ra_t4[1]])
                            P.op("dve", I("tensor_tensor", out=rav[:, 2], in0=t2, in1=cs, op=ALU.mult),
                                 reads=[qf_t[qi_], t_const], writes=[ra_t4[2]])
                            P.op("dve", I("tensor_tensor", out=rav[:, 3], in0=t1, in1=sn, op=ALU.mult),
                                 reads=[qf_t[qi_], t_const], writes=[ra_t4[3]])
                            P.op("dve", I("tensor_tensor", out=dst[:, :, 0:8], in0=rav[:, 0], in1=rav[:, 1],
                                          op=ALU.subtract), reads=[ra_t4[0], ra_t4[1]], writes=[qk_tt[tb]])
                            P.op("dve", I("tensor_tensor", out=dst[:, :, 8:16], in0=rav[:, 2], in1=rav[:, 3],
                                          op=ALU.add), reads=[ra_t4[2], ra_t4[3]], writes=[qk_tt[tb]])

                        tm_group(t, Wg, wt, ev_qk)
                        chk()
                        WS.prefetch()

                        def do_tr(qk_=qk_, qk_tt=qk_tt, half=half, scr=scr, b=b, j=j):
                            for c in range(4):
                                bk = trring.next()
                                pT = banks[bk][:].bitcast(BF16)
                                for tb in range(4):
                                    P.op("pe", I("transpose",
                                        out=pT[:, tb * 128:(tb + 1) * 128], in_=qk_[:, tb, c * 128:(c + 1) * 128],
                                        identity=identb[:]), reads=[qk_tt[tb], t_const], writes=[bt[bk]],
                                        signal=(tb == 3))
                                qi = qsr.next()
                                P.op("dve", I("tensor_copy", out=qst[qi][:], in_=pT[:, 0:512]),
                                     reads=[bt[bk]], writes=[qst_t[qi]])
                                P.dma("sp", scr[b, half * 4 + c, :, j * 512:(j + 1) * 512], qst[qi][:], qst_t[qi],
                                      reads=[qst_t[qi]], store=True)
                        run_deferred(step[0])
                        deferred.append((step[0] + 2, do_tr))
                        if kind == "q" and half == 0 and t + 1 < NTT:
                            prep(t + 1)

                for half in range(2):
                    Wg, wt = WS.get()
                    vb = step[0] % 2
                    step[0] += 1

                    def ev_v(tb, bk, vb=vb):
                        if tb % 2 == 0:
                            P.op("act", I("activation", out=vtk[vb][:, tb, :], in_=banks[bk][:], func=AF.Copy),
                                 reads=[bt[bk]], writes=[vtk_t[vb]])
                        else:
                            P.op("dve", I("tensor_copy", out=vtk[vb][:, tb, :], in_=banks[bk][:]),
                                 reads=[bt[bk]], writes=[vtk_t[vb]])
                    tm_group(t, Wg, wt, ev_v)
                    chk()
                    WS.prefetch()
                    P.dma("sp", v_s[b, j * 512:(j + 1) * 512, (half) * 512:(half + 1) * 512].rearrange(
                        "(tb p) d -> p tb d", p=128), vtk[vb][:], vtk_t[vb], reads=[vtk_t[vb]], store=True)
                    run_deferred(step[0])
                run_deferred(10 ** 9)
            try:
                phaseA_body()
            except _Stop:
                pass
            P.wait_stores("sp")
            P.emit()

        with ExitStack() as st:
          if upto >= 2:
            P.stack = st

            def sb(name, shape, dt):
                return st.enter_context(nc.sbuf_tensor(name, list(shape), dt))

            kTb = [sb("kTb%d" % i, [128, S], BF16) for i in range(2)]
            qTb = [sb("qTb%d" % i, [128, S], BF16) for i in range(2)]
            vb_ = [sb("vb%d" % i, [128, NBLK, 128], BF16) for i in range(2)]
            gAb = [sb("gAb%d" % i, [128, S], BF16) for i in range(2)]
            ppb = [sb("ppb%d" % i, [128, S], BF16) for i in range(2)]
            ld_t = [{n: Tok("%s%d" % (n, i)) for n in ("k", "q", "v", "g", "p")} for i in range(2)]
            NE = 4
            Et = [sb("E%d" % i, [128, 2, 512], BF16) for i in range(NE)]
            Et_t = [Tok("E%d" % i) for i in range(NE)]
            rz = [sb("rz%d" % c, [128, 512], F32) for c in range(2)]
            tt_ = [sb("tt%d" % c, [128, 512], F32) for c in range(2)]
            ob = sb("ob", [128, 512], F32)
            sq = sb("sq", [128, 512], F32)
            vo = sb("vo", [128, 512], F32)
            rso = sb("rso", [128, 512], F32)
            a1 = sb("a1", [128, 512], F32)
            a2 = sb("a2", [128, 512], F32)
            post_t = {n: Tok("post_" + n) for n in ("rz0", "rz1", "tt0", "tt1", "ob", "sq", "vo", "rso", "a1", "a2")}
            mst = [sb("mst%d" % i, [128, 512], BF16) for i in range(2)]
            mst_t = [Tok("mst%d" % i) for i in range(2)]
            Ocp = [sb("Ocp%d" % i, [128, 2, 512], F32) for i in range(2)]
            Zcp = [sb("Zcp%d" % i, [128, 2, 512], F32) for i in range(2)]
            Ocp_t = [Tok("Ocp%d" % i) for i in range(2)]
            Zcp_t = [Tok("Zcp%d" % i) for i in range(2)]
            ob2 = [sb("ob2_%d" % i, [128, 512], F32) for i in range(2)]
            ob2_t = [Tok("ob2_%d" % i) for i in range(2)]
            sq2 = [sb("sq2_%d" % i, [128, 512], F32) for i in range(2)]
            sq2_t = [Tok("sq2_%d" % i) for i in range(2)]
            zz = [sb("zz%d" % i, [128, 512], F32) for i in range(2)]
            zz_t = [Tok("zz%d" % i) for i in range(2)]
            pending = []

            spairs = Ring([(0, 1), (2, 3)])
            O1, O2, Z1, Z2 = 4, 5, 6, 7
            ering = Ring(list(range(NE)))

            def load_bh(i):
                b, hh = divmod(i, NH)
                s = i % 2
                P.dma("sp", kTb[s][:], kT_s[b, hh], ld_t[s]["k"], writes=[ld_t[s]["k"]])
                P.dma("sp", qTb[s][:], qT_s[b, hh], ld_t[s]["q"], writes=[ld_t[s]["q"]])
                P.dma("sp", vb_[s][:], v_s[b, :, hh * 128:(hh + 1) * 128].rearrange("(k p) d -> p k d", p=128),
                      ld_t[s]["v"], writes=[ld_t[s]["v"]])
                P.dma("sp", gAb[s][:], gA_s[b, hh], ld_t[s]["g"], writes=[ld_t[s]["g"]])
                P.dma("sp", ppb[s][:], pp_s[b, hh], ld_t[s]["p"], writes=[ld_t[s]["p"]])

            NBH = NB * NH
            load_bh(0)
            mcount = 0
            for i in range(NBH):
                b, hh = divmod(i, NH)
                s = i % 2
                L = ld_t[s]
                for j in range(NT):
                    nkb = 4 * j + 4
                    qc0 = j * 512
                    if j == 1 or (NT == 1 and j == 0):
                        if NT == 1:
                            while pending:
                                pending.pop(0)()
                        if i + 1 < NBH:
                            load_bh(i + 1)

                    def s_mm(kb):
                        q0 = 128 * max(0, kb - 4 * j)
                        pr = spairs.next()
                        for c in range(2):
                            lo, hi = 64 * c, 64 * (c + 1)
                            P.op("pe", I("matmul",
                                banks[pr[c]][:, q0:512], lhsT=kTb[s][lo:hi, kb * 128:(kb + 1) * 128],
                                rhs=qTb[s][lo:hi, qc0 + q0:qc0 + 512], start=True, stop=True),
                                reads=[L["k"], L["q"]], writes=[bt[pr[c]]])
                        return pr, q0

                    def exp_pv(kb, pr, q0):
                        ei = ering.next()
                        P.op("act", I("activation", out=Et[ei][:, :, q0:512], in_=psum_all[:, pr[0]:pr[0] + 2, q0:512],
                                      func=AF.Exp, scale=0.125),
                             reads=[bt[pr[0]], bt[pr[1]]], writes=[Et_t[ei]])
                        if kb >= 4 * j:
                            P.op("pool", I("tensor_tensor", out=Et[ei][:, :, q0:q0 + 128], in0=Et[ei][:, :, q0:q0 + 128],
                                           in1=maskb[:].unsqueeze(1).to_broadcast([128, 2, 128]), op=ALU.mult),
                                 reads=[t_const], writes=[Et_t[ei]])
                        first, last = (kb == 0), (kb == nkb - 1)
                        for c, (ob_, zb_) in enumerate(((O1, Z1), (O2, Z2))):
                            P.op("pe", I("matmul",
                                banks[ob_][:, q0:512], lhsT=vb_[s][:, kb, :], rhs=Et[ei][:, c, q0:512],
                                start=first, stop=last), reads=[L["v"], Et_t[ei]], writes=[bt[ob_]], signal=False)
                            P.op("pe", I("matmul",
                                banks[zb_][:, q0:512], lhsT=onesb[:], rhs=Et[ei][:, c, q0:512],
                                start=first, stop=last), reads=[t_const, Et_t[ei]], writes=[bt[zb_]],
                                signal=(c == 1))

                    prev = s_mm(0)
                    trig = min(9, nkb - 1)
                    for kb in range(nkb):
                        nxt = s_mm(kb + 1) if kb + 1 < nkb else None
                        exp_pv(kb, prev[0], prev[1])
                        prev = nxt
                        if kb == trig and pending:
                            pending.pop(0)()
                    pi_ = mcount % 2
                    P.op("act", I("activation", out=Ocp[pi_][:], in_=psum_all[:, 4:6, :], func=AF.Copy),
                         reads=[bt[O1], bt[O2]], writes=[Ocp_t[pi_]])
                    P.op("dve", I("tensor_copy", out=Zcp[pi_][:], in_=psum_all[:, 6:8, :]),
                         reads=[bt[Z1], bt[Z2]], writes=[Zcp_t[pi_]])
                    P.op("dve", I("tensor_tensor", out=zz[pi_][:], in0=Zcp[pi_][:, 0, :], in1=Zcp[pi_][:, 1, :], op=ALU.mult),
                         reads=[Zcp_t[pi_]], writes=[zz_t[pi_]])
                    P.op("dve", I("tensor_tensor", out=Ocp[pi_][:, 0, :], in0=Ocp[pi_][:, 0, :], in1=Zcp[pi_][:, 1, :],
                                  op=ALU.mult), reads=[Zcp_t[pi_], Ocp_t[pi_]], writes=[Ocp_t[pi_]])
                    P.op("dve", I("tensor_tensor", out=Ocp[pi_][:, 1, :], in0=Ocp[pi_][:, 1, :], in1=Zcp[pi_][:, 0, :],
                                  op=ALU.mult), reads=[Zcp_t[pi_], Ocp_t[pi_]], writes=[Ocp_t[pi_]])
                    P.op("dve", I("scalar_tensor_tensor", out=ob2[pi_][:], in0=Ocp[pi_][:, 1, :], scalar=neglam[:, 0:1],
                                  in1=Ocp[pi_][:, 0, :], op0=ALU.mult, op1=ALU.add),
                         reads=[Ocp_t[pi_], t_const], writes=[ob2_t[pi_]])
                    P.op("dve", I("tensor_tensor", out=sq2[pi_][:], in0=ob2[pi_][:], in1=ob2[pi_][:], op=ALU.mult),
                         reads=[ob2_t[pi_]], writes=[sq2_t[pi_]])
                    P.op("dve", I("scalar_tensor_tensor", out=zz[pi_][:], in0=zz[pi_][:], scalar=EPS, in1=zz[pi_][:],
                                  op0=ALU.mult, op1=ALU.mult), reads=[zz_t[pi_]], writes=[zz_t[pi_]])
                    mi = mcount % 2
                    mcount += 1

                    def part2(pi_=pi_, mi=mi, s=s, L=L, qc0=qc0, b=b, hh=hh):
                        pt = post_t
                        pr = spairs.next()
                        spairs.next()
                        P.op("pe", I("matmul", banks[pr[0]][:], lhsT=onesf[:], rhs=sq2[pi_][:], start=True, stop=True),
                             reads=[sq2_t[pi_], t_const], writes=[bt[pr[0]], bt[pr[1]]])
                        P.op("dve", I("scalar_tensor_tensor", out=vo[:], in0=banks[pr[0]][:], scalar=1.0 / 128,
                                      in1=zz[pi_][:], op0=ALU.mult, op1=ALU.add),
                             reads=[bt[pr[0]], zz_t[pi_]], writes=[pt["vo"]])
                        P.op("act", I("activation", out=vo[:], in_=vo[:], func=AF.Ln), reads=[pt["vo"]], writes=[pt["vo"]])
                        P.op("act", I("activation", out=rso[:], in_=vo[:], func=AF.Exp, scale=-0.5),
                             reads=[pt["vo"]], writes=[pt["rso"]])
                        P.op("dve", I("scalar_tensor_tensor", out=a1[:], in0=ob2[pi_][:], scalar=gs_col[:, 0:1],
                                      in1=rso[:], op0=ALU.mult, op1=ALU.mult),
                             reads=[ob2_t[pi_], pt["rso"], t_const], writes=[pt["a1"]])
                        P.op("dve", I("scalar_tensor_tensor", out=a2[:], in0=gAb[s][:, qc0:qc0 + 512], scalar=1.0,
                                      in1=a1[:], op0=ALU.add, op1=ALU.mult),
                             reads=[L["g"], pt["a1"]], writes=[pt["a2"]])
                        P.op("pool", I("tensor_tensor", out=mst[mi][:], in0=a2[:], in1=ppb[s][:, qc0:qc0 + 512],
                                       op=ALU.add), reads=[pt["a2"], L["p"]], writes=[mst_t[mi]])
                        P.dma("sp", m_s[b, hh, :, qc0:qc0 + 512], mst[mi][:], mst_t[mi], reads=[mst_t[mi]], store=True)
                    pending.append(part2)
            while pending:
                pending.pop(0)()
            P.wait_stores("sp")
            P.emit()

        with ExitStack() as st:
          if upto >= 3:
            P.stack = st

            def sb(name, shape, dt):
                return st.enter_context(nc.sbuf_tensor(name, list(shape), dt))

            xt = [sb("xc%d" % i, [128, 4, D], F32) for i in range(2)]
            xt_t = [[Tok("xc%d_%d" % (i, tb)) for tb in range(4)] for i in range(2)]
            mT = [sb("mT%d" % i, [128, 8, 512], BF16) for i in range(2)]
            mT_t = [Tok("mT%d" % i) for i in range(2)]
            junk = sb("junkC", [128, D], BF16)
            junk_t = Tok("junkC")
            ss = sb("ssC", [128, 8], F32)
            vv = sb("vvC", [128, 8], F32)
            rstd = sb("rstdC", [128, 8], F32)
            st2_t = Tok("stats2")
            st3_t = Tok("stats3")
            hb = sb("hbC", [128, 4, D], BF16)
            hb_t = [Tok("hbC%d" % i) for i in range(4)]
            h2T = sb("h2T", [128, 8, 512], BF16)
            h2T_t = [Tok("h2T%d" % k) for k in range(8)]
            zT = sb("zT", [128, 32, 512], BF16)
            zT_t = [Tok("zT%d" % k) for k in range(32)]
            NR = 3
            rt = [sb("rt%d" % i, [128, 512], F32) for i in range(NR)]
            rt_t = [Tok("rt%d" % i) for i in range(NR)]
            NWS = 4
            wsl = [sb("wslC%d" % i, [128, 8, 512], BF16) for i in range(NWS)]
            wsl_t = [Tok("wslC%d" % i) for i in range(NWS)]

            mmring = Ring([0, 1, 2, 3, 4, 5])
            trring = Ring([6, 7])
            rring = Ring(list(range(NR)))
            order = list(G_OUT) + list(G_UP) + list(G_DOWN)
            WS = WStream(P, wsl, wsl_t, wscr, order * NTT)

            def load_c(t):
                b, j = divmod(t, NT)
                P.dma("sp", xt[t % 2][:], x_d[b, j * 512:(j + 1) * 512, :].rearrange("(tb p) d -> p tb d", p=128),
                      xt_t[t % 2][0], writes=xt_t[t % 2])
                P.dma("sp", mT[t % 2][:], m_s[b, :, :, j * 512:(j + 1) * 512].rearrange("h p n -> p h n"),
                      mT_t[t % 2], writes=[mT_t[t % 2]])

            load_c(0)
            stageC = [0]

            def chkC():
                stageC[0] += 1
                if stageC[0] >= dbgC:
                    raise _Stop()

            def phaseC_body():
              for t in range(NTT):
                b, j = divmod(t, NT)
                xb, xtk = xt[t % 2], xt_t[t % 2]
                mb, mtk = mT[t % 2], mT_t[t % 2]
                if t + 1 < NTT:
                    load_c(t + 1)
                for nh in range(2):
                    Wg, wt = WS.get()
                    for tb in range(4):
                        bk = mmring.next()
                        for kc in range(8):
                            P.op("pe", I("matmul",
                                banks[bk][:], lhsT=mb[:, kc, tb * 128:(tb + 1) * 128], rhs=Wg[:, kc, :],
                                start=(kc == 0), stop=(kc == 7)), reads=[wt, mtk], writes=[bt[bk]], signal=(kc == 7))
                        P.op("dve", I("tensor_tensor",
                            out=xb[:, tb, nh * 512:(nh + 1) * 512], in0=banks[bk][:],
                            in1=xb[:, tb, nh * 512:(nh + 1) * 512], op=ALU.add), reads=[bt[bk], xtk[tb]], writes=[xtk[tb]])
                    WS.prefetch()
                chkC()
                for tb in range(4):
                    P.op("act", I("activation", out=junk[:], in_=xb[:, tb, :], func=AF.Square,
                                                              accum_out=ss[:, tb:tb + 1]),
                         reads=[xtk[tb]], writes=[junk_t, st2_t])
                P.op("dve", I("tensor_scalar", out=vv[:, 0:4], in0=ss[:, 0:4], scalar1=1.0 / D, scalar2=EPS,
                                                      op0=ALU.mult, op1=ALU.add), reads=[st2_t], writes=[st2_t])
                P.op("act", I("activation", out=vv[:, 0:4], in_=vv[:, 0:4], func=AF.Ln), reads=[st2_t], writes=[st2_t])
                P.op("act", I("activation", out=rstd[:, 0:4], in_=vv[:, 0:4], func=AF.Exp, scale=-0.5), reads=[st2_t], writes=[st2_t])
                for tb in range(4):
                    P.op("act", I("activation", out=hb[:, tb, :], in_=xb[:, tb, :], func=AF.Copy,
                                                              scale=rstd[:, tb:tb + 1]),
                         reads=[xtk[tb], st2_t], writes=[hb_t[tb]])
                for kc in range(8):
                    bk = trring.next()
                    pT = banks[bk][:].bitcast(BF16)
                    for tb in range(4):
                        P.op("pe", I("transpose",
                            out=pT[:, tb * 128:(tb + 1) * 128], in_=hb[:, tb, kc * 128:(kc + 1) * 128],
                            identity=identb[:]), reads=[hb_t[tb], t_const], writes=[bt[bk]], signal=(tb == 3))
                    P.op("dve", I("tensor_copy", out=h2T[:, kc, :], in_=pT[:, 0:512]),
                         reads=[bt[bk]], writes=[h2T_t[kc]])
                chkC()
                for g in range(8):
                    Wg, wt = WS.get()
                    for c in range(4):
                        bk = mmring.next()
                        for kc in range(8):
                            P.op("pe", I("matmul",
                                banks[bk][:], lhsT=Wg[:, kc, c * 128:(c + 1) * 128], rhs=h2T[:, kc, :],
                                start=(kc == 0), stop=(kc == 7)), reads=[wt, h2T_t[kc]], writes=[bt[bk]],
                                signal=(kc == 7))
                        ri = rring.next()
                        fi = 4 * g + c
                        P.op("act", I("activation", out=rt[ri][:], in_=banks[bk][:], func=AF.Relu),
                             reads=[bt[bk]], writes=[rt_t[ri]])
                        P.op("dve", I("tensor_tensor",
                            out=zT[:, fi, :], in0=banks[bk][:], in1=rt[ri][:], op=ALU.mult),
                            reads=[bt[bk], rt_t[ri]], writes=[zT_t[fi]])
                    WS.prefetch()
                chkC()
                for nh in range(2):
                    bks = [mmring.next() for _ in range(4)]
                    for fg in range(4):
                        Wg, wt = WS.get()
                        for tb in range(4):
                            bk = bks[tb]
                            for fc in range(8):
                                fi = 8 * fg + fc
                                P.op("pe", I("matmul",
                                    banks[bk][:], lhsT=zT[:, fi, tb * 128:(tb + 1) * 128], rhs=Wg[:, fc, :],
                                    start=(fg == 0 and fc == 0), stop=(fg == 3 and fc == 7)),
                                    reads=[wt, zT_t[fi]], writes=[bt[bk]], signal=(fc == 7))
                            if fg == 3:
                                P.op("dve", I("tensor_tensor",
                                    out=xb[:, tb, nh * 512:(nh + 1) * 512], in0=banks[bk][:],
                                    in1=xb[:, tb, nh * 512:(nh + 1) * 512], op=ALU.add),
                                    reads=[bt[bk], xtk[tb]], writes=[xtk[tb]])
                        WS.prefetch()
                chkC()
                for tb in range(4):
                    P.op("act", I("activation", out=junk[:], in_=xb[:, tb, :], func=AF.Square,
                                                              accum_out=ss[:, 4 + tb:5 + tb]),
                         reads=[xtk[tb]], writes=[junk_t, st3_t])
                P.op("dve", I("tensor_scalar", out=vv[:, 4:8], in0=ss[:, 4:8], scalar1=1.0 / D, scalar2=EPS,
                                                      op0=ALU.mult, op1=ALU.add), reads=[st3_t], writes=[st3_t])
                P.op("act", I("activation", out=vv[:, 4:8], in_=vv[:, 4:8], func=AF.Ln), reads=[st3_t], writes=[st3_t])
                P.op("act", I("activation", out=rstd[:, 4:8], in_=vv[:, 4:8], func=AF.Exp, scale=-0.5), reads=[st3_t], writes=[st3_t])
                for tb in range(4):
                    P.op("dve", I("scalar_tensor_tensor",
                        out=xb[:, tb, :], in0=xb[:, tb, :], scalar=rstd[:, 4 + tb:5 + tb], in1=gfin_bc[:],
                        op0=ALU.mult, op1=ALU.mult), reads=[xtk[tb], st3_t, t_const], writes=[xtk[tb]])
                P.dma("sp", y_d[b, j * 512:(j + 1) * 512, :].rearrange("(tb p) d -> p tb d", p=128), xb[:],
                      xtk[0], reads=xtk, store=True)
            try:
                phaseC_body()
            except _Stop:
                pass
            P.wait_stores("sp")
            P.emit()
    return nc


def _consts():
    ident = np.eye(128, dtype=np.float32)
    kk = np.arange(128)[:, None]
    qq = np.arange(128)[None, :]
    mask = (qq >= kk).astype(np.float32)
    invf = np.power(np.float32(500000.0), -(np.arange(0, 16, 2, dtype=np.float32) / np.float32(16))).astype(np.float32)
    wct = np.zeros((4, 16), np.float32)
    for c in range(4):
        w = 2 ** (c + 1)
        for t_ in range(16):
            wct[c, t_] = w / min(t_ + 1, w)
    return ident, mask, invf, wct


def prep_core_inputs(inputs, b0, NB):
    S = inputs["x"].shape[1]
    f = lambda a: np.ascontiguousarray(np.asarray(a, dtype=np.float32))
    ident, mask, invf, wct = _consts()
    pos = np.asarray(inputs["positions"])[b0:b0 + NB].astype(np.int32)
    pos_t = np.ascontiguousarray(pos.reshape(NB, S // 128, 128).transpose(0, 2, 1))
    return {
        "x": f(inputs["x"][b0:b0 + NB]),
        "pos_t": pos_t,
        "w_in": f(inputs["w_in"][0]),
        "w_out": f(inputs["w_out"][0]),
        "w_up": f(inputs["w_up"][0]),
        "w_down": f(inputs["w_down"][0]),
        "w_pool_r": f(np.asarray(inputs["w_pool"][0]).transpose(1, 0, 2)),
        "gcol_attn": f(np.asarray(inputs["norm_attn_g"][0]).reshape(8, 128).T),
        "gcol_mlp": f(np.asarray(inputs["norm_mlp_g"][0]).reshape(8, 128).T),
        "gfin": f(inputs["final_norm_g"]),
        "subln_col": f(np.asarray(inputs["subln_g"][0]).reshape(128, 1)),
        "pool_scale": f(inputs["pool_scale"][0]),
        "lam4": f(np.stack([np.asarray(inputs[k][0]) for k in ("lam_q1", "lam_k1", "lam_q2", "lam_k2")])),
        "c_ident": ident, "c_mask": mask, "c_invf": invf, "c_wct": wct,
    }


_NC_CACHE = {}


def kernel(**inputs):
    x = np.asarray(inputs["x"])
    B, S, _ = x.shape
    n = N_CORES
    NB = B // n
    key = (NB, S)
    if key not in _NC_CACHE:
        _NC_CACHE[key] = build_nc(NB, S)
    nc = _NC_CACHE[key]
    in_maps = [prep_core_inputs(inputs, i * NB, NB) for i in range(n)]
    res = run_bass_kernel_spmd(nc, in_maps, core_ids=list(range(n)))
    return np.concatenate([np.asarray(r["y"]) for r in res.results], axis=0).astype(np.float32)
```

```python
import math
from contextlib import ExitStack

import numpy as np
import concourse.bass as bass
import concourse.mybir as mybir
from concourse.bass_utils import run_bass_kernel_spmd

F32 = mybir.dt.float32
BF16 = mybir.dt.bfloat16
I32 = mybir.dt.int32
AF = mybir.ActivationFunctionType
ALU = mybir.AluOpType

D = 1024
NH = 8
DFF = 4096
INW = 5632
EPS = 1e-6
LAM_INIT = 0.8 - 0.6 * math.exp(-0.3 * 0)
N_CORES = 8
TWO_PI = 2.0 * math.pi
CW1 = 6.28125
CW2 = TWO_PI - CW1


class Tok:
    __slots__ = ("name", "w", "r", "dsem", "dkey", "dval")

    def __init__(self, name):
        self.name = name
        self.w = None
        self.r = {}
        self.dsem = None
        self.dkey = None
        self.dval = 0


class Eng:
    def __init__(self, name, sem):
        self.name = name
        self.sem = sem
        self.prog = []
        self.count = 0
        self.seen = {}
        self.nwait = 0
        self.nops = 0


class Prog:
    def __init__(self, nc, stack):
        self.nc = nc
        self.gstack = stack
        self.stack = stack
        self.engs = {}
        self.semmap = {}
        self.nkey = 0
        for name in ("pe", "act", "dve", "pool", "sp"):
            s = stack.enter_context(nc.semaphore("p_" + name))
            self.engs[name] = Eng(name, s)
        self.stores = []

    def new_dsem(self, name):
        s = self.gstack.enter_context(self.nc.semaphore(name))
        self.nkey += 1
        self.semmap[self.nkey] = s
        return s, self.nkey

    def _deps(self, reads, writes, extra):
        deps = {}

        def add(k, v):
            if deps.get(k, 0) < v:
                deps[k] = v
        for t in reads:
            if t.w is not None:
                add(*t.w)
        for t in writes:
            if t.w is not None:
                add(*t.w)
            for k, v in t.r.items():
                add(k, v)
        for ev in extra:
            if ev is not None:
                add(*ev)
        return deps

    def _wait(self, e, deps):
        for k, v in deps.items():
            if k == "pe" and e.name == "pe":
                continue
            if e.seen.get(k, 0) >= v:
                continue
            e.seen[k] = v
            sem = self.engs[k].sem if isinstance(k, str) else self.semmap[k]
            e.prog.append(("wait", sem, v))
            e.nwait += 1

    @staticmethod
    def _record(ev, reads, writes):
        k, v = ev
        for t in reads:
            if t.r.get(k, 0) < v:
                t.r[k] = v
        for t in writes:
            t.w = ev
            t.r = {}

    def op(self, eng, fn, reads=(), writes=(), signal=True, extra=()):
        e = self.engs[eng]
        self._wait(e, self._deps(reads, writes, extra))
        e.prog.append(("inst", fn, signal))
        e.nops += 1
        if signal:
            e.count += 1
            ev = (e.name, e.count)
        else:
            ev = (e.name, e.count + 1)
        self._record(ev, reads, writes)
        return ev

    def dma(self, q, out, in_, tok, reads=(), writes=(), extra=(), store=False):
        e = self.engs[q]
        self._wait(e, self._deps(reads, writes, extra))
        if tok.dsem is None:
            tok.dsem, tok.dkey = self.new_dsem("d_" + tok.name)
        e.prog.append(("dma", out, in_, tok.dsem))
        e.nops += 1
        tok.dval += 16
        ev = (tok.dkey, tok.dval)
        self._record(ev, reads, writes)
        if store:
            self.stores.append(ev)
        return ev

    def wait_toks(self, q, toks):
        e = self.engs[q]
        self._wait(e, self._deps(toks, (), ()))

    def wait_stores(self, q="sp"):
        e = self.engs[q]
        deps = {}
        for k, v in self.stores:
            if deps.get(k, 0) < v:
                deps[k] = v
        self.stores = []
        self._wait(e, deps)

    def emit(self):
        nc = self.nc
        with nc.Block() as block:
            def mk(e):
                def body(h):
                    for it in e.prog:
                        if it[0] == "wait":
                            h.wait_ge(it[1], it[2])
                        elif it[0] == "inst":
                            nm, a_, kw_ = it[1]
                            inst = getattr(h, nm)(*a_, **kw_)
                            if it[2]:
                                inst.then_inc(e.sem, 1)
                        else:
                            h.dma_start(out=it[1], in_=it[2]).then_inc(it[3], 16)
                    e.prog = []
                return body
            block.tensor(mk(self.engs["pe"]))
            block.scalar(mk(self.engs["act"]))
            block.vector(mk(self.engs["dve"]))
            block.gpsimd(mk(self.engs["pool"]))
            block.sync(mk(self.engs["sp"]))


def I(name, *args, **kw):
    return (name, args, kw)


class Ring:
    def __init__(self, items):
        self.items = items
        self.i = 0

    def next(self):
        it = self.items[self.i % len(self.items)]
        self.i += 1
        return it


class WStream:
    def __init__(self, P, slots, toks, wscr, seq, extra=()):
        self.P = P
        self.slots = slots
        self.toks = toks
        self.wscr = wscr
        self.seq = seq
        self.issued = 0
        self.used = 0
        self.extra = extra

    def _issue(self):
        i = self.issued
        s = i % len(self.slots)
        self.P.dma("sp", self.slots[s][:], self.wscr[self.seq[i]], self.toks[s],
                   writes=[self.toks[s]], extra=self.extra)
        self.issued += 1

    def get(self):
        i = self.used
        while self.issued <= i:
            self._issue()
        self.used += 1
        return self.slots[i % len(self.slots)], self.toks[i % len(self.slots)]

    def prefetch(self):
        R = len(self.slots)
        while self.issued < min(len(self.seq), self.used + R):
            self._issue()


class _Stop(Exception):
    pass


def build_nc(NB, S, upto=3, debug=False, dbgA=10 ** 9, dbgC=10 ** 9):
    NT = S // 512
    NBLK = S // 128
    NTT = NB * NT
    nc = bass.Bass("TRN2", target_bir_lowering=False)

    def din(name, shape, dt=F32):
        return nc.dram_tensor(name, list(shape), dt, kind="ExternalInput").ap()

    x_d = din("x", [NB, S, D])
    post_d = din("pos_t", [NB, 128, NBLK], I32)
    w_in_d = din("w_in", [D, INW])
    w_out_d = din("w_out", [D, D])
    w_up_d = din("w_up", [D, DFF])
    w_down_d = din("w_down", [DFF, D])
    wpool_d = din("w_pool_r", [128, 4, 256])
    gattn_d = din("gcol_attn", [128, 8])
    gmlp_d = din("gcol_mlp", [128, 8])
    gfin_d = din("gfin", [D])
    subln_d = din("subln_col", [128, 1])
    pscale_d = din("pool_scale", [D])
    lam_d = din("lam4", [4, 64])
    ident_d = din("c_ident", [128, 128])
    mask_d = din("c_mask", [128, 128])
    invf_d = din("c_invf", [8])
    wct_d = din("c_wct", [4, 16])
    y_d = nc.dram_tensor("y", [NB, S, D], F32, kind="ExternalOutput").ap()

    def dscr(name, shape, dt=BF16):
        return nc.dram_tensor(name, list(shape), dt, kind=("ExternalOutput" if debug else "Internal")).ap()

    NWG = 29
    wscr = dscr("wscr", [NWG, 128, 8, 512])
    qT_s = dscr("qT_s", [NB, NH, 128, S])
    kT_s = dscr("kT_s", [NB, NH, 128, S])
    v_s = dscr("v_s", [NB, S, D])
    gA_s = dscr("gA_s", [NB, NH, 128, S])
    pp_s = dscr("pp_s", [NB, NH, 128, S])
    m_s = dscr("m_s", [NB, NH, 128, S])

    G_Q, G_K, G_V, G_POOL, G_GA, G_GP = (0, 1), (2, 3), (4, 5), (6,), (7, 8), (9, 10)
    G_OUT = (11, 12)
    G_UP = tuple(range(13, 21))
    G_DOWN = tuple(range(21, 29))

    with ExitStack() as gst:
        P = Prog(nc, gst)

        def gsb(name, shape, dt):
            return gst.enter_context(nc.sbuf_tensor(name, list(shape), dt))

        psum_all = gst.enter_context(nc.psum_tensor("psum_all", [128, 8, 512], F32))
        banks = [psum_all[:, i, :] for i in range(8)]
        bt = [Tok("bank%d" % i) for i in range(8)]

        identb = gsb("identb", [128, 128], BF16)
        onesb = gsb("onesb", [128, 128], BF16)
        onesf = gsb("onesf", [128, 128], F32)
        maskb = gsb("maskb", [128, 128], BF16)
        gfin_bc = gsb("gfin_bc", [128, D], F32)
        wpoolb = gsb("wpoolb", [128, 4, 256], BF16)
        gattn = gsb("gattn", [128, 8], F32)
        gmlp = gsb("gmlp", [128, 8], F32)
        gs_col = gsb("gs_col", [128, 1], F32)
        neglam = gsb("neglam", [128, 1], F32)
        cosb = gsb("cosb", [128, NB * NBLK, 8], F32)
        sinb = gsb("sinb", [128, NB * NBLK, 8], F32)
        wct = gsb("wct", [128, 4, 16], F32)
        mhalf = gsb("mhalf", [128, 512], F32)
        eps_col = gsb("eps_col", [128, 1], F32)
        t_const = Tok("consts")

        with ExitStack() as st:
            P.stack = st

            def sb(name, shape, dt):
                return st.enter_context(nc.sbuf_tensor(name, list(shape), dt))

            idf = sb("idf", [128, 128], F32)
            mkf = sb("mkf", [128, 128], F32)
            wpf = sb("wpf", [128, 4, 256], F32)
            psb = sb("psb", [128, D], F32)
            lamt = sb("lamt", [128, 4, 64], F32)
            lprod = sb("lprod", [128, 2, 64], F32)
            lsum = sb("lsum", [128, 2], F32)
            lexp = sb("lexp", [128, 2], F32)
            subl = sb("subl", [128, 1], F32)
            posi = sb("posi", [128, NB * NBLK], I32)
            posf = sb("posf", [128, NB * NBLK], F32)
            invf = sb("invf", [128, 8], F32)
            NA = NB * NBLK * 8
            ang = sb("ang", [128, NA], F32)
            tq = sb("tq", [128, NA], F32)
            ki = sb("ki", [128, NA], I32)
            kf = sb("kf", [128, NA], F32)
            r0 = sb("r0", [128, NA], F32)
            r1 = sb("r1", [128, NA], F32)
            mk1 = sb("mk1", [128, NA], F32)
            rc = sb("rc", [128, NA], F32)
            tl = {n: Tok(n) for n in ("idf", "mkf", "wpf", "psb", "lamt", "subl", "posi", "invf", "gat", "gml",
                                      "gfin", "wct")}
            P.dma("sp", idf[:], ident_d, tl["idf"], writes=[tl["idf"]])
            P.dma("sp", mkf[:], mask_d, tl["mkf"], writes=[tl["mkf"]])
            P.dma("sp", wpf[:], wpool_d, tl["wpf"], writes=[tl["wpf"]])
            P.dma("sp", psb[:], pscale_d.partition_broadcast(128), tl["psb"], writes=[tl["psb"]])
            P.dma("sp", lamt[:], lam_d.partition_broadcast(128), tl["lamt"], writes=[tl["lamt"]])
            P.dma("sp", subl[:], subln_d, tl["subl"], writes=[tl["subl"]])
            P.dma("sp", posi[:].rearrange("p (b k) -> p b k", b=NB), post_d.rearrange("b p k -> p b k"),
                  tl["posi"], writes=[tl["posi"]])
            P.dma("sp", invf[:], invf_d.partition_broadcast(128), tl["invf"], writes=[tl["invf"]])
            P.dma("sp", gattn[:], gattn_d, tl["gat"], writes=[tl["gat"]])
            P.dma("sp", gmlp[:], gmlp_d, tl["gml"], writes=[tl["gml"]])
            P.dma("sp", gfin_bc[:], gfin_d.partition_broadcast(128), tl["gfin"], writes=[tl["gfin"]])
            P.dma("sp", wct[:], wct_d.partition_broadcast(128), tl["wct"], writes=[tl["wct"]])

            tc_ = t_const
            P.op("dve", I("tensor_copy", out=identb[:], in_=idf[:]), reads=[tl["idf"]], writes=[tc_])
            P.op("dve", I("tensor_copy", out=maskb[:], in_=mkf[:]), reads=[tl["mkf"]], writes=[tc_])
            P.op("pool", I("memset", onesb[:], 1.0), writes=[tc_])
            P.op("pool", I("memset", onesf[:], 1.0), writes=[tc_])
            P.op("pool", I("memset", mhalf[:], -0.5), writes=[tc_])
            P.op("pool", I("memset", eps_col[:], EPS), writes=[tc_])
            for g in range(4):
                wg_ = float(2 ** (g + 1))
                P.op("dve", I("scalar_tensor_tensor",
                    out=wpoolb[:, g, :], in0=wpf[:, g, :], scalar=0.5 / wg_, in1=psb[:, g * 256:(g + 1) * 256],
                    op0=ALU.mult, op1=ALU.mult), reads=[tl["wpf"], tl["psb"]], writes=[tc_])
            P.op("dve", I("tensor_scalar", out=gs_col[:], in0=subl[:], scalar1=0.5 * (1.0 - LAM_INIT),
                                                  scalar2=None, op0=ALU.mult), reads=[tl["subl"]], writes=[tc_])
            tlp = Tok("lprod")
            P.op("dve", I("tensor_tensor", out=lprod[:, 0, :], in0=lamt[:, 0, :], in1=lamt[:, 1, :],
                                                  op=ALU.mult), reads=[tl["lamt"]], writes=[tlp])
            P.op("dve", I("tensor_tensor", out=lprod[:, 1, :], in0=lamt[:, 2, :], in1=lamt[:, 3, :],
                                                  op=ALU.mult), reads=[tl["lamt"], tlp], writes=[tlp])
            P.op("dve", I("tensor_reduce", out=lsum[:], in_=lprod[:], op=ALU.add,
                                                  axis=mybir.AxisListType.X), reads=[tlp], writes=[tlp])
            P.op("act", I("activation", out=lexp[:], in_=lsum[:], func=AF.Exp), reads=[tlp], writes=[tlp])
            P.op("dve", I("tensor_tensor", out=lsum[:, 0:1], in0=lexp[:, 1:2], in1=lexp[:, 0:1],
                                                  op=ALU.subtract), reads=[tlp], writes=[tlp])
            P.op("dve", I("tensor_scalar", out=neglam[:], in0=lsum[:, 0:1], scalar1=-LAM_INIT, scalar2=None,
                                                  op0=ALU.add), reads=[tlp], writes=[tc_])
            ta = Tok("ang")
            P.op("dve", I("tensor_copy", out=posf[:], in_=posi[:]), reads=[tl["posi"]], writes=[ta])
            P.op("dve", I("tensor_tensor",
                out=ang[:].rearrange("p (k f) -> p k f", f=8),
                in0=posf[:].unsqueeze(2).to_broadcast([128, NB * NBLK, 8]),
                in1=invf[:].unsqueeze(1).to_broadcast([128, NB * NBLK, 8]), op=ALU.mult),
                reads=[tl["invf"], ta], writes=[ta])

            def reduce_to(dst, src_shift):
                P.op("dve", I("tensor_scalar", out=tq[:], in0=ang[:], scalar1=src_shift, scalar2=1.0 / TWO_PI,
                                                      op0=ALU.add, op1=ALU.mult), reads=[ta], writes=[ta])
                P.op("dve", I("tensor_copy", out=ki[:], in_=tq[:]), reads=[ta], writes=[ta])
                P.op("dve", I("tensor_copy", out=kf[:], in_=ki[:]), reads=[ta], writes=[ta])
                P.op("dve", I("tensor_scalar", out=rc[:], in0=ang[:], scalar1=src_shift, scalar2=None,
                                                      op0=ALU.add), reads=[ta], writes=[ta])
                P.op("dve", I("scalar_tensor_tensor", out=r0[:], in0=kf[:], scalar=-CW1, in1=rc[:],
                                                             op0=ALU.mult, op1=ALU.add), reads=[ta], writes=[ta])
                P.op("dve", I("scalar_tensor_tensor", out=r1[:], in0=kf[:], scalar=-CW2, in1=r0[:],
                                                             op0=ALU.mult, op1=ALU.add), reads=[ta], writes=[ta])
                P.op("dve", I("tensor_scalar", out=mk1[:], in0=r1[:], scalar1=math.pi, scalar2=-TWO_PI,
                                                      op0=ALU.is_gt, op1=ALU.mult), reads=[ta], writes=[ta])
                P.op("dve", I("tensor_tensor", out=r0[:], in0=r1[:], in1=mk1[:], op=ALU.add),
                     reads=[ta], writes=[ta])
                P.op("dve", I("tensor_scalar", out=mk1[:], in0=r0[:], scalar1=-math.pi, scalar2=TWO_PI,
                                                      op0=ALU.is_lt, op1=ALU.mult), reads=[ta], writes=[ta])
                P.op("dve", I("tensor_tensor", out=r1[:], in0=r0[:], in1=mk1[:], op=ALU.add),
                     reads=[ta], writes=[ta])
                P.op("dve", I("tensor_scalar", out=r1[:], in0=r1[:], scalar1=math.pi, scalar2=-math.pi,
                                                      op0=ALU.min, op1=ALU.max), reads=[ta], writes=[ta])
                P.op("act", I("activation", out=dst[:].rearrange("p k f -> p (k f)"), in_=r1[:], func=AF.Sin),
                     reads=[ta], writes=[ta, tc_])

            reduce_to(sinb, 0.0)
            reduce_to(cosb, 0.5 * math.pi)

            NST = 4
            wst = [sb("wst%d" % i, [128, 8, 512], F32) for i in range(NST)]
            wbo = [sb("wbo%d" % i, [128, 8, 512], BF16) for i in range(NST)]
            wst_t = [Tok("wst%d" % i) for i in range(NST)]
            wbo_t = [Tok("wbo%d" % i) for i in range(NST)]
            w_in_v = w_in_d.rearrange("(kc p) n -> p kc n", p=128)
            w_out_v = w_out_d.rearrange("(kc p) n -> p kc n", p=128)
            w_up_v = w_up_d.rearrange("(kc p) n -> p kc n", p=128)
            w_dn_v = w_down_d.rearrange("(a p) n -> p a n", p=128)
            jobs = []
            for g in range(11):
                jobs.append((g, w_in_v[:, :, g * 512:(g + 1) * 512], gattn, "gat"))
            for g in range(2):
                jobs.append((11 + g, w_out_v[:, :, g * 512:(g + 1) * 512], None, None))
            for g in range(8):
                jobs.append((13 + g, w_up_v[:, :, g * 512:(g + 1) * 512], gmlp, "gml"))
            for nh in range(2):
                for fg in range(4):
                    jobs.append((21 + nh * 4 + fg, w_dn_v[:, fg * 8:(fg + 1) * 8, nh * 512:(nh + 1) * 512], None, None))
            engs3 = ["dve", "act", "dve"]
            late_jobs = jobs[11:]
            for i, (gid, src, gcol, gk) in enumerate(jobs[:11]):
                s = i % NST
                P.dma("sp", wst[s][:], src, wst_t[s], writes=[wst_t[s]])
                eng = engs3[i % 3]
                if gcol is None:
                    if eng == "act":
                        P.op("act", I("activation", out=wbo[s][:], in_=wst[s][:], func=AF.Copy),
                             reads=[wst_t[s]], writes=[wbo_t[s]])
                    else:
                        P.op(eng, I("tensor_copy", out=wbo[s][:], in_=wst[s][:]),
                             reads=[wst_t[s]], writes=[wbo_t[s]])
                else:
                    for kc in range(8):
                        if eng == "act":
                            P.op("act", I("activation",
                                out=wbo[s][:, kc, :], in_=wst[s][:, kc, :], func=AF.Copy, scale=gcol[:, kc:kc + 1]),
                                reads=[wst_t[s], tl[gk]], writes=[wbo_t[s]])
                        else:
                            P.op(eng, I("tensor_scalar",
                                out=wbo[s][:, kc, :], in0=wst[s][:, kc, :], scalar1=gcol[:, kc:kc + 1], scalar2=None,
                                op0=ALU.mult), reads=[wst_t[s], tl[gk]], writes=[wbo_t[s]])
                P.dma("sp", wscr[gid], wbo[s][:], wbo_t[s], reads=[wbo_t[s]], store=True)
            P.wait_stores("sp")
            P.wait_toks("sp", list(tl.values()))
            P.emit()

        with ExitStack() as st:
          if upto >= 1:
            P.stack = st

            def sb(name, shape, dt):
                return st.enter_context(nc.sbuf_tensor(name, list(shape), dt))

            xt = [sb("xt%d" % i, [128, 4, D], F32) for i in range(2)]
            xt_t = [Tok("xt%d" % i) for i in range(2)]
            junk = sb("junkA", [128, D], BF16)
            junk_t = Tok("junkA")
            ss = sb("ssA", [128, 4], F32)
            vv = sb("vvA", [128, 4], F32)
            rstd = sb("rstdA", [128, 4], F32)
            st_t = Tok("statsA")
            hb = sb("hbA", [128, 4, D], BF16)
            hb_t = [Tok("hbA%d" % i) for i in range(4)]
            hT = [sb("hT%d" % i, [128, 8, 512], BF16) for i in range(2)]
            hT_t = [[Tok("hT%d_%d" % (i, k)) for k in range(8)] for i in range(2)]
            NWS = 3
            wsl = [sb("wslA%d" % i, [128, 8, 512], BF16) for i in range(NWS)]
            wsl_t = [Tok("wslA%d" % i) for i in range(NWS)]
            qkt = [sb("qkt%d" % i, [128, 4, 512], BF16) for i in range(3)]
            qkt_t = [[Tok("qkt%d_%d" % (i, tb)) for tb in range(4)] for i in range(3)]
            NQS = 4
            qst = [sb("qst%d" % i, [128, 512], BF16) for i in range(NQS)]
            qst_t = [Tok("qst%d" % i) for i in range(NQS)]
            vtk = [sb("vtk%d" % i, [128, 4, 512], BF16) for i in range(2)]
            vtk_t = [Tok("vtk%d" % i) for i in range(2)]
            tg = sb("tgA", [128, 16, 512], BF16)
            tg_t = [Tok("tg%d" % i) for i in range(16)]
            up = [sb("up%d" % i, [128, 4, 528], F32) for i in range(2)]
            up_t = [[Tok("up%d_%d" % (i, c)) for c in range(4)] for i in range(2)]
            T1 = sb("poolT1", [128, 528], F32)
            T2 = sb("poolT2", [128, 528], F32)
            T3 = sb("poolT3", [128, 528], F32)
            pl_t = Tok("poolT")
            dT = sb("dT", [128, 4, 512], BF16)
            dT_t = [Tok("dT%d" % c) for c in range(4)]
            NPS = 3
            pst = [sb("pst%d" % i, [128, 512], BF16) for i in range(NPS)]
            pst_t = [Tok("pst%d" % i) for i in range(NPS)]
            ra = sb("ropeA", [128, 4, 64], F32)
            ra_t4 = [Tok("ropeA%d" % i) for i in range(4)]
            fixt = sb("fixt", [128, 16], F32)
            NQF = 3
            qf = [sb("qf%d" % i, [128, 512], F32) for i in range(NQF)]
            qf_t = [Tok("qf%d" % i) for i in range(NQF)]
            qfr = Ring(list(range(NQF)))

            mmring = Ring([0, 1, 2, 3, 4, 5])
            trring = Ring([6, 7])
            qsr = Ring(list(range(NQS)))
            psr = Ring(list(range(NPS)))

            order = [G_POOL[0], G_GP[0], G_GP[1], G_GA[0], G_GA[1], G_Q[0], G_Q[1], G_K[0], G_K[1], G_V[0], G_V[1]]
            WS = WStream(P, wsl, wsl_t, wscr, order * NTT)

            def load_x(t):
                b, j = divmod(t, NT)
                P.dma("sp", xt[t % 2][:], x_d[b, j * 512:(j + 1) * 512, :].rearrange("(tb p) d -> p tb d", p=128),
                      xt_t[t % 2], writes=[xt_t[t % 2]])

            def prep(t):
                xb, xtok = xt[t % 2], xt_t[t % 2]
                for tb in range(4):
                    P.op("act", I("activation", out=junk[:], in_=xb[:, tb, :], func=AF.Square,
                                                              accum_out=ss[:, tb:tb + 1]),
                         reads=[xtok], writes=[junk_t, st_t])
                P.op("dve", I("tensor_scalar", out=vv[:], in0=ss[:], scalar1=1.0 / D, scalar2=EPS,
                                                      op0=ALU.mult, op1=ALU.add), reads=[st_t], writes=[st_t])
                P.op("act", I("activation", out=vv[:], in_=vv[:], func=AF.Ln), reads=[st_t], writes=[st_t])
                P.op("act", I("activation", out=rstd[:], in_=vv[:], func=AF.Exp, scale=-0.5), reads=[st_t], writes=[st_t])
                for tb in range(4):
                    P.op("act", I("activation", out=hb[:, tb, :], in_=xb[:, tb, :], func=AF.Copy,
                                                              scale=rstd[:, tb:tb + 1]),
                         reads=[xtok, st_t], writes=[hb_t[tb]])

            def prep_tr(t):
                for kc in range(8):
                    bk = trring.next()
                    pT = banks[bk][:].bitcast(BF16)
                    for tb in range(4):
                        P.op("pe", I("transpose",
                            out=pT[:, tb * 128:(tb + 1) * 128], in_=hb[:, tb, kc * 128:(kc + 1) * 128],
                            identity=identb[:]), reads=[hb_t[tb], t_const], writes=[bt[bk]], signal=(tb == 3))
                    P.op("dve", I("tensor_copy", out=hT[t % 2][:, kc, :], in_=pT[:, 0:512]),
                         reads=[bt[bk]], writes=[hT_t[t % 2][kc]])

            def fm_group(t, Wg, wt, evac):
                for c in range(4):
                    bk = mmring.next()
                    for kc in range(8):
                        P.op("pe", I("matmul",
                            banks[bk][:], lhsT=Wg[:, kc, c * 128:(c + 1) * 128], rhs=hT[t % 2][:, kc, :],
                            start=(kc == 0), stop=(kc == 7)),
                            reads=[wt, hT_t[t % 2][kc]], writes=[bt[bk]], signal=(kc == 7))
                    evac(c, bk)

            def tm_group(t, Wg, wt, evac):
                for tb in range(4):
                    bk = mmring.next()
                    for kc in range(8):
                        P.op("pe", I("matmul",
                            banks[bk][:], lhsT=hT[t % 2][:, kc, tb * 128:(tb + 1) * 128], rhs=Wg[:, kc, :],
                            start=(kc == 0), stop=(kc == 7)),
                            reads=[wt, hT_t[t % 2][kc]], writes=[bt[bk]], signal=(kc == 7))
                    evac(tb, bk)

            deferred = []

            def run_deferred(now):
                keep = []
                for when, fn in deferred:
                    if when <= now:
                        fn()
                    else:
                        keep.append((when, fn))
                deferred[:] = keep

            step = [0]
            stage = [0]

            def chk():
                stage[0] += 1
                if stage[0] >= dbgA:
                    raise _Stop()

            def phaseA_body():
              for t in range(NTT):
                b, j = divmod(t, NT)
                if t == 0:
                    load_x(0)
                    prep(0)
                    prep_tr(0)
                    chk()
                if t + 1 < NTT:
                    load_x(t + 1)
                ub = up[t % 2]
                ubt = up_t[t % 2]
                pub = up[(t + 1) % 2]
                pubt = up_t[(t + 1) % 2]

                Wg, wt = WS.get()

                def ev_pool(c, bk):
                    P.op("dve", I("tensor_copy", out=ub[:, c, 16:528], in_=banks[bk][:]),
                         reads=[bt[bk]], writes=[ubt[c]])
                    if j == 0:
                        P.op("pool", I("memset", ub[:, c, 0:16], 0.0), writes=[ubt[c]])
                    else:
                        P.op("pool", I("tensor_copy", out=ub[:, c, 0:16], in_=pub[:, c, 512:528]),
                             reads=[pubt[c]], writes=[ubt[c]])
                    U = ub[:, c, :]
                    w = 2 ** (c + 1)
                    dst = dT[:, c, :]
                    if c == 0:
                        P.op("pool", I("tensor_tensor", out=dst, in0=U[:, 15:527], in1=U[:, 16:528],
                                                               op=ALU.subtract), reads=[ubt[c]], writes=[dT_t[c]])
                        if j == 0:
                            P.op("pool", I("memset", dst[:, 0:1], 0.0), writes=[dT_t[c]])
                        return
                    P.op("pool", I("tensor_tensor", out=T1[:, 1:528], in0=U[:, 1:528], in1=U[:, 0:527],
                                                           op=ALU.add), reads=[ubt[c]], writes=[pl_t])
                    P.op("pool", I("tensor_tensor", out=T2[:, 3:528], in0=T1[:, 3:528], in1=T1[:, 1:526],
                                                           op=ALU.add), reads=[pl_t], writes=[pl_t])
                    cur = T2
                    if c >= 2:
                        P.op("pool", I("tensor_tensor", out=T1[:, 7:528], in0=T2[:, 7:528], in1=T2[:, 3:524],
                                                               op=ALU.add), reads=[pl_t], writes=[pl_t])
                        cur = T1
                    if c >= 3:
                        P.op("pool", I("tensor_tensor", out=T2[:, 15:528], in0=T1[:, 15:528],
                                                               in1=T1[:, 7:520], op=ALU.add),
                             reads=[pl_t], writes=[pl_t])
                        cur = T2
                    P.op("pool", I("tensor_scalar", out=T3[:, 16:528], in0=U[:, 16:528], scalar1=-float(w),
                                                           scalar2=None, op0=ALU.mult),
                         reads=[ubt[c], pl_t], writes=[pl_t])
                    P.op("pool", I("tensor_tensor", out=dst, in0=cur[:, 16:528], in1=T3[:, 16:528],
                                                           op=ALU.add), reads=[pl_t], writes=[dT_t[c]])
                    if j == 0:
                        P.op("pool", I("tensor_tensor", out=fixt[:, 0:w - 1], in0=cur[:, 16:16 + w - 1],
                                                               in1=wct[:, c, 0:w - 1], op=ALU.mult),
                             reads=[pl_t, t_const], writes=[pl_t])
                        P.op("pool", I("tensor_tensor", out=dst[:, 0:w - 1], in0=fixt[:, 0:w - 1],
                                                               in1=T3[:, 16:16 + w - 1], op=ALU.add),
                             reads=[pl_t], writes=[dT_t[c], pl_t])

                fm_group(t, Wg, wt, ev_pool)
                chk()
                WS.prefetch()

                for gi, base in ((0, 8), (1, 12), (2, 0), (3, 4)):
                    Wg, wt = WS.get()

                    def ev_gate(c, bk, base=base):
                        idx = base + c
                        P.op("act", I("activation", out=tg[:, idx, :], in_=banks[bk][:], func=AF.Tanh,
                                                           scale=0.5), reads=[bt[bk]], writes=[tg_t[idx]])
                        if idx < 8:
                            P.dma("sp", gA_s[b, idx, :, j * 512:(j + 1) * 512], tg[:, idx, :], tg_t[idx],
                                  reads=[tg_t[idx]], store=True)
                    fm_group(t, Wg, wt, ev_gate)
                    chk()
                    WS.prefetch()
                    if gi == 1:
                        def do_pool_out(b=b, j=j):
                            for c in range(4):
                                for e in range(2):
                                    bk = mmring.next()
                                    P.op("pe", I("matmul",
                                        banks[bk][:], lhsT=wpoolb[:, c, e * 128:(e + 1) * 128], rhs=dT[:, c, :],
                                        start=True, stop=True), reads=[dT_t[c], t_const], writes=[bt[bk]])
                                    ps_i = psr.next()
                                    idx = 8 + 2 * c + e
                                    P.op("dve", I("scalar_tensor_tensor",
                                        out=pst[ps_i][:], in0=tg[:, idx, :], scalar=1.0, in1=banks[bk][:],
                                        op0=ALU.add, op1=ALU.mult), reads=[bt[bk], tg_t[idx]], writes=[pst_t[ps_i]])
                                    P.dma("sp", pp_s[b, 2 * c + e, :, j * 512:(j + 1) * 512], pst[ps_i][:], pst_t[ps_i],
                                          reads=[pst_t[ps_i]], store=True)


                        deferred.append((step[0] + 4, do_pool_out))

                for kind, scr in (("q", qT_s), ("k", kT_s)):
                    for half in range(2):
                        Wg, wt = WS.get()
                        qb = step[0] % 3
                        step[0] += 1
                        qk_ = qkt[qb]
                        qk_tt = qkt_t[qb]

                        def ev_qk(tb, bk, qk_=qk_, qk_tt=qk_tt):
                            qi_ = qfr.next()
                            P.op("act", I("activation", out=qf[qi_][:], in_=banks[bk][:], func=AF.Copy),
                                 reads=[bt[bk]], writes=[qf_t[qi_]])
                            src = qf[qi_][:].rearrange("p (a d) -> p a d", a=8)
                            dst = qk_[:, tb, :].rearrange("p (a d) -> p a d", a=8)
                            blk = b * NBLK + 4 * j + tb
                            cs = cosb[:, blk, :].unsqueeze(1).to_broadcast([128, 8, 8])
                            sn = sinb[:, blk, :].unsqueeze(1).to_broadcast([128, 8, 8])
                            rav = ra[:].rearrange("p k (a f) -> p k a f", a=8)
                            P.op("act", I("activation", out=dst[:, :, 16:64], in_=src[:, :, 16:64], func=AF.Copy),
                                 reads=[qf_t[qi_]], writes=[qk_tt[tb]])
                            t1, t2 = src[:, :, 0:8], src[:, :, 8:16]
                            P.op("dve", I("tensor_tensor", out=rav[:, 0], in0=t1, in1=cs, op=ALU.mult),
                                 reads=[qf_t[qi_], t_const], writes=[ra_t4[0]])
                            P.op("dve", I("tensor_tensor", out=rav[:, 1], in0=t2, in1=sn, op=ALU.mult),
                                 reads=[qf_t[qi_], t_const], writes=[ra_t4[1]])
                            P.op("dve", I("tensor_tensor", out=rav[:, 2], in0=t2, in1=cs, op=ALU.mult),
                                 reads=[qf_t[qi_], t_const], writes=[ra_t4[2]])
                            P.op("dve", I("tensor_tensor", out=rav[:, 3], in0=t1, in1=sn, op=ALU.mult),
                                 reads=[qf_t[qi_], t_const], writes=[ra_t4[3]])
                            P.op("dve", I("tensor_tensor", out=dst[:, :, 0:8], in0=rav[:, 0], in1=rav[:, 1],
                                          op=ALU.subtract), reads=[ra_t4[0], ra_t4[1]], writes=[qk_tt[tb]])
                            P.op("dve", I("tensor_tensor", out=dst[:, :, 8:16], in0=rav[:, 2], in1=rav[:, 3],
                                          op=ALU.add), reads=[ra_t4[2], ra_t4[3]], writes=[qk_tt[tb]])

                        tm_group(t, Wg, wt, ev_qk)
                        chk()
                        WS.prefetch()

                        def do_tr(qk_=qk_, qk_tt=qk_tt, half=half, scr=scr, b=b, j=j):
                            for c in range(4):
                                bk = trring.next()
                                pT = banks[bk][:].bitcast(BF16)
                                for tb in range(4):
                                    P.op("pe", I("transpose",
                                        out=pT[:, tb * 128:(tb + 1) * 128], in_=qk_[:, tb, c * 128:(c + 1) * 128],
                                        identity=identb[:]), reads=[qk_tt[tb], t_const], writes=[bt[bk]],
                                        signal=(tb == 3))
                                qi = qsr.next()
                                P.op("dve", I("tensor_copy", out=qst[qi][:], in_=pT[:, 0:512]),
                                     reads=[bt[bk]], writes=[qst_t[qi]])
                                P.dma("sp", scr[b, half * 4 + c, :, j * 512:(j + 1) * 512], qst[qi][:], qst_t[qi],
                                      reads=[qst_t[qi]], store=True)
                        run_deferred(step[0])
                        deferred.append((step[0] + 2, do_tr))
                        if kind == "q" and half == 0 and t + 1 < NTT:
                            prep(t + 1)
                            deferred.append((step[0] + 3, lambda t=t: prep_tr(t + 1)))

                for half in range(2):
                    Wg, wt = WS.get()
                    vb = step[0] % 2
                    step[0] += 1

                    def ev_v(tb, bk, vb=vb):
                        if tb % 2 == 0:
                            P.op("act", I("activation", out=vtk[vb][:, tb, :], in_=banks[bk][:], func=AF.Copy),
                                 reads=[bt[bk]], writes=[vtk_t[vb]])
                        else:
                            P.op("dve", I("tensor_copy", out=vtk[vb][:, tb, :], in_=banks[bk][:]),
                                 reads=[bt[bk]], writes=[vtk_t[vb]])
                    tm_group(t, Wg, wt, ev_v)
                    chk()
                    WS.prefetch()
                    P.dma("sp", v_s[b, j * 512:(j + 1) * 512, (half) * 512:(half + 1) * 512].rearrange(
                        "(tb p) d -> p tb d", p=128), vtk[vb][:], vtk_t[vb], reads=[vtk_t[vb]], store=True)
                    run_deferred(step[0])
                run_deferred(10 ** 9)
            try:
                phaseA_body()
            except _Stop:
                pass
            P.wait_stores("sp")
            P.emit()

        with ExitStack() as st:
          if upto >= 2:
            P.stack = st

            def sb(name, shape, dt):
                return st.enter_context(nc.sbuf_tensor(name, list(shape), dt))

            kTb = [sb("kTb%d" % i, [128, S], BF16) for i in range(2)]
            qTb = [sb("qTb%d" % i, [128, S], BF16) for i in range(2)]
            vb_ = [sb("vb%d" % i, [128, NBLK, 128], BF16) for i in range(2)]
            gAb = [sb("gAb%d" % i, [128, S], BF16) for i in range(2)]
            ppb = [sb("ppb%d" % i, [128, S], BF16) for i in range(2)]
            ld_t = [{n: Tok("%s%d" % (n, i)) for n in ("k", "q", "v", "g", "p")} for i in range(2)]
            NE = 4
            Et = [sb("E%d" % i, [128, 2, 512], BF16) for i in range(NE)]
            Et_t = [Tok("E%d" % i) for i in range(NE)]
            lst = [sb("lst%d" % i, [128, 8, 512], F32) for i in range(2)]
            lbo = [sb("lbo%d" % i, [128, 8, 512], BF16) for i in range(2)]
            lst_t = [Tok("lst%d" % i) for i in range(2)]
            lbo_t = [Tok("lbo%d" % i) for i in range(2)]
            ljob = [0, 0]

            def late_load():
                n = ljob[0]
                if n >= len(late_jobs):
                    return
                gid, src, gcol, gk = late_jobs[n]
                P.dma("sp", lst[n % 2][:], src, lst_t[n % 2], writes=[lst_t[n % 2]])
                ljob[0] += 1

            def late_cast():
                n = ljob[1]
                if n >= ljob[0]:
                    return
                gid, src, gcol, gk = late_jobs[n]
                sl = n % 2
                for kc in range(8):
                    if gcol is None:
                        P.op("dve", I("tensor_copy", out=lbo[sl][:, kc, :], in_=lst[sl][:, kc, :]),
                             reads=[lst_t[sl]], writes=[lbo_t[sl]])
                    else:
                        P.op("dve", I("tensor_scalar", out=lbo[sl][:, kc, :], in0=lst[sl][:, kc, :],
                                      scalar1=gcol[:, kc:kc + 1], scalar2=None, op0=ALU.mult),
                             reads=[lst_t[sl], t_const], writes=[lbo_t[sl]])
                P.dma("sp", wscr[gid], lbo[sl][:], lbo_t[sl], reads=[lbo_t[sl]], store=True)
                ljob[1] += 1
            vo = sb("vo", [128, 512], F32)
            rso = sb("rso", [128, 512], F32)
            a1 = sb("a1", [128, 512], F32)
            a2 = sb("a2", [128, 512], F32)
            post_t = {n: Tok("post_" + n) for n in ("rz0", "rz1", "tt0", "tt1", "ob", "sq", "vo", "rso", "a1", "a2")}
            mst = [sb("mst%d" % i, [128, 512], BF16) for i in range(2)]
            mst_t = [Tok("mst%d" % i) for i in range(2)]
            Ocp = [sb("Ocp%d" % i, [128, 2, 512], F32) for i in range(2)]
            Zcp = [sb("Zcp%d" % i, [128, 2, 512], F32) for i in range(2)]
            Ocp_t = [Tok("Ocp%d" % i) for i in range(2)]
            Zcp_t = [Tok("Zcp%d" % i) for i in range(2)]
            ob2 = [sb("ob2_%d" % i, [128, 512], F32) for i in range(2)]
            ob2_t = [Tok("ob2_%d" % i) for i in range(2)]
            sq2 = [sb("sq2_%d" % i, [128, 512], F32) for i in range(2)]
            sq2_t = [Tok("sq2_%d" % i) for i in range(2)]
            zz = [sb("zz%d" % i, [128, 512], F32) for i in range(2)]
            zz_t = [Tok("zz%d" % i) for i in range(2)]
            pending = []

            spairs = Ring([(0, 1), (2, 3)])
            O1, O2, Z1, Z2 = 4, 5, 6, 7
            ering = Ring(list(range(NE)))

            def load_bh(i):
                b, hh = divmod(i, NH)
                s = i % 2
                P.dma("sp", kTb[s][:], kT_s[b, hh], ld_t[s]["k"], writes=[ld_t[s]["k"]])
                P.dma("sp", qTb[s][:], qT_s[b, hh], ld_t[s]["q"], writes=[ld_t[s]["q"]])
                P.dma("sp", vb_[s][:], v_s[b, :, hh * 128:(hh + 1) * 128].rearrange("(k p) d -> p k d", p=128),
                      ld_t[s]["v"], writes=[ld_t[s]["v"]])
                P.dma("sp", gAb[s][:], gA_s[b, hh], ld_t[s]["g"], writes=[ld_t[s]["g"]])
                P.dma("sp", ppb[s][:], pp_s[b, hh], ld_t[s]["p"], writes=[ld_t[s]["p"]])

            NBH = NB * NH
            load_bh(0)
            mcount = 0
            for i in range(NBH):
                b, hh = divmod(i, NH)
                s = i % 2
                L = ld_t[s]
                for j in range(NT):
                    nkb = 4 * j + 4
                    qc0 = j * 512
                    if j in (2, 5) or NT < 6:
                        late_cast()
                        late_load()
                    if j == 1 or (NT == 1 and j == 0):
                        if NT == 1:
                            while pending:
                                pending.pop(0)()
                        if i + 1 < NBH:
                            load_bh(i + 1)

                    def s_mm(kb):
                        q0 = 128 * max(0, kb - 4 * j)
                        pr = spairs.next()
                        for c in range(2):
                            lo, hi = 64 * c, 64 * (c + 1)
                            P.op("pe", I("matmul",
                                banks[pr[c]][:, q0:512], lhsT=kTb[s][lo:hi, kb * 128:(kb + 1) * 128],
                                rhs=qTb[s][lo:hi, qc0 + q0:qc0 + 512], start=True, stop=True),
                                reads=[L["k"], L["q"]], writes=[bt[pr[c]]])
                        return pr, q0

                    def exp_pv(kb, pr, q0):
                        ei = ering.next()
                        P.op("act", I("activation", out=Et[ei][:, :, q0:512], in_=psum_all[:, pr[0]:pr[0] + 2, q0:512],
                                      func=AF.Exp, scale=0.125),
                             reads=[bt[pr[0]], bt[pr[1]]], writes=[Et_t[ei]])
                        if kb >= 4 * j:
                            P.op("pool", I("tensor_tensor", out=Et[ei][:, :, q0:q0 + 128], in0=Et[ei][:, :, q0:q0 + 128],
                                           in1=maskb[:].unsqueeze(1).to_broadcast([128, 2, 128]), op=ALU.mult),
                                 reads=[t_const], writes=[Et_t[ei]])
                        first, last = (kb == 0), (kb == nkb - 1)
                        for c, (ob_, zb_) in enumerate(((O1, Z1), (O2, Z2))):
                            P.op("pe", I("matmul",
                                banks[ob_][:, q0:512], lhsT=vb_[s][:, kb, :], rhs=Et[ei][:, c, q0:512],
                                start=first, stop=last), reads=[L["v"], Et_t[ei]], writes=[bt[ob_]], signal=False)
                            P.op("pe", I("matmul",
                                banks[zb_][:, q0:512], lhsT=onesb[:], rhs=Et[ei][:, c, q0:512],
                                start=first, stop=last), reads=[t_const, Et_t[ei]], writes=[bt[zb_]],
                                signal=(c == 1))

                    prev = s_mm(0)
                    trig = min(9, nkb - 1)
                    for kb in range(nkb):
                        nxt = s_mm(kb + 1) if kb + 1 < nkb else None
                        exp_pv(kb, prev[0], prev[1])
                        prev = nxt
                        if kb == trig and trig < nkb - 1 and pending:
                            pending.pop(0)()
                    pi_ = mcount % 2
                    P.op("act", I("activation", out=Ocp[pi_][:], in_=psum_all[:, 4:6, :], func=AF.Copy),
                         reads=[bt[O1], bt[O2]], writes=[Ocp_t[pi_]])
                    P.op("dve", I("tensor_copy", out=Zcp[pi_][:], in_=psum_all[:, 6:8, :]),
                         reads=[bt[Z1], bt[Z2]], writes=[Zcp_t[pi_]])
                    if pending:
                        pending.pop(0)()
                    P.op("dve", I("tensor_tensor", out=zz[pi_][:], in0=Zcp[pi_][:, 0, :], in1=Zcp[pi_][:, 1, :], op=ALU.mult),
                         reads=[Zcp_t[pi_]], writes=[zz_t[pi_]])
                    P.op("dve", I("tensor_tensor", out=Ocp[pi_][:, 0, :], in0=Ocp[pi_][:, 0, :], in1=Zcp[pi_][:, 1, :],
                                  op=ALU.mult), reads=[Zcp_t[pi_], Ocp_t[pi_]], writes=[Ocp_t[pi_]])
                    P.op("dve", I("tensor_tensor", out=Ocp[pi_][:, 1, :], in0=Ocp[pi_][:, 1, :], in1=Zcp[pi_][:, 0, :],
                                  op=ALU.mult), reads=[Zcp_t[pi_], Ocp_t[pi_]], writes=[Ocp_t[pi_]])
                    P.op("dve", I("scalar_tensor_tensor", out=ob2[pi_][:], in0=Ocp[pi_][:, 1, :], scalar=neglam[:, 0:1],
                                  in1=Ocp[pi_][:, 0, :], op0=ALU.mult, op1=ALU.add),
                         reads=[Ocp_t[pi_], t_const], writes=[ob2_t[pi_]])
                    P.op("dve", I("tensor_tensor", out=sq2[pi_][:], in0=ob2[pi_][:], in1=ob2[pi_][:], op=ALU.mult),
                         reads=[ob2_t[pi_]], writes=[sq2_t[pi_]])
                    P.op("dve", I("scalar_tensor_tensor", out=zz[pi_][:], in0=zz[pi_][:], scalar=EPS, in1=zz[pi_][:],
                                  op0=ALU.mult, op1=ALU.mult), reads=[zz_t[pi_]], writes=[zz_t[pi_]])
                    mi = mcount % 2
                    mcount += 1

                    def part2(pi_=pi_, mi=mi, s=s, L=L, qc0=qc0, b=b, hh=hh):
                        pt = post_t
                        pr = spairs.next()
                        spairs.next()
                        P.op("pe", I("matmul", banks[pr[0]][:], lhsT=onesf[:], rhs=sq2[pi_][:], start=True, stop=True),
                             reads=[sq2_t[pi_], t_const], writes=[bt[pr[0]], bt[pr[1]]])
                        P.op("dve", I("scalar_tensor_tensor", out=vo[:], in0=banks[pr[0]][:], scalar=1.0 / 128,
                                      in1=zz[pi_][:], op0=ALU.mult, op1=ALU.add),
                             reads=[bt[pr[0]], zz_t[pi_]], writes=[pt["vo"]])
                        P.op("act", I("activation", out=vo[:], in_=vo[:], func=AF.Ln), reads=[pt["vo"]], writes=[pt["vo"]])
                        P.op("act", I("activation", out=rso[:], in_=vo[:], func=AF.Exp, scale=-0.5),
                             reads=[pt["vo"]], writes=[pt["rso"]])
                        P.op("dve", I("scalar_tensor_tensor", out=a1[:], in0=ob2[pi_][:], scalar=gs_col[:, 0:1],
                                      in1=rso[:], op0=ALU.mult, op1=ALU.mult),
                             reads=[ob2_t[pi_], pt["rso"], t_const], writes=[pt["a1"]])
                        P.op("dve", I("scalar_tensor_tensor", out=a2[:], in0=gAb[s][:, qc0:qc0 + 512], scalar=1.0,
                                      in1=a1[:], op0=ALU.add, op1=ALU.mult),
                             reads=[L["g"], pt["a1"]], writes=[pt["a2"]])
                        P.op("pool", I("tensor_tensor", out=mst[mi][:], in0=a2[:], in1=ppb[s][:, qc0:qc0 + 512],
                                       op=ALU.add), reads=[pt["a2"], L["p"]], writes=[mst_t[mi]])
                        P.dma("sp", m_s[b, hh, :, qc0:qc0 + 512], mst[mi][:], mst_t[mi], reads=[mst_t[mi]], store=True)
                    pending.append(part2)
            while pending:
                pending.pop(0)()
            while ljob[1] < len(late_jobs):
                late_load()
                late_cast()
            P.wait_stores("sp")
            P.emit()

        with ExitStack() as st:
          if upto >= 3:
            P.stack = st

            def sb(name, shape, dt):
                return st.enter_context(nc.sbuf_tensor(name, list(shape), dt))

            xt = [sb("xc%d" % i, [128, 4, D], F32) for i in range(2)]
            xt_t = [[Tok("xc%d_%d" % (i, tb)) for tb in range(4)] for i in range(2)]
            mT = [sb("mT%d" % i, [128, 8, 512], BF16) for i in range(2)]
            mT_t = [Tok("mT%d" % i) for i in range(2)]
            junk = sb("junkC", [128, D], BF16)
            junk_t = Tok("junkC")
            ss = sb("ssC", [128, 8], F32)
            vv = sb("vvC", [128, 8], F32)
            rstd = sb("rstdC", [128, 8], F32)
            st2_t = Tok("stats2")
            st3_t = Tok("stats3")
            hb = sb("hbC", [128, 4, D], BF16)
            hb_t = [Tok("hbC%d" % i) for i in range(4)]
            h2T = sb("h2T", [128, 8, 512], BF16)
            h2T_t = [Tok("h2T%d" % k) for k in range(8)]
            zT = sb("zT", [128, 32, 512], BF16)
            zT_t = [Tok("zT%d" % k) for k in range(32)]
            NR = 3
            rt = [sb("rt%d" % i, [128, 512], F32) for i in range(NR)]
            rt_t = [Tok("rt%d" % i) for i in range(NR)]
            NWS = 4
            wsl = [sb("wslC%d" % i, [128, 8, 512], BF16) for i in range(NWS)]
            wsl_t = [Tok("wslC%d" % i) for i in range(NWS)]

            mmring = Ring([0, 1, 2, 3, 4, 5])
            trring = Ring([6, 7])
            rring = Ring(list(range(NR)))
            order = list(G_OUT) + list(G_UP) + list(G_DOWN)
            WS = WStream(P, wsl, wsl_t, wscr, order * NTT)

            def load_c(t):
                b, j = divmod(t, NT)
                P.dma("sp", xt[t % 2][:], x_d[b, j * 512:(j + 1) * 512, :].rearrange("(tb p) d -> p tb d", p=128),
                      xt_t[t % 2][0], writes=xt_t[t % 2])
                P.dma("sp", mT[t % 2][:], m_s[b, :, :, j * 512:(j + 1) * 512].rearrange("h p n -> p h n"),
                      mT_t[t % 2], writes=[mT_t[t % 2]])

            load_c(0)
            stageC = [0]

            def chkC():
                stageC[0] += 1
                if stageC[0] >= dbgC:
                    raise _Stop()

            def phaseC_body():
              for t in range(NTT):
                b, j = divmod(t, NT)
                xb, xtk = xt[t % 2], xt_t[t % 2]
                mb, mtk = mT[t % 2], mT_t[t % 2]
                if t + 1 < NTT:
                    load_c(t + 1)
                for nh in range(2):
                    Wg, wt = WS.get()
                    for tb in range(4):
                        bk = mmring.next()
                        for kc in range(8):
                            P.op("pe", I("matmul",
                                banks[bk][:], lhsT=mb[:, kc, tb * 128:(tb + 1) * 128], rhs=Wg[:, kc, :],
                                start=(kc == 0), stop=(kc == 7)), reads=[wt, mtk], writes=[bt[bk]], signal=(kc == 7))
                        P.op("dve", I("tensor_tensor",
                            out=xb[:, tb, nh * 512:(nh + 1) * 512], in0=banks[bk][:],
                            in1=xb[:, tb, nh * 512:(nh + 1) * 512], op=ALU.add), reads=[bt[bk], xtk[tb]], writes=[xtk[tb]])
                    WS.prefetch()
                chkC()
                for tb in range(4):
                    P.op("act", I("activation", out=junk[:], in_=xb[:, tb, :], func=AF.Square,
                                                              accum_out=ss[:, tb:tb + 1]),
                         reads=[xtk[tb]], writes=[junk_t, st2_t])
                P.op("dve", I("tensor_scalar", out=vv[:, 0:4], in0=ss[:, 0:4], scalar1=1.0 / D, scalar2=EPS,
                                                      op0=ALU.mult, op1=ALU.add), reads=[st2_t], writes=[st2_t])
                P.op("act", I("activation", out=vv[:, 0:4], in_=vv[:, 0:4], func=AF.Ln), reads=[st2_t], writes=[st2_t])
                P.op("act", I("activation", out=rstd[:, 0:4], in_=vv[:, 0:4], func=AF.Exp, scale=-0.5), reads=[st2_t], writes=[st2_t])
                for tb in range(4):
                    P.op("act", I("activation", out=hb[:, tb, :], in_=xb[:, tb, :], func=AF.Copy,
                                                              scale=rstd[:, tb:tb + 1]),
                         reads=[xtk[tb], st2_t], writes=[hb_t[tb]])
                for kc in range(8):
                    bk = trring.next()
                    pT = banks[bk][:].bitcast(BF16)
                    for tb in range(4):
                        P.op("pe", I("transpose",
                            out=pT[:, tb * 128:(tb + 1) * 128], in_=hb[:, tb, kc * 128:(kc + 1) * 128],
                            identity=identb[:]), reads=[hb_t[tb], t_const], writes=[bt[bk]], signal=(tb == 3))
                    P.op("dve", I("tensor_copy", out=h2T[:, kc, :], in_=pT[:, 0:512]),
                         reads=[bt[bk]], writes=[h2T_t[kc]])
                chkC()
                for g in range(8):
                    Wg, wt = WS.get()
                    for c in range(4):
                        bk = mmring.next()
                        for kc in range(8):
                            P.op("pe", I("matmul",
                                banks[bk][:], lhsT=Wg[:, kc, c * 128:(c + 1) * 128], rhs=h2T[:, kc, :],
                                start=(kc == 0), stop=(kc == 7)), reads=[wt, h2T_t[kc]], writes=[bt[bk]],
                                signal=(kc == 7))
                        ri = rring.next()
                        fi = 4 * g + c
                        P.op("act", I("activation", out=rt[ri][:], in_=banks[bk][:], func=AF.Relu),
                             reads=[bt[bk]], writes=[rt_t[ri]])
                        P.op("dve", I("tensor_tensor",
                            out=zT[:, fi, :], in0=banks[bk][:], in1=rt[ri][:], op=ALU.mult),
                            reads=[bt[bk], rt_t[ri]], writes=[zT_t[fi]])
                    WS.prefetch()
                chkC()
                for nh in range(2):
                    bks = [mmring.next() for _ in range(4)]
                    for fg in range(4):
                        Wg, wt = WS.get()
                        for tb in range(4):
                            bk = bks[tb]
                            for fc in range(8):
                                fi = 8 * fg + fc
                                P.op("pe", I("matmul",
                                    banks[bk][:], lhsT=zT[:, fi, tb * 128:(tb + 1) * 128], rhs=Wg[:, fc, :],
                                    start=(fg == 0 and fc == 0), stop=(fg == 3 and fc == 7)),
                                    reads=[wt, zT_t[fi]], writes=[bt[bk]], signal=(fc == 7))
                            if fg == 3:
                                P.op("dve", I("tensor_tensor",
                                    out=xb[:, tb, nh * 512:(nh + 1) * 512], in0=banks[bk][:],
                                    in1=xb[:, tb, nh * 512:(nh + 1) * 512], op=ALU.add),
                                    reads=[bt[bk], xtk[tb]], writes=[xtk[tb]])
                        WS.prefetch()
                chkC()
                for tb in range(4):
                    P.op("act", I("activation", out=junk[:], in_=xb[:, tb, :], func=AF.Square,
                                                              accum_out=ss[:, 4 + tb:5 + tb]),
                         reads=[xtk[tb]], writes=[junk_t, st3_t])
                P.op("dve", I("tensor_scalar", out=vv[:, 4:8], in0=ss[:, 4:8], scalar1=1.0 / D, scalar2=EPS,
                                                      op0=ALU.mult, op1=ALU.add), reads=[st3_t], writes=[st3_t])
                P.op("act", I("activation", out=vv[:, 4:8], in_=vv[:, 4:8], func=AF.Ln), reads=[st3_t], writes=[st3_t])
                P.op("act", I("activation", out=rstd[:, 4:8], in_=vv[:, 4:8], func=AF.Exp, scale=-0.5), reads=[st3_t], writes=[st3_t])
                for tb in range(4):
                    P.op("dve", I("scalar_tensor_tensor",
                        out=xb[:, tb, :], in0=xb[:, tb, :], scalar=rstd[:, 4 + tb:5 + tb], in1=gfin_bc[:],
                        op0=ALU.mult, op1=ALU.mult), reads=[xtk[tb], st3_t, t_const], writes=[xtk[tb]])
                P.dma("sp", y_d[b, j * 512:(j + 1) * 512, :].rearrange("(tb p) d -> p tb d", p=128), xb[:],
                      xtk[0], reads=xtk, store=True)
            try:
                phaseC_body()
            except _Stop:
                pass
            P.wait_stores("sp")
            P.emit()
    return nc


def _consts():
    ident = np.eye(128, dtype=np.float32)
    kk = np.arange(128)[:, None]
    qq = np.arange(128)[None, :]
    mask = (qq >= kk).astype(np.float32)
    invf = np.power(np.float32(500000.0), -(np.arange(0, 16, 2, dtype=np.float32) / np.float32(16))).astype(np.float32)
    wct = np.zeros((4, 16), np.float32)
    for c in range(4):
        w = 2 ** (c + 1)
        for t_ in range(16):
            wct[c, t_] = w / min(t_ + 1, w)
    return ident, mask, invf, wct


def prep_core_inputs(inputs, b0, NB):
    S = inputs["x"].shape[1]
    f = lambda a: np.ascontiguousarray(np.asarray(a, dtype=np.float32))
    ident, mask, invf, wct = _consts()
    pos = np.asarray(inputs["positions"])[b0:b0 + NB].astype(np.int32)
    pos_t = np.ascontiguousarray(pos.reshape(NB, S // 128, 128).transpose(0, 2, 1))
    return {
        "x": f(inputs["x"][b0:b0 + NB]),
        "pos_t": pos_t,
        "w_in": f(inputs["w_in"][0]),
        "w_out": f(inputs["w_out"][0]),
        "w_up": f(inputs["w_up"][0]),
        "w_down": f(inputs["w_down"][0]),
        "w_pool_r": f(np.asarray(inputs["w_pool"][0]).transpose(1, 0, 2)),
        "gcol_attn": f(np.asarray(inputs["norm_attn_g"][0]).reshape(8, 128).T),
        "gcol_mlp": f(np.asarray(inputs["norm_mlp_g"][0]).reshape(8, 128).T),
        "gfin": f(inputs["final_norm_g"]),
        "subln_col": f(np.asarray(inputs["subln_g"][0]).reshape(128, 1)),
        "pool_scale": f(inputs["pool_scale"][0]),
        "lam4": f(np.stack([np.asarray(inputs[k][0]) for k in ("lam_q1", "lam_k1", "lam_q2", "lam_k2")])),
        "c_ident": ident, "c_mask": mask, "c_invf": invf, "c_wct": wct,
    }


_NC_CACHE = {}


def kernel(**inputs):
    x = np.asarray(inputs["x"])
    B, S, _ = x.shape
    n = N_CORES
    NB = B // n
    key = (NB, S)
    if key not in _NC_CACHE:
        _NC_CACHE[key] = build_nc(NB, S)
    nc = _NC_CACHE[key]
    in_maps = [prep_core_inputs(inputs, i * NB, NB) for i in range(n)]
    res = run_bass_kernel_spmd(nc, in_maps, core_ids=list(range(n)))
    return np.concatenate([np.asarray(r["y"]) for r in res.results], axis=0).astype(np.float32)
```
